# Optimizing a Trainium2 kernel written in Bass

```python
import jax, jax.numpy as jnp
from jax import lax
import numpy as np

D_MODEL = 1024
BATCH = 4
SEQ = 4096
DEPTH = 2

CTX_LEN = 256
GRID_W = 64
EPS = 1e-6
N_MOD = 6
NEG_INF = -1e30

MLA_HEADS = 8
MLA_Q_LORA = 384
MLA_KV_LORA = 256
MLA_NOPE = 64
MLA_ROPE = 32
MLA_V = 64
ROPE_THETA = 10000.0
Q_BLOCK = 128

NA_HEADS = 8
NA_HEAD_DIM = 64
NA_KH = 8
NA_KW = 16

GLA_HEADS = 4
GLA_DK = 64
GLA_DV = 128
GLA_GATE_RANK = 16
GLA_GATE_NORM = 16.0
GLA_CHUNK = 64

PEER_HEADS = 8
PEER_N_KEYS = 128
PEER_N_EXPERTS = PEER_N_KEYS * PEER_N_KEYS
PEER_QUERY_DIM = 128
PEER_TOPK = 16
PEER_BLOCK = 128

MLA_WIDTH = MLA_HEADS * MLA_V
NA_WIDTH = NA_HEADS * NA_HEAD_DIM
GLA_K_WIDTH = GLA_HEADS * GLA_DK
GLA_V_WIDTH = GLA_HEADS * GLA_DV
N_BRANCH = 3

IN_SPLITS = (MLA_Q_LORA, MLA_KV_LORA, MLA_ROPE, NA_WIDTH, NA_WIDTH, NA_WIDTH,
             GLA_K_WIDTH, GLA_K_WIDTH, GLA_V_WIDTH, GLA_V_WIDTH, GLA_GATE_RANK, GLA_GATE_RANK,
             N_BRANCH * D_MODEL)
IN_COLS = sum(IN_SPLITS)

kernel_name = 'hybrid_mla_natten_gla_peer_block'


def rms_norm(x, w):
    xf = x.astype(jnp.float32)
    y = xf * lax.rsqrt(jnp.mean(xf * xf, axis=-1, keepdims=True) + EPS)
    return (y * w.astype(jnp.float32)).astype(x.dtype)


def modulate(h, shift, scale):
    return h * (1 + scale) + shift


def split_cols(p):
    out, o = [], 0
    for w in IN_SPLITS:
        out.append(p[..., o:o + w])
        o += w
    return out


def axial_rope(L, dtype):
    t = jnp.arange(L)
    rows = (t // GRID_W).astype(jnp.float32)
    cols = (t % GRID_W).astype(jnp.float32)
    half = MLA_ROPE // 2
    inv = ROPE_THETA ** (-jnp.arange(0, half, 2, dtype=jnp.float32) / half)
    ar = rows[:, None] * inv
    ac = cols[:, None] * inv
    ang = jnp.concatenate([ar, ar, ac, ac], axis=-1)
    return jnp.cos(ang).astype(dtype), jnp.sin(ang).astype(dtype)


def apply_rope(x, cos, sin):
    q = MLA_ROPE // 4
    x0, x1, x2, x3 = x[..., :q], x[..., q:2 * q], x[..., 2 * q:3 * q], x[..., 3 * q:]
    rot = jnp.concatenate([-x1, x0, -x3, x2], axis=-1)
    return x * cos + rot * sin


def attend_blocks(q, k, v, scale):
    B, L, H, d = q.shape
    nb = L // Q_BLOCK
    qb = q.reshape(B, nb, Q_BLOCK, H, d).transpose(1, 0, 2, 3, 4)

    def one(qblk):
        s = jnp.einsum('bqhd,bkhd->bhqk', qblk, k).astype(jnp.float32) * scale
        p = jax.nn.softmax(s, axis=-1).astype(v.dtype)
        return jnp.einsum('bhqk,bkhe->bqhe', p, v)

    o = lax.map(one, qb)
    return o.transpose(1, 0, 2, 3, 4).reshape(B, L, H, v.shape[-1])


def mla_queries(cq, lp):
    B, L, _ = cq.shape
    q = rms_norm(cq, lp['mla_q_norm']) @ lp['mla_w_uq']
    return q.reshape(B, L, MLA_HEADS, MLA_NOPE + MLA_ROPE)


def mla_keys_values(ckv, k_rope, lp):
    B, L, _ = ckv.shape
    kv = (rms_norm(ckv, lp['mla_kv_norm']) @ lp['mla_w_ukv']).reshape(B, L, MLA_HEADS, MLA_NOPE + MLA_V)
    k_nope, v = kv[..., :MLA_NOPE], kv[..., MLA_NOPE:]
    k_rope = jnp.broadcast_to(k_rope[:, :, None, :], (B, L, MLA_HEADS, MLA_ROPE))
    return jnp.concatenate([k_nope, k_rope], axis=-1), v


def mla_mixer(pl, pc, lp, need_ctx):
    B, L, _ = pl[0].shape
    scale = (MLA_NOPE + MLA_ROPE) ** -0.5
    cos, sin = axial_rope(L, pl[0].dtype)
    q = mla_queries(pl[0], lp)
    q = jnp.concatenate([q[..., :MLA_NOPE], apply_rope(q[..., MLA_NOPE:], cos[:, None, :], sin[:, None, :])], axis=-1)
    k, v = mla_keys_values(pl[1], apply_rope(pl[2], cos, sin), lp)
    k_c, v_c = mla_keys_values(pc[1], pc[2], lp)
    o = attend_blocks(q, jnp.concatenate([k_c, k], axis=1), jnp.concatenate([v_c, v], axis=1), scale)
    o = o.reshape(B, L, MLA_WIDTH)
    o_c = None
    if need_ctx:
        q_c = mla_queries(pc[0], lp)
        o_c = attend_blocks(q_c, k_c, v_c, scale).reshape(B, pc[0].shape[1], MLA_WIDTH)
    return o, o_c


def na_heads(t):
    B, L, _ = t.shape
    return t.reshape(B, L, NA_HEADS, NA_HEAD_DIM)


def na_mixer(pl, pc, lp, need_ctx):
    q, k, v = na_heads(pl[3]), na_heads(pl[4]), na_heads(pl[5])
    k_c, v_c = na_heads(pc[4]), na_heads(pc[5])
    B, L = q.shape[:2]
    rows = L // GRID_W
    kh = min(NA_KH, rows)
    scale = NA_HEAD_DIM ** -0.5
    r = jnp.arange(rows)
    row_idx = jnp.clip(r - kh // 2, 0, rows - kh)[:, None] + jnp.arange(kh)
    w = jnp.arange(GRID_W)
    col_start = jnp.clip(w - NA_KW // 2, 0, GRID_W - NA_KW)
    col_in = (w[None, :] >= col_start[:, None]) & (w[None, :] < col_start[:, None] + NA_KW)
    dr = row_idx - r[:, None] + (NA_KH - 1)
    dc = jnp.clip(w[None, :] - w[:, None], -(NA_KW - 1), NA_KW - 1) + (NA_KW - 1)
    bias = lp['na_rpb'][:, dr[:, None, :, None], dc[None, :, None, :]]
    bias = bias.transpose(1, 0, 2, 3, 4).astype(jnp.float32)
    qg = q.reshape(B, rows, GRID_W, NA_HEADS, NA_HEAD_DIM)
    kr = k.reshape(B, rows, GRID_W, NA_HEADS, NA_HEAD_DIM)[:, row_idx]
    vr = v.reshape(B, rows, GRID_W, NA_HEADS, NA_HEAD_DIM)[:, row_idx]
    s_lat = jnp.einsum('brqhd,brawhd->brhqaw', qg, kr).astype(jnp.float32) * scale + bias
    s_lat = jnp.where(col_in[:, None, :], s_lat, NEG_INF)
    s_ctx = jnp.einsum('brqhd,bchd->brhqc', qg, k_c).astype(jnp.float32) * scale
    s = jnp.concatenate([s_lat.reshape(B, rows, NA_HEADS, GRID_W, kh * GRID_W), s_ctx], axis=-1)
    p = jax.nn.softmax(s, axis=-1).astype(v.dtype)
    p_lat = p[..., :kh * GRID_W].reshape(B, rows, NA_HEADS, GRID_W, kh, GRID_W)
    p_ctx = p[..., kh * GRID_W:]
    o = jnp.einsum('brhqaw,brawhd->brqhd', p_lat, vr) + jnp.einsum('brhqc,bchd->brqhd', p_ctx, v_c)
    o = o.reshape(B, L, NA_WIDTH)
    o_c = None
    if need_ctx:
        o_c = attend_blocks(na_heads(pc[3]), k_c, v_c, scale).reshape(B, pc[3].shape[1], NA_WIDTH)
    return o, o_c


def gla_heads(t, dh):
    B, L, _ = t.shape
    return t.reshape(B, L, GLA_HEADS, dh).transpose(0, 2, 1, 3).astype(jnp.float32)


def gla_gate(low, w, b):
    return gla_heads(jax.nn.log_sigmoid((low @ w + b).astype(jnp.float32)) / GLA_GATE_NORM, GLA_DK)


def flip_seq(t):
    return jnp.flip(t, axis=2)


def gla_scan(q, k, v, g, s0):
    B, H, L, DK = q.shape
    DV = v.shape[-1]
    n, C = L // GLA_CHUNK, GLA_CHUNK
    q, k, g = [t.reshape(B, H, n, C, DK) for t in (q, k, g)]
    v = v.reshape(B, H, n, C, DV)
    b = jnp.cumsum(g, axis=-2)
    b_last = b[..., -1:, :]
    q_in = q * jnp.exp(b)
    k_in = k * jnp.exp(-b)
    k_end = k * jnp.exp(b_last - b)
    mask = jnp.tril(jnp.ones((C, C), dtype=bool))
    a = jnp.where(mask, jnp.einsum('bhnik,bhnjk->bhnij', q_in, k_in), 0.0)
    o_intra = jnp.einsum('bhnij,bhnjv->bhniv', a, v)
    decay = jnp.exp(b_last[..., 0, :])

    def step(s, inp):
        qi, ke, vi, de = inp
        o = jnp.einsum('bhik,bhkv->bhiv', qi, s)
        s = de[..., None] * s + jnp.einsum('bhjk,bhjv->bhkv', ke, vi)
        return s, o

    xs = tuple(jnp.moveaxis(t, 2, 0) for t in (q_in, k_end, v, decay))
    s_fin, o_inter = lax.scan(step, s0, xs)
    o = o_intra + jnp.moveaxis(o_inter, 0, 2)
    return o.reshape(B, H, L, DV), s_fin


def gla_final_state(k, v, g):
    b = jnp.cumsum(g, axis=2)
    return jnp.einsum('bhtk,bhtv->bhkv', k * jnp.exp(b[:, :, -1:, :] - b), v)


def gla_bidir(q, k, v, gf, gb, s0f, s0b):
    of, sf = gla_scan(q, k, v, gf, s0f)
    ob, sb = gla_scan(flip_seq(q), flip_seq(k), flip_seq(v), flip_seq(gb), s0b)
    return of + flip_seq(ob), sf, sb


def gla_out(o, og, norm_w):
    B, H, L, DV = o.shape
    o = rms_norm(o.transpose(0, 2, 1, 3), norm_w)
    return (o.reshape(B, L, GLA_V_WIDTH) * jax.nn.silu(og.astype(jnp.float32))).astype(og.dtype)


def gla_mixer(pl, pc, lp, need_ctx):
    scale = GLA_DK ** -0.5
    q = gla_heads(pl[6], GLA_DK) * scale
    k, v = gla_heads(pl[7], GLA_DK), gla_heads(pl[8], GLA_DV)
    gf = gla_gate(pl[10], lp['gla_w_gk_fwd'], lp['gla_b_gk_fwd'])
    gb = gla_gate(pl[11], lp['gla_w_gk_bwd'], lp['gla_b_gk_bwd'])
    k_c, v_c = gla_heads(pc[7], GLA_DK), gla_heads(pc[8], GLA_DV)
    gf_c = gla_gate(pc[10], lp['gla_w_gk_fwd'], lp['gla_b_gk_fwd'])
    gb_c = gla_gate(pc[11], lp['gla_w_gk_bwd'], lp['gla_b_gk_bwd'])
    o_c = None
    if need_ctx:
        q_c = gla_heads(pc[6], GLA_DK) * scale
        B = q_c.shape[0]
        z = jnp.zeros((B, GLA_HEADS, GLA_DK, GLA_DV), jnp.float32)
        o_c_raw, sf, sb = gla_bidir(q_c, k_c, v_c, gf_c, gb_c, z, z)
        o_c = gla_out(o_c_raw, pc[9], lp['gla_norm'])
    else:
        sf = gla_final_state(k_c, v_c, gf_c)
        sb = gla_final_state(flip_seq(k_c), flip_seq(v_c), flip_seq(gb_c))
    o_raw, _, _ = gla_bidir(q, k, v, gf, gb, sf, sb)
    return gla_out(o_raw, pl[9], lp['gla_norm']), o_c


def merge_branches(o_a, o_b, o_g, gate_cols, lp):
    ga, gb, gg = jnp.split(jax.nn.sigmoid(gate_cols), N_BRANCH, axis=-1)
    y = ga * (o_a @ lp['w_o_mla']) + gb * (o_b @ lp['w_o_na']) + gg * (o_g @ lp['w_o_gla'])
    return y @ lp['w_out']


def peer_ffn(h, lp):
    B, L, D = h.shape
    q = (h @ lp['peer_w_q']).reshape(B, L, PEER_HEADS, 2, PEER_QUERY_DIM // 2)
    s = jnp.einsum('blhpd,pkd->blhpk', q, lp['peer_sub_keys']).astype(jnp.float32)
    v_half, i_half = lax.top_k(s, PEER_TOPK)
    cand = (v_half[..., 0, :, None] + v_half[..., 1, None, :]).reshape(B, L, PEER_HEADS, PEER_TOPK * PEER_TOPK)
    cid = (i_half[..., 0, :, None] * PEER_N_KEYS + i_half[..., 1, None, :]).reshape(B, L, PEER_HEADS, PEER_TOPK * PEER_TOPK)
    top, pos = lax.top_k(cand, PEER_TOPK)
    eid = jnp.take_along_axis(cid, pos, axis=-1)
    gate = jax.nn.softmax(top, axis=-1).astype(h.dtype)
    nb = (B * L) // PEER_BLOCK
    kk = PEER_HEADS * PEER_TOPK
    hb = h.reshape(nb, PEER_BLOCK, D)
    eb = eid.reshape(nb, PEER_BLOCK, kk)
    gb = gate.reshape(nb, PEER_BLOCK, kk)
    u_tab, v_tab = lp['peer_u'], lp['peer_v']

    def one(args):
        ht, et, gt = args
        act = jnp.einsum('tkd,td->tk', u_tab[et], ht)
        return jnp.einsum('tk,tkd->td', gt * jax.nn.gelu(act), v_tab[et])

    out = lax.map(one, (hb, eb, gb))
    return out.reshape(B, L, D)


def hybrid_layer(x, xc, mod, mod_c, lp, need_ctx):
    sh1, sc1, g1, sh2, sc2, g2 = [mod[:, i, None, :] for i in range(N_MOD)]
    sh1c, sc1c, g1c, sh2c, sc2c, g2c = [mod_c[i] for i in range(N_MOD)]
    h = modulate(rms_norm(x, lp['norm1']), sh1, sc1)
    hc = modulate(rms_norm(xc, lp['norm1']), sh1c, sc1c)
    pl = split_cols(h @ lp['w_in'])
    pc = split_cols(hc @ lp['w_in'])
    o_a, o_a_c = mla_mixer(pl, pc, lp, need_ctx)
    o_b, o_b_c = na_mixer(pl, pc, lp, need_ctx)
    o_g, o_g_c = gla_mixer(pl, pc, lp, need_ctx)
    x = x + g1 * merge_branches(o_a, o_b, o_g, pl[12], lp)
    x = x + g2 * peer_ffn(modulate(rms_norm(x, lp['norm2']), sh2, sc2), lp)
    if need_ctx:
        xc = xc + g1c * merge_branches(o_a_c, o_b_c, o_g_c, pc[12], lp)
        xc = xc + g2c * peer_ffn(modulate(rms_norm(xc, lp['norm2']), sh2c, sc2c), lp)
    return x, xc


def setup_inputs(seed: int = 0) -> dict:
    key = jax.random.key(seed)
    ks = iter(jax.random.split(key, 40))

    def nrm(shape, s):
        return jax.random.normal(next(ks), shape, jnp.float32) * s

    d = D_MODEL
    return {
        'x': nrm((BATCH, SEQ, d), 1.0),
        'c': nrm((BATCH, d), 1.0),
        'ctx': nrm((BATCH, CTX_LEN, d), 1.0),
        'c_ctx': nrm((d,), 1.0),
        'w_ada': nrm((DEPTH, d, N_MOD * d), 0.25 * d ** -0.5),
        'b_ada': nrm((DEPTH, N_MOD * d), 0.01),
        'norm1': 1.0 + nrm((DEPTH, d), 0.02),
        'w_in': nrm((DEPTH, d, IN_COLS), d ** -0.5),
        'mla_q_norm': 1.0 + nrm((DEPTH, MLA_Q_LORA), 0.02),
        'mla_w_uq': nrm((DEPTH, MLA_Q_LORA, MLA_HEADS * (MLA_NOPE + MLA_ROPE)), MLA_Q_LORA ** -0.5),
        'mla_kv_norm': 1.0 + nrm((DEPTH, MLA_KV_LORA), 0.02),
        'mla_w_ukv': nrm((DEPTH, MLA_KV_LORA, MLA_HEADS * (MLA_NOPE + MLA_V)), MLA_KV_LORA ** -0.5),
        'na_rpb': nrm((DEPTH, NA_HEADS, 2 * NA_KH - 1, 2 * NA_KW - 1), 0.1),
        'gla_w_gk_fwd': nrm((DEPTH, GLA_GATE_RANK, GLA_K_WIDTH), GLA_GATE_RANK ** -0.5),
        'gla_b_gk_fwd': nrm((DEPTH, GLA_K_WIDTH), 0.01),
        'gla_w_gk_bwd': nrm((DEPTH, GLA_GATE_RANK, GLA_K_WIDTH), GLA_GATE_RANK ** -0.5),
        'gla_b_gk_bwd': nrm((DEPTH, GLA_K_WIDTH), 0.01),
        'gla_norm': 1.0 + nrm((DEPTH, GLA_DV), 0.02),
        'w_o_mla': nrm((DEPTH, MLA_WIDTH, d), MLA_WIDTH ** -0.5),
        'w_o_na': nrm((DEPTH, NA_WIDTH, d), NA_WIDTH ** -0.5),
        'w_o_gla': nrm((DEPTH, GLA_V_WIDTH, d), GLA_V_WIDTH ** -0.5),
        'w_out': nrm((DEPTH, d, d), d ** -0.5),
        'norm2': 1.0 + nrm((DEPTH, d), 0.02),
        'peer_w_q': nrm((DEPTH, d, PEER_HEADS * PEER_QUERY_DIM), d ** -0.5),
        'peer_sub_keys': nrm((DEPTH, 2, PEER_N_KEYS, PEER_QUERY_DIM // 2), (PEER_QUERY_DIM // 2) ** -0.5),
        'peer_u': nrm((DEPTH, PEER_N_EXPERTS, d), d ** -0.5),
        'peer_v': nrm((DEPTH, PEER_N_EXPERTS, d), 0.5),
        'final_norm': 1.0 + nrm((d,), 0.02),
    }


def reference(x, c, ctx, c_ctx, w_ada, b_ada, norm1, w_in, mla_q_norm, mla_w_uq, mla_kv_norm, mla_w_ukv,
              na_rpb, gla_w_gk_fwd, gla_b_gk_fwd, gla_w_gk_bwd, gla_b_gk_bwd, gla_norm, w_o_mla, w_o_na,
              w_o_gla, w_out, norm2, peer_w_q, peer_sub_keys, peer_u, peer_v, final_norm):
    B = x.shape[0]
    xc = ctx
    for l in range(DEPTH):
        lp = dict(norm1=norm1[l], w_in=w_in[l], mla_q_norm=mla_q_norm[l], mla_w_uq=mla_w_uq[l],
                  mla_kv_norm=mla_kv_norm[l], mla_w_ukv=mla_w_ukv[l], na_rpb=na_rpb[l],
                  gla_w_gk_fwd=gla_w_gk_fwd[l], gla_b_gk_fwd=gla_b_gk_fwd[l],
                  gla_w_gk_bwd=gla_w_gk_bwd[l], gla_b_gk_bwd=gla_b_gk_bwd[l], gla_norm=gla_norm[l],
                  w_o_mla=w_o_mla[l], w_o_na=w_o_na[l], w_o_gla=w_o_gla[l], w_out=w_out[l],
                  norm2=norm2[l], peer_w_q=peer_w_q[l], peer_sub_keys=peer_sub_keys[l],
                  peer_u=peer_u[l], peer_v=peer_v[l])
        mod = (jax.nn.silu(c) @ w_ada[l] + b_ada[l]).reshape(B, N_MOD, D_MODEL)
        mod_c = (jax.nn.silu(c_ctx) @ w_ada[l] + b_ada[l]).reshape(N_MOD, D_MODEL)
        x, xc = hybrid_layer(x, xc, mod, mod_c, lp, l < DEPTH - 1)
    return rms_norm(x, final_norm)
```

```python
import contextlib
import types
import numpy as np
import concourse.bass as bass
import concourse.mybir as mybir
from concourse.bass_utils import run_bass_kernel_spmd

F32 = mybir.dt.float32
BF16 = mybir.dt.bfloat16
AF = mybir.ActivationFunctionType
ALU = mybir.AluOpType

SEM_LIMIT = 16000
DMA_LIMIT = 900
DMA_R = 6
SAMESYNC = True

D = 1024
NCTX = 256
NLAT = 4096
NTOK = NCTX + NLAT
NT = NTOK // 128
EPS = 1e-6
INX = 6880
C_GATE = 3776
FM_ROWS = 2080
FM_NAQ, FM_NAK, FM_GQ, FM_GK, FM_OG, FM_LOW = 0, 512, 1024, 1280, 1536, 2048
SC_MLA = 96 ** -0.5
NEXP = 16384


def _freeze(fn):
    if fn.__closure__:
        cells = []
        for c in fn.__closure__:
            try:
                cells.append(types.CellType(c.cell_contents))
            except ValueError:
                cells.append(c)
        fn = types.FunctionType(fn.__code__, fn.__globals__, fn.__name__, fn.__defaults__, tuple(cells))
    return fn


class Prog:
    ENG = ['pe', 'act', 'dve', 'pool', 'sp']

    def __init__(self, nc, stack):
        self.nc = nc
        self.stack = stack
        self.stream = {e: [] for e in self.ENG}
        self.sems = {}
        self.ecount = {e: 0 for e in self.ENG}
        self.known = {e: {} for e in self.ENG}
        self.evclock = {}
        self.lastw = {}
        self.readers = {}
        self.dslot = {}
        self.dcount = {}
        self.nops = 0

    def sem(self, key):
        if key not in self.sems:
            self.sems[key] = self.stack.enter_context(self.nc.semaphore("s" + "_".join(str(k) for k in key)))
        return self.sems[key]

    def _deps(self, eng, reads, writes, samesync):
        deps = {}

        def add(ev):
            sk, v = ev
            if not samesync and sk[0] == 'e' and sk[1] == eng:
                return
            if deps.get(sk, 0) < v:
                deps[sk] = v
        for k in reads:
            if k in self.lastw:
                add(self.lastw[k])
        for k in writes:
            if k in self.lastw:
                add(self.lastw[k])
            for ev in self.readers.get(k, ()):
                add(ev)
        kn = self.known[eng]
        waits = []
        for sk, v in deps.items():
            if kn.get(sk, 0) >= v:
                continue
            waits.append((sk, v))
        for sk, v in waits:
            if kn.get(sk, 0) < v:
                kn[sk] = v
            for sk2, v2 in self.evclock.get((sk, v), {}).items():
                if kn.get(sk2, 0) < v2:
                    kn[sk2] = v2
        return waits

    def _commit(self, ev, reads, writes):
        for k in reads:
            self.readers.setdefault(k, []).append(ev)
        for k in writes:
            self.lastw[k] = ev
            self.readers[k] = []

    def op(self, eng, fn, reads=(), writes=()):
        fn = _freeze(fn)
        waits = self._deps(eng, reads, writes, SAMESYNC and eng != 'pe')
        self.ecount[eng] += 1
        n = self.ecount[eng]
        sk = ('e', eng, (n - 1) // SEM_LIMIT)
        ev = (sk, (n - 1) % SEM_LIMIT + 1)
        self.evclock[ev] = dict(self.known[eng])
        self.stream[eng].append((waits, fn, ev, 1))
        self._commit(ev, reads, writes)
        self.nops += 1
        return ev

    def dma(self, q, out, in_, reads=(), writes=(), **kw):
        waits = self._deps(q, reads, writes, True)
        i = self.dslot.get(q, 0)
        self.dslot[q] = (i + 1) % DMA_R
        c = self.dcount.get((q, i), 0)
        ep, cc = divmod(c, DMA_LIMIT)
        sk = ('d', q, i, ep)
        kn = self.known[q]
        if cc > 0:
            if kn.get(sk, 0) < cc * 16:
                waits.append((sk, cc * 16))
                kn[sk] = cc * 16
        elif ep > 0:
            skp = ('d', q, i, ep - 1)
            if kn.get(skp, 0) < DMA_LIMIT * 16:
                waits.append((skp, DMA_LIMIT * 16))
                kn[skp] = DMA_LIMIT * 16
        self.dcount[(q, i)] = c + 1
        ev = (sk, (cc + 1) * 16)
        self.evclock[ev] = dict(kn)
        fn = lambda e, out=out, in_=in_, kw=kw: e.dma_start(out=out, in_=in_, **kw)
        self.stream[q].append((waits, fn, ev, 16))
        self._commit(ev, reads, writes)
        self.nops += 1
        return ev

    def _all_events(self):
        evs = []
        for (q, i), c in self.dcount.items():
            ep, cc = divmod(c, DMA_LIMIT)
            if cc == 0:
                ep, cc = ep - 1, DMA_LIMIT
            evs.append((('d', q, i, ep), cc * 16))
        for e in self.ENG:
            n = self.ecount[e]
            if n:
                evs.append((('e', e, (n - 1) // SEM_LIMIT), (n - 1) % SEM_LIMIT + 1))
        return evs

    def barrier(self):
        evs = self._all_events()
        for eng in self.ENG:
            kn = self.known[eng]
            waits = []
            for sk, v in evs:
                if sk[0] == 'e' and sk[1] == eng:
                    continue
                if kn.get(sk, 0) < v:
                    waits.append((sk, v))
                    kn[sk] = v
            if waits:
                self.stream[eng].append((waits, None, None, 0))
        self.lastw = {}
        self.readers = {}

    def emit(self):
        nc = self.nc
        for e in self.ENG:
            for waits, fn, ev, inc in self.stream[e]:
                for sk, v in waits:
                    self.sem(sk)
                if ev is not None:
                    self.sem(ev[0])
        with nc.Block() as block:
            def replay(name, e):
                for waits, fn, ev, inc in self.stream[name]:
                    for sk, v in waits:
                        e.wait_ge(self.sems[sk], v)
                    if fn is not None:
                        fn(e).then_inc(self.sems[ev[0]], inc)

            @block.sync
            def _(e):
                replay('sp', e)

            @block.tensor
            def _(e):
                replay('pe', e)

            @block.scalar
            def _(e):
                replay('act', e)

            @block.vector
            def _(e):
                replay('dve', e)

            @block.gpsimd
            def _(e):
                replay('pool', e)


ARENA_WORDS = 51000
GLA_STAGE = 99
GLA_NT = 0
MLA_STAGE = 99
PEER_GROUPS = 0
PEER_STAGE = 99
PEER_SUB = 99
MLA_SKIP = set()
GLA_HH = 2


class LazyIn(dict):
    def __init__(self, nc):
        super().__init__()
        self.nc = nc
        self.shapes = {}

    def __missing__(self, name):
        sh = self.shapes[name]
        shape, dt = sh[0], sh[1]
        kind = sh[2] if len(sh) > 2 else "ExternalInput"
        ap = self.nc.dram_tensor(name, shape, dt, kind=kind).ap()
        self[name] = ap
        return ap


class Builder:
    def __init__(self, nc, st, nlayers=2, dbg=False, phases=None, scr_in=()):
        self.phases = phases
        self.scr_in = set(scr_in)
        self.nc = nc
        self.P = Prog(nc, st)
        self.arena = st.enter_context(nc.sbuf_tensor("arena", [128, ARENA_WORDS], F32))
        self.top = 0
        self.ps = [st.enter_context(nc.psum_tensor(f"ps{i}", [128, 512], F32)) for i in range(8)]
        self.nlayers = nlayers
        self.dbg = dbg
        self.uid = 0
        self.I = LazyIn(nc)
        self.S = LazyIn(nc)

    def alloc(self, n, dt=F32):
        w = n if dt == F32 else (n + 1) // 2
        assert self.top + w <= ARENA_WORDS, (self.top, w)
        a = self.arena[:, self.top:self.top + w]
        self.top += w
        return a if dt == F32 else a.bitcast(dt)

    def key(self, s):
        self.uid += 1
        return f"{s}#{self.uid}"

    def psb(self, i):
        return self.ps[i][:, :].bitcast(BF16)

    def inp(self, name, shape, dt=F32):
        self.I.shapes[name] = (list(shape), dt)

    def scratch(self, name, shape, dt):
        kind = "ExternalInput" if name in self.scr_in else ("ExternalOutput" if self.dbg else "Internal")
        self.S.shapes[name] = (list(shape), dt, kind)

    def new_phase(self):
        self.P.barrier()
        self.top = self.persist_top

    def declare(self):
        L = 2
        self.inp("x", [NLAT, D]); self.inp("ctx", [NCTX, D]); self.inp("cvecs", [2, D])
        self.inp("w_ada", [L, D, 6 * D]); self.inp("b_ada", [L, 6 * D]); self.inp("norm1", [L, D])
        self.inp("w_in_x", [L, D, INX]); self.inp("mla_q_norm", [L, 384]); self.inp("w_uq_x", [L, 384, 1536])
        self.inp("mla_kv_norm", [L, 256]); self.inp("mla_w_ukv", [L, 256, 1024])
        self.inp("na_bias", [L, 8, 128, 4480])
        self.inp("gla_w_gk_fwd", [L, 16, 256]); self.inp("gla_b_gk_fwd", [L, 256])
        self.inp("gla_w_gk_bwd", [L, 16, 256]); self.inp("gla_b_gk_bwd", [L, 256]); self.inp("gla_norm", [L, 128])
        self.inp("w_o_mla", [L, 512, D]); self.inp("w_o_na", [L, 512, D]); self.inp("w_o_gla", [L, 512, D])
        self.inp("w_out", [L, D, D]); self.inp("norm2", [L, D]); self.inp("peer_w_q", [L, D, D])
        self.inp("peer_sub_keys", [L, 2, 128, 64]); self.inp("peer_u", [L, NEXP, D]); self.inp("peer_v", [L, NEXP, D])
        self.inp("final_norm", [D])
        self.inp("k_cos", [NTOK, 32]); self.inp("k_sin", [NTOK, 32])
        self.inp("q_cosT", [32, NTOK]); self.inp("q_sinT", [32, NTOK])
        self.inp("na_mask", [128, 4480]); self.inp("gla_c", [4, 128, 128])
        self.out = self.nc.dram_tensor("out", [NLAT, D], F32, kind="ExternalOutput").ap()
        self.scratch("xs", [NTOK, D], F32); self.scratch("xmid", [NTOK, D], F32)
        self.scratch("mod", [L, 2, 6 * D], F32)
        self.scratch("proj_tm", [NTOK, INX], BF16); self.scratch("proj_fm", [FM_ROWS, NTOK], BF16)
        self.scratch("oT", [3, 512, NTOK], BF16)
        self.scratch("uT", [8, 128, NEXP], BF16); self.scratch("vb", [NEXP, D], BF16)

    def build(self):
        P = self.P
        self.declare()
        self.identf = self.alloc(128); self.identb = self.alloc(128, BF16)
        self.onesf = self.alloc(128); self.onesb = self.alloc(128, BF16)
        P.op('pool', lambda e: e.memset(self.identf, 1.0), writes=['identf'])
        P.op('pool', lambda e: e.affine_select(self.identf, self.identf, [[-1, 128]], ALU.is_equal, 0.0, base=0, channel_multiplier=1), reads=['identf'], writes=['identf'])
        P.op('dve', lambda e: e.tensor_copy(self.identb, self.identf), reads=['identf'], writes=['identb'])
        P.op('pool', lambda e: e.memset(self.onesf, 1.0), writes=['onesf'])
        P.op('pool', lambda e: e.memset(self.onesb, 1.0), writes=['onesb'])
        self.persist_top = self.top
        if "xs" not in self.scr_in:
            P.dma('sp', self.S["xs"][0:NCTX, :], self.I["ctx"][:, :], writes=['xs'])
            P.dma('sp', self.S["xs"][NCTX:NTOK, :], self.I["x"][:, :], writes=['xs'])
        P.barrier()
        allp = ["mod", "AP", "mla", "na", "gla", "merge", "peer_prep", "peer"]
        for l in range(self.nlayers):
            for ph in allp:
                if self.phases is None or ph in self.phases:
                    getattr(self, "phase_" + ph)(l)
        if self.phases is None or "final" in self.phases:
            self.phase_final()
        P.barrier()
        P.emit()

    def bvec(self, dst, src_row, key):
        self.P.dma('sp', dst, src_row.partition_broadcast(128), writes=[key])

    def qtiles(self, l):
        return list(range(0, NT)) if l == 0 else list(range(2, NT))

    def phase_mod(self, l):
        P = self.P
        self.new_phase()
        sc = self.alloc(16).rearrange("p (r k) -> p r k", r=2)
        brow = self.alloc(6144)
        modsb = self.alloc(6144)
        wb = [self.alloc(4096).rearrange("p (k n) -> p k n", k=8) for _ in range(2)]
        for r in range(2):
            P.dma('sp', sc[:, r, :], self.I["cvecs"][r, :].rearrange("(k p) -> p k", p=128), writes=['sc'], allow_slow_non_contiguous=True)
        P.op('act', lambda e: e.activation(sc, sc, AF.Silu), reads=['sc'], writes=['sc'])
        P.dma('sp', brow[0:1, :], self.I["b_ada"][l:l + 1, :], writes=['brow'])
        for g in range(12):
            w = wb[g % 2]
            wk = f"modw{g % 2}"
            P.dma('sp', w, self.I["w_ada"][l, :, g * 512:(g + 1) * 512].rearrange("(k p) n -> p k n", p=128), writes=[wk])
            pk = f"ps{g % 2}"
            pst = self.ps[g % 2]
            for k in range(8):
                P.op('pe', lambda e, k=k, w=w, pst=pst: e.matmul(pst[0:2, :], sc[:, :, k], w[:, k, :], start=(k == 0), stop=False), reads=['sc', wk], writes=[pk])
            P.op('pe', lambda e, g=g, pst=pst: e.matmul(pst[0:2, :], self.onesf[0:1, 0:2], brow[0:1, g * 512:(g + 1) * 512], start=False, stop=True), reads=['brow', 'onesf'], writes=[pk])
            P.op('act', lambda e, g=g, pst=pst: e.copy(modsb[0:2, g * 512:(g + 1) * 512], pst[0:2, :]), reads=[pk], writes=['modsb'])
        P.dma('sp', self.S["mod"][l, :, :], modsb[0:2, :], reads=['modsb'], writes=['mod'])

    def modrow(self, l, r, i):
        return self.S["mod"][l, r:r + 1, i * D:(i + 1) * D]

    def load_modvecs(self, l, norm_name, i_sh, i_sc):
        P = self.P
        res = []
        nrm = self.alloc(D)
        self.bvec(nrm, self.I[norm_name][l:l + 1, :], 'nrm')
        for r in range(2):
            weff = self.alloc(D); sh = self.alloc(D)
            kw, ks = self.key('weff'), self.key('sh')
            self.bvec(weff, self.modrow(l, r, i_sc), kw)
            self.bvec(sh, self.modrow(l, r, i_sh), ks)
            P.op('dve', lambda e, weff=weff: e.scalar_tensor_tensor(weff, weff, 1.0, nrm, ALU.add, ALU.mult), reads=[kw, 'nrm'], writes=[kw])
            res.append((weff, kw, sh, ks))
        return res

    def norm_mod(self, xt, kx, dst, kd, weff, kw, sh, ks, tmpf, kt, st, kst):
        P = self.P
        P.op('act', lambda e: e.activation(tmpf, xt, AF.Square, accum_out=st[:, 0:1]), reads=[kx], writes=[kt, kst])
        P.op('act', lambda e: e.activation(st[:, 1:2], st[:, 0:1], AF.Sqrt, bias=EPS, scale=1.0 / D), reads=[kst], writes=[kst])
        P.op('dve', lambda e: e.reciprocal(st[:, 2:3], st[:, 1:2]), reads=[kst], writes=[kst])
        P.op('dve', lambda e: e.scalar_tensor_tensor(tmpf, xt, st[:, 2:3], weff, ALU.mult, ALU.mult), reads=[kx, kst, kw], writes=[kt])
        P.op('dve', lambda e: e.tensor_tensor(dst, tmpf, sh, ALU.add), reads=[kt, ks], writes=[kd])

    def phase_AP(self, l):
        P = self.P
        self.new_phase()
        hT = self.alloc(8 * NTOK, BF16).rearrange("p (k t) -> p k t", k=8)
        m0 = self.top
        mv = self.load_modvecs(l, "norm1", 0, 1)
        xts = [self.alloc(D) for _ in range(2)]
        hbs = [self.alloc(D, BF16) for _ in range(2)]
        tmpf = self.alloc(D)
        stt = self.alloc(4 * NT)
        for tt in range(NT):
            b = tt % 2
            xt, hb = xts[b], hbs[b]
            kx, kh, kp = f"xt{b}", f"hb{b}", f"ps{b}"
            st = stt[:, 4 * tt:4 * tt + 4]
            kst = f"st{tt}"
            P.dma('sp', xt, self.S["xs"][tt * 128:(tt + 1) * 128, :], writes=[kx])
            weff, kw, sh, ks = mv[1 if tt < 2 else 0]
            self.norm_mod(xt, kx, hb, kh, weff, kw, sh, ks, tmpf, 'tmpf', st, kst)
            pv = self.psb(b).rearrange("p (k t) -> p k t", k=8)
            for k in range(8):
                P.op('pe', lambda e, k=k, hb=hb, pv=pv: e.transpose(pv[:, k, :], hb[:, k * 128:(k + 1) * 128], self.identb), reads=[kh, 'identb'], writes=[kp])
            P.op('act', lambda e, tt=tt, pv=pv: e.copy(hT[:, :, tt * 128:(tt + 1) * 128], pv), reads=[kp], writes=[f"hT{tt}"])
        hkeys = [f"hT{tt}" for tt in range(NT)]
        P.barrier()
        for tt in range(NT):
            P.lastw[f"hT{tt}"] = None
        P.lastw = {}
        self.top = m0
        groups = [(i * 512, 512) for i in range(7)] + [(3584, 192)] + [(C_GATE + i * 512, 512) for i in range(6)] + [(6848, 32)]
        wfs = [self.alloc(4096).rearrange("p (k n) -> p k n", k=8) for _ in range(2)]
        wbs = [self.alloc(4096, BF16).rearrange("p (k n) -> p k n", k=8) for _ in range(2)]
        stg = [self.alloc(512, BF16) for _ in range(3)]
        cnt = 0
        for gi, (c0, ncol) in enumerate(groups):
            b = gi % 2
            wf, wbb = wfs[b], wbs[b]
            P.dma('sp', wf[:, :, 0:ncol], self.I["w_in_x"][l, :, c0:c0 + ncol].rearrange("(k p) n -> p k n", p=128), writes=[f"wf{b}"])
            P.op('pool', lambda e, wf=wf, wbb=wbb, ncol=ncol: e.tensor_copy(wbb[:, :, 0:ncol], wf[:, :, 0:ncol]), reads=[f"wf{b}"], writes=[f"wb{b}"])
            func = AF.Sigmoid if (C_GATE <= c0 < 6848) else AF.Copy
            for tt in range(NT):
                pb = cnt % 3
                sb_ = cnt % 3
                cnt += 1
                pst = self.ps[pb]
                for k in range(8):
                    P.op('pe', lambda e, k=k, tt=tt, pst=pst, wbb=wbb, ncol=ncol: e.matmul(pst[:, 0:ncol], hT[:, k, tt * 128:(tt + 1) * 128], wbb[:, k, 0:ncol], start=(k == 0), stop=(k == 7)), reads=[f"wb{b}"], writes=[f"ps{pb}"])
                s_ = stg[sb_]
                P.op('act', lambda e, pst=pst, s_=s_, ncol=ncol, func=func: e.activation(s_[:, 0:ncol], pst[:, 0:ncol], func), reads=[f"ps{pb}"], writes=[f"stg{sb_}"])
                P.dma('pool', self.S["proj_tm"][tt * 128:(tt + 1) * 128, c0:c0 + ncol], s_[:, 0:ncol], reads=[f"stg{sb_}"], writes=['proj_tm'])
        fm = [(672 + 128 * i, 128, 0.125, FM_NAQ + 128 * i) for i in range(4)] + [(1184 + 128 * i, 128, 1.0, FM_NAK + 128 * i) for i in range(4)] + \
             [(2208 + 128 * i, 128, 0.125, FM_GQ + 128 * i) for i in range(2)] + [(2464 + 128 * i, 128, 1.0, FM_GK + 128 * i) for i in range(2)] + \
             [(3232 + 128 * i, 128, 1.0, FM_OG + 128 * i) for i in range(4)] + [(3744, 32, 1.0, FM_LOW)]
        tgs = [(0, 256)] + [(256 + 512 * i, 512) for i in range(8)]
        for gi, (c0, nr, scl, r0) in enumerate(fm):
            b = gi % 2
            wf, wbb = wfs[b], wbs[b]
            P.dma('sp', wf[:, :, 0:nr], self.I["w_in_x"][l, :, c0:c0 + nr].rearrange("(k p) n -> p k n", p=128), writes=[f"wf{b}"])
            P.op('pool', lambda e, wf=wf, wbb=wbb, nr=nr: e.tensor_copy(wbb[:, :, 0:nr], wf[:, :, 0:nr]), reads=[f"wf{b}"], writes=[f"wb{b}"])
            for (t0, n) in tgs:
                pb = cnt % 3
                cnt += 1
                pst = self.ps[pb]
                s_ = stg[pb]
                for k in range(8):
                    P.op('pe', lambda e, k=k, pst=pst, wbb=wbb, nr=nr, t0=t0, n=n: e.matmul(pst[0:nr, 0:n], wbb[:, k, 0:nr], hT[:, k, t0:t0 + n], start=(k == 0), stop=(k == 7)), reads=[f"wb{b}"], writes=[f"ps{pb}"])
                P.op('act', lambda e, pst=pst, s_=s_, nr=nr, n=n, scl=scl: e.activation(s_[0:nr, 0:n], pst[0:nr, 0:n], AF.Copy, scale=scl), reads=[f"ps{pb}"], writes=[f"stg{pb}"])
                P.dma('pool', self.S["proj_fm"][r0:r0 + nr, t0:t0 + n], s_[0:nr, 0:n], reads=[f"stg{pb}"], writes=['proj_fm'])

    def attn_fin(self, O, n, kO, dst, kdst, rec, krec):
        P = self.P
        P.op('dve', lambda e: e.reciprocal(rec[0:64, 0:n], O[64:128, 0:n]), reads=[kO], writes=[krec])
        P.op('dve', lambda e: e.tensor_tensor(dst, O[0:64, 0:n], rec[0:64, 0:n], ALU.mult), reads=[kO, krec], writes=[kdst])

    def phase_mla(self, l):
        P = self.P
        self.new_phase()
        need_ctx = (l == 0)
        cqnT = self.alloc(3 * NTOK, BF16).rearrange("p (k t) -> p k t", k=3)
        ckvnT = self.alloc(2 * NTOK, BF16).rearrange("p (k t) -> p k t", k=2)
        kTs = [self.alloc(NTOK, BF16) for _ in range(2)]
        qTs = [self.alloc(NTOK, BF16) for _ in range(2)]
        Vps = [self.alloc(NT * 128, BF16).rearrange("p (t c) -> p t c", c=128) for _ in range(2)]
        ost1 = self.alloc(NTOK, BF16)
        osts = [ost1, ost1]
        wuq = self.alloc(3 * 1536, BF16).rearrange("p (k n) -> p k n", k=3)
        wukv = self.alloc(2 * 1024, BF16).rearrange("p (k n) -> p k n", k=2)
        cosT = self.alloc(NTOK); sinT = self.alloc(NTOK)
        kcs = self.alloc(NT * 32).rearrange("p (t c) -> p t c", c=32)
        ksn = self.alloc(NT * 32).rearrange("p (t c) -> p t c", c=32)
        PTs = [self.alloc(512, BF16) for _ in range(3)]
        rec = self.alloc(512); tq1 = self.alloc(512); tq2 = self.alloc(512)
        qn = self.alloc(3); kvn = self.alloc(2)
        m0 = self.top
        wtmp = self.alloc(3 * 1536).rearrange("p (k n) -> p k n", k=3)
        P.dma('sp', wtmp, self.I["w_uq_x"][l].rearrange("(k p) n -> p k n", p=128), writes=['wtmp'])
        P.dma('sp', qn, self.I["mla_q_norm"][l, :].rearrange("(k p) -> p k", p=128), writes=['qn'], allow_slow_non_contiguous=True)
        for k in range(3):
            P.op('dve', lambda e, k=k: e.tensor_scalar(wuq[:, k, :], wtmp[:, k, :], qn[:, k:k + 1], None, ALU.mult), reads=['wtmp', 'qn'], writes=['wuq'])
        wtmp2 = wtmp.rearrange("p k n -> p (k n)")[:, 0:2048].rearrange("p (k n) -> p k n", k=2)
        P.dma('sp', wtmp2, self.I["mla_w_ukv"][l].rearrange("(k p) n -> p k n", p=128), writes=['wtmp'])
        P.dma('sp', kvn, self.I["mla_kv_norm"][l, :].rearrange("(k p) -> p k", p=128), writes=['kvn'], allow_slow_non_contiguous=True)
        for k in range(2):
            P.op('dve', lambda e, k=k: e.tensor_scalar(wukv[:, k, :], wtmp2[:, k, :], kvn[:, k:k + 1], None, ALU.mult), reads=['wtmp', 'kvn'], writes=['wukv'])
        P.dma('sp', cosT[64:96, :], self.I["q_cosT"][:, :], writes=['cosT'])
        P.dma('sp', sinT[64:96, :], self.I["q_sinT"][:, :], writes=['sinT'])
        P.dma('sp', kcs, self.I["k_cos"].rearrange("(t p) c -> p t c", p=128), writes=['kcs'])
        P.dma('sp', ksn, self.I["k_sin"].rearrange("(t p) c -> p t c", p=128), writes=['ksn'])
        for b in range(2):
            P.op('pool', lambda e, b=b: e.memset(Vps[b][:, :, 64:128], 1.0), writes=[f"Vp{b}"])
        if MLA_STAGE < 1:
            return
        pjs = [self.alloc(704, BF16) for _ in range(2)]
        junk = self.alloc(384)
        stt = self.alloc(8 * NT)
        cqn = [self.alloc(384, BF16) for _ in range(2)]
        ckvn = [self.alloc(256, BF16) for _ in range(2)]
        kro = [self.alloc(128, BF16) for _ in range(2)]
        for b_ in range(2):
            P.op('pool', lambda e, b_=b_: e.memset(kro[b_], 0.0), writes=[f"kro{b_}"])
        t1 = self.alloc(32); t2 = self.alloc(32)
        for tt in range(NT):
            b = tt % 2
            pj = pjs[b]
            kpj = f"pj{b}"
            st = stt[:, 8 * tt:8 * tt + 8]
            kst = f"st{tt}"
            P.dma('sp', pj[:, 0:672], self.S["proj_tm"][tt * 128:(tt + 1) * 128, 0:672], writes=[kpj])
            P.dma('sp', pj[:, 672:704], self.S["proj_tm"][tt * 128:(tt + 1) * 128, 6848:6880], writes=[kpj])
            if 'sq' not in MLA_SKIP: P.op('act', lambda e, pj=pj, st=st: e.activation(junk[:, 0:384], pj[:, 0:384], AF.Square, accum_out=st[:, 0:1]), reads=[kpj], writes=['junk', kst])
            if 'sq' not in MLA_SKIP: P.op('act', lambda e, pj=pj, st=st: e.activation(junk[:, 0:256], pj[:, 384:640], AF.Square, accum_out=st[:, 1:2]), reads=[kpj], writes=['junk', kst])
            if 'sq' not in MLA_SKIP: P.op('act', lambda e, st=st: e.activation(st[:, 2:3], st[:, 0:1], AF.Sqrt, bias=EPS, scale=1.0 / 384), reads=[kst], writes=[kst])
            if 'sq' not in MLA_SKIP: P.op('act', lambda e, st=st: e.activation(st[:, 3:4], st[:, 1:2], AF.Sqrt, bias=EPS, scale=1.0 / 256), reads=[kst], writes=[kst])
            if 'sq' not in MLA_SKIP: P.op('dve', lambda e, st=st: e.reciprocal(st[:, 4:6], st[:, 2:4]), reads=[kst], writes=[kst])
            if 'norm' not in MLA_SKIP: P.op('dve', lambda e, pj=pj, st=st, b=b: e.tensor_scalar(cqn[b], pj[:, 0:384], st[:, 4:5], None, ALU.mult), reads=[kpj, kst], writes=[f"cqn{b}"])
            if 'norm' not in MLA_SKIP: P.op('dve', lambda e, pj=pj, st=st, b=b: e.tensor_scalar(ckvn[b], pj[:, 384:640], st[:, 5:6], None, ALU.mult), reads=[kpj, kst], writes=[f"ckvn{b}"])
            if 'rope' not in MLA_SKIP: P.op('dve', lambda e, pj=pj, tt=tt: e.tensor_tensor(t1, pj[:, 640:672], kcs[:, tt, :], ALU.mult), reads=[kpj, 'kcs'], writes=['t1'])
            if 'rope' not in MLA_SKIP: P.op('dve', lambda e, pj=pj, tt=tt: e.tensor_tensor(t2, pj[:, 672:704], ksn[:, tt, :], ALU.mult), reads=[kpj, 'ksn'], writes=['t2'])
            if 'rope' not in MLA_SKIP: P.op('dve', lambda e, b=b: e.tensor_tensor(kro[b][:, 64:96], t1, t2, ALU.add), reads=['t1', 't2'], writes=[f"kro{b}"])
            pv = self.psb(b).rearrange("p (k t) -> p k t", k=8)
            kp = f"ps{b}"
            for k in range(3):
                if 'tr' not in MLA_SKIP: P.op('pe', lambda e, k=k, b=b, pv=pv: e.transpose(pv[:, k, :], cqn[b][:, k * 128:(k + 1) * 128], self.identb), reads=[f"cqn{b}"], writes=[kp])
            for k in range(2):
                if 'tr' not in MLA_SKIP: P.op('pe', lambda e, k=k, b=b, pv=pv: e.transpose(pv[:, 3 + k, :], ckvn[b][:, k * 128:(k + 1) * 128], self.identb), reads=[f"ckvn{b}"], writes=[kp])
            if 'tr' not in MLA_SKIP: P.op('pe', lambda e, b=b, pv=pv: e.transpose(pv[:, 5, :], kro[b], self.identb), reads=[f"kro{b}"], writes=[kp])
            sl = slice(tt * 128, (tt + 1) * 128)
            if 'tr' not in MLA_SKIP: P.op('act', lambda e, pv=pv, sl=sl: e.copy(cqnT[:, :, sl], pv[:, 0:3, :]), reads=[kp], writes=['cqnT'])
            if 'tr' not in MLA_SKIP: P.op('act', lambda e, pv=pv, sl=sl: e.copy(ckvnT[:, :, sl], pv[:, 3:5, :]), reads=[kp], writes=['ckvnT'])
            if 'kc' not in MLA_SKIP: P.op('act', lambda e, pv=pv, sl=sl: e.copy(kTs[0][64:96, sl], pv[64:96, 5, :]), reads=[kp], writes=['kT0r'])
            if 'kc' not in MLA_SKIP: P.op('act', lambda e, pv=pv, sl=sl: e.copy(kTs[1][64:96, sl], pv[64:96, 5, :]), reads=[kp], writes=['kT1r'])
        if MLA_STAGE < 2:
            return
        self.top = m0
        tgs = [(0, 256)] + [(256 + 512 * i, 512) for i in range(8)]
        qgs = ([(0, 256, [0, 1])] if need_ctx else []) + [(256 + 512 * i, 512, list(range(NT))) for i in range(8)]
        scnt = 0
        ocnt = 0
        for h in range(8):
            b = h % 2
            kT, qT, Vp, ost = kTs[b], qTs[b], Vps[b], osts[b]
            kkT, kqT, kVp, kost = f"kT{b}", f"qT{b}", f"Vp{b}", "ost"
            for (t0, n, _) in qgs:
                p1, p2 = self.ps[5], self.ps[6]
                for k in range(3):
                    P.op('pe', lambda e, k=k, t0=t0, n=n, h=h: e.matmul(p1[0:96, 0:n], wuq[:, k, h * 192:h * 192 + 96], cqnT[:, k, t0:t0 + n], start=(k == 0), stop=(k == 2)), reads=['wuq', 'cqnT'], writes=['ps5'])
                for k in range(3):
                    P.op('pe', lambda e, k=k, t0=t0, n=n, h=h: e.matmul(p2[0:96, 0:n], wuq[:, k, h * 192 + 96:h * 192 + 192], cqnT[:, k, t0:t0 + n], start=(k == 0), stop=(k == 2)), reads=['wuq', 'cqnT'], writes=['ps6'])
                P.op('act', lambda e, t0=t0, n=n, qT=qT: e.activation(qT[0:64, t0:t0 + n], p1[0:64, 0:n], AF.Copy, scale=SC_MLA), reads=['ps5'], writes=[kqT])
                P.op('dve', lambda e, t0=t0, n=n: e.tensor_tensor(tq1[64:96, 0:n], p1[64:96, 0:n], cosT[64:96, t0:t0 + n], ALU.mult), reads=['ps5', 'cosT'], writes=['tq1'])
                P.op('dve', lambda e, t0=t0, n=n: e.tensor_tensor(tq2[64:96, 0:n], p2[64:96, 0:n], sinT[64:96, t0:t0 + n], ALU.mult), reads=['ps6', 'sinT'], writes=['tq2'])
                P.op('dve', lambda e, t0=t0, n=n, qT=qT: e.tensor_tensor(qT[64:96, t0:t0 + n], tq1[64:96, 0:n], tq2[64:96, 0:n], ALU.add), reads=['tq1', 'tq2'], writes=[kqT])
            for (t0, n) in tgs:
                p3 = self.ps[7]
                for k in range(2):
                    P.op('pe', lambda e, k=k, t0=t0, n=n, h=h: e.matmul(p3[0:64, 0:n], wukv[:, k, h * 128:h * 128 + 64], ckvnT[:, k, t0:t0 + n], start=(k == 0), stop=(k == 1)), reads=['wukv', 'ckvnT'], writes=['ps7'])
                P.op('act', lambda e, t0=t0, n=n, kT=kT: e.copy(kT[0:64, t0:t0 + n], p3[0:64, 0:n]), reads=['ps7'], writes=[kkT])
            for t8 in range(0, NT, 8):
                p3 = self.ps[7]
                nt8 = min(8, NT - t8)
                pv3 = p3[:, :].rearrange("p (t c) -> p t c", c=64)
                for j in range(nt8):
                    tt = t8 + j
                    for k in range(2):
                        P.op('pe', lambda e, k=k, j=j, tt=tt, h=h: e.matmul(pv3[:, j, :], ckvnT[:, k, tt * 128:(tt + 1) * 128], wukv[:, k, h * 128 + 64:h * 128 + 128], start=(k == 0), stop=(k == 1)), reads=['wukv', 'ckvnT'], writes=['ps7'])
                P.op('dve', lambda e, t8=t8, nt8=nt8, Vp=Vp: e.tensor_copy(Vp[:, t8:t8 + nt8, 0:64], pv3[:, 0:nt8, :]), reads=['ps7'], writes=[kVp])
            for (t0, n, kts) in (qgs if MLA_STAGE >= 3 else []):
                ob = 3 + (ocnt % 2)
                ocnt += 1
                O = self.ps[ob]
                kO = f"ps{ob}"
                pend = []
                nk = len(kts)

                def qk(i):
                    nonlocal scnt
                    sb_ = scnt % 3
                    scnt += 1
                    kt = kts[i]
                    S_ = self.ps[sb_]
                    P.op('pe', lambda e, S_=S_, kt=kt: e.matmul(S_[:, 0:n], kT[0:96, kt * 128:(kt + 1) * 128], qT[0:96, t0:t0 + n], start=True, stop=True), reads=[kkT, kqT, 'kT0r', 'kT1r'], writes=[f"ps{sb_}"])
                    PT = PTs[sb_]
                    P.op('act', lambda e, S_=S_, PT=PT: e.activation(PT[:, 0:n], S_[:, 0:n], AF.Exp), reads=[f"ps{sb_}"], writes=[f"PT{sb_}"])
                    return sb_

                def pvm(i, sb_):
                    kt = kts[i]
                    PT = PTs[sb_]
                    P.op('pe', lambda e, PT=PT, kt=kt, i=i: e.matmul(O[:, 0:n], Vp[:, kt, :], PT[:, 0:n], start=(i == 0), stop=(i == nk - 1)), reads=[kVp, f"PT{sb_}"], writes=[kO])
                q = []
                for i in range(nk):
                    q.append((i, qk(i)))
                    if len(q) > 2:
                        pvm(*q.pop(0))
                while q:
                    pvm(*q.pop(0))
                self.attn_fin(O, n, kO, ost[0:64, t0:t0 + n], kost, rec, 'rec')
            P.dma('pool', self.S["oT"][0, h * 64:(h + 1) * 64, :], ost[0:64, :], reads=[kost], writes=['oT'])

    def phase_na(self, l):
        P = self.P
        self.new_phase()
        need_ctx = (l == 0)
        nmask = self.alloc(4480)
        P.dma('sp', nmask, self.I["na_mask"][:, :], writes=['nmask'])
        kTs = [self.alloc(NTOK, BF16) for _ in range(2)]
        qTs = [self.alloc(NTOK, BF16) for _ in range(2)]
        Vps = [self.alloc(NT * 128, BF16).rearrange("p (t c) -> p t c", c=128) for _ in range(2)]
        osts = [self.alloc(NTOK, BF16) for _ in range(2)]
        biasm = [self.alloc(4480).rearrange("p (a j q) -> p a j q", a=5, j=7) for _ in range(2)]
        T1s = [self.alloc(896).rearrange("p (j q) -> p j q", j=7) for _ in range(2)]
        PTs = [self.alloc(896, BF16).rearrange("p (j q) -> p j q", j=7) for _ in range(2)]
        PTc = self.alloc(512, BF16).rearrange("p (j q) -> p j q", j=2)
        rec = self.alloc(256)
        for b in range(2):
            P.op('pool', lambda e, b=b: e.memset(Vps[b][:, :, 64:128], 1.0), writes=[f"Vp{b}"])
        cnt = 0
        for h in range(8):
            b = h % 2
            kT, qT, Vp, ost, bm = kTs[b], qTs[b], Vps[b], osts[b], biasm[b]
            kkT, kqT, kVp, kost, kbm = f"kT{b}", f"qT{b}", f"Vp{b}", f"ost{b}", f"bm{b}"
            P.dma('sp', qT[0:64, :], self.S["proj_fm"][FM_NAQ + h * 64:FM_NAQ + (h + 1) * 64, :], writes=[kqT])
            P.dma('sp', kT[0:64, :], self.S["proj_fm"][FM_NAK + h * 64:FM_NAK + (h + 1) * 64, :], writes=[kkT])
            P.dma('sp', Vp[:, :, 0:64], self.S["proj_tm"][:, 1696 + h * 64:1696 + (h + 1) * 64].rearrange("(t p) c -> p t c", p=128), writes=[kVp])
            bmf = bm.rearrange("p a j q -> p (a j q)")
            P.dma('sp', bmf, self.I["na_bias"][l, h, :, :], writes=[kbm])
            P.op('pool', lambda e, bmf=bmf: e.tensor_tensor(bmf, bmf, nmask, ALU.add), reads=[kbm, 'nmask'], writes=[kbm])
            for m in range(32):
                pc = 0 if m == 0 else 1 if m == 1 else 3 if m == 30 else 4 if m == 31 else 2
                kb = min(max(m - 2, 0), 27)
                kts = [2 + kb + j for j in range(5)] + [0, 1]
                s = cnt % 2
                cnt += 1
                SA, SB = self.ps[2 * s], self.ps[2 * s + 1]
                kSA, kSB = f"ps{2 * s}", f"ps{2 * s + 1}"
                q0 = (2 + m) * 128
                for j, kt in enumerate(kts):
                    dst = SA[:, j * 128:(j + 1) * 128] if j < 4 else SB[:, (j - 4) * 128:(j - 3) * 128]
                    P.op('pe', lambda e, dst=dst, kt=kt, q0=q0: e.matmul(dst, kT[0:64, kt * 128:(kt + 1) * 128], qT[0:64, q0:q0 + 128], start=True, stop=True), reads=[kkT, kqT], writes=[kSA if j < 4 else kSB])
                T1, PT = T1s[s], PTs[s]
                P.op('dve', lambda e, T1=T1, SA=SA, pc=pc: e.tensor_tensor(T1[:, 0:4, :], SA[:, :].rearrange("p (j q) -> p j q", j=4), bm[:, pc, 0:4, :], ALU.add), reads=[kSA, kbm], writes=[f"T1{s}"])
                P.op('dve', lambda e, T1=T1, SB=SB, pc=pc: e.tensor_tensor(T1[:, 4:7, :], SB[:, 0:384].rearrange("p (j q) -> p j q", j=3), bm[:, pc, 4:7, :], ALU.add), reads=[kSB, kbm], writes=[f"T1{s}"])
                P.op('act', lambda e, T1=T1, PT=PT: e.activation(PT, T1, AF.Exp), reads=[f"T1{s}"], writes=[f"PT{s}"])
                ob = 4 + s
                O = self.ps[ob]
                for j, kt in enumerate(kts):
                    P.op('pe', lambda e, O=O, kt=kt, j=j, PT=PT: e.matmul(O[:, 0:128], Vp[:, kt, :], PT[:, j, :], start=(j == 0), stop=(j == 6)), reads=[kVp, f"PT{s}"], writes=[f"ps{ob}"])
                self.attn_fin(O, 128, f"ps{ob}", ost[0:64, q0:q0 + 128], kost, rec, 'rec')
            if need_ctx:
                S_ = self.ps[6]
                for jt in range(2):
                    P.op('pe', lambda e, jt=jt: e.matmul(S_[:, jt * 256:(jt + 1) * 256], kT[0:64, jt * 128:(jt + 1) * 128], qT[0:64, 0:256], start=True, stop=True), reads=[kkT, kqT], writes=['ps6'])
                P.op('act', lambda e: e.activation(PTc, S_[:, :].rearrange("p (j q) -> p j q", j=2), AF.Exp), reads=['ps6'], writes=['PTc'])
                O = self.ps[7]
                for jt in range(2):
                    P.op('pe', lambda e, jt=jt: e.matmul(O[:, 0:256], Vp[:, jt, :], PTc[:, jt, :], start=(jt == 0), stop=(jt == 1)), reads=[kVp, 'PTc'], writes=['ps7'])
                self.attn_fin(O, 256, 'ps7', ost[0:64, 0:256], kost, rec, 'rec')
            P.dma('pool', self.S["oT"][1, h * 64:(h + 1) * 64, :], ost[0:64, :], reads=[kost], writes=['oT'])

    def phase_gla(self, l):
        P = self.P
        self.new_phase()
        STG = GLA_STAGE

        def PO(stg, *a, **k):
            if STG >= stg:
                P.op(*a, **k)

        need_ctx = (l == 0)
        gc = self.alloc(512).rearrange("p (a t) -> p a t", a=4)
        P.dma('sp', gc, self.I["gla_c"].rearrange("a p t -> p a t"), writes=['gc'])
        onesM = self.alloc(128)
        P.op('pool', lambda e: e.memset(onesM, 1.0 / 128), writes=['onesM'])
        gnorm = self.alloc(1)
        P.dma('sp', gnorm, self.I["gla_norm"][l, :].rearrange("(p o) -> p o", o=1), writes=['gnorm'])
        wgk = [self.alloc(256) for _ in range(2)]
        brow = [self.alloc(256) for _ in range(2)]
        lowb = [self.alloc(NTOK, BF16) for _ in range(2)]
        low = [self.alloc(NTOK) for _ in range(2)]
        for d_, nm in enumerate(["fwd", "bwd"]):
            P.dma('sp', wgk[d_][0:16, :], self.I[f"gla_w_gk_{nm}"][l, :, :], writes=[f"wgk{d_}"])
            P.dma('sp', brow[d_][0:1, :], self.I[f"gla_b_gk_{nm}"][l:l + 1, :], writes=[f"brow{d_}"])
            P.dma('sp', lowb[d_][0:16, :], self.S["proj_fm"][FM_LOW + 16 * d_:FM_LOW + 16 * d_ + 16, :], writes=[f"lowb{d_}"])
            P.op('dve', lambda e, d_=d_: e.tensor_copy(low[d_][0:16, :], lowb[d_][0:16, :]), reads=[f"lowb{d_}"], writes=[f"low{d_}"])
        qT = self.alloc(NTOK, BF16); kT = self.alloc(NTOK, BF16)
        vt = self.alloc(NT * 256, BF16).rearrange("p (t c) -> p t c", c=256)
        obuf = self.alloc(2 * NTOK, BF16).rearrange("p (h t) -> p h t", h=2)
        ogT = self.alloc(2 * NTOK, BF16).rearrange("p (h t) -> p h t", h=2)
        gst = self.alloc(2 * NTOK, BF16).rearrange("p (h t) -> p h t", h=2)
        Sf = [self.alloc(256) for _ in range(2)]
        Sb = [self.alloc(256, BF16) for _ in range(2)]
        e1 = self.alloc(128); sp_ = self.alloc(128); Eq = self.alloc(128); Ek = self.alloc(128)
        kin = self.alloc(128, BF16)
        qinz = [self.alloc(128, BF16) for _ in range(2)]
        kintz = [self.alloc(128, BF16) for _ in range(2)]
        for i_ in range(2):
            P.op('pool', lambda e, i_=i_: e.memset(qinz[i_], 0.0), writes=['qin'])
            P.op('pool', lambda e, i_=i_: e.memset(kintz[i_], 0.0), writes=['kint'])
        am = self.alloc(256, BF16).rearrange("p (h t) -> p h t", h=2)
        Kp = self.alloc(512).rearrange("p (c n) -> p c n", c=2)
        ot = self.alloc(128); sq = self.alloc(128); sd = self.alloc(128); on = self.alloc(128); sg = self.alloc(128)
        pz, pb_, pkt, pa, pK, pO, pss = self.ps[0], self.ps[1], self.psb(2), self.ps[3], self.ps[4], self.ps[5], self.ps[6]
        pa3 = pa[:, 0:256].rearrange("p (h t) -> p h t", h=2)
        pK3 = pK[:, :].rearrange("p (c n) -> p c n", c=2)
        pO3 = [pO[:, 0:128], self.ps[7][:, 0:128]]
        kpO = ['ps5', 'ps7']
        for pr in range(2):
            P.dma('sp', qT, self.S["proj_fm"][FM_GQ + pr * 128:FM_GQ + (pr + 1) * 128, :], writes=['qT'])
            P.dma('sp', kT, self.S["proj_fm"][FM_GK + pr * 128:FM_GK + (pr + 1) * 128, :], writes=['kT'])
            P.dma('sp', vt, self.S["proj_tm"][:, 2720 + pr * 256:2720 + (pr + 1) * 256].rearrange("(t p) c -> p t c", p=128), writes=['vt'])
            for hh in range(2):
                P.dma('sp', ogT[:, hh, :], self.S["proj_fm"][FM_OG + (2 * pr + hh) * 128:FM_OG + (2 * pr + hh + 1) * 128, :], writes=['ogT'])
            for d_ in (1, 0):
                tiles = [1, 0] + list(range(NT - 1, 1, -1)) if d_ == 1 else list(range(NT))
                corder = (1, 0) if d_ == 1 else (0, 1)
                cur = 0
                P.op('pool', lambda e: e.memset(Sf[0], 0.0), writes=['Sf0'])
                P.op('pool', lambda e: e.memset(Sb[0], 0.0), writes=['Sb0'])
                if GLA_NT:
                    tiles = tiles[:GLA_NT]
                for tt in tiles:
                    sl = slice(tt * 128, (tt + 1) * 128)
                    isq = need_ctx or tt >= 2
                    PO(1, 'pe', lambda e, sl=sl, d_=d_, pr=pr: e.matmul(pz[:, 0:128], low[d_][0:16, sl], wgk[d_][0:16, pr * 128:(pr + 1) * 128], start=True, stop=False), reads=[f"low{d_}", f"wgk{d_}"], writes=['ps0'])
                    PO(1, 'pe', lambda e, d_=d_, pr=pr: e.matmul(pz[:, 0:128], self.onesf[0:1, :], brow[d_][0:1, pr * 128:(pr + 1) * 128], start=False, stop=True), reads=[f"brow{d_}", 'onesf'], writes=['ps0'])
                    PO(1, 'act', lambda e: e.activation(e1, pz[:, 0:128], AF.Exp, scale=-1.0), reads=['ps0'], writes=['e1'])
                    PO(1, 'act', lambda e: e.activation(sp_, e1, AF.Ln, bias=1.0), reads=['e1'], writes=['sp'])
                    PO(2, 'pe', lambda e, d_=d_: e.matmul(pb_[:, 0:128], sp_, gc[:, d_, :], start=True, stop=True), reads=['sp', 'gc'], writes=['ps1'])
                    PO(2, 'act', lambda e: e.activation(Eq, pb_[:, 0:128], AF.Exp), reads=['ps1'], writes=['Eq'])
                    PO(2, 'act', lambda e: e.activation(Ek, pb_[:, 0:128], AF.Exp, scale=-1.0), reads=['ps1'], writes=['Ek'])
                    for hh in range(2):
                        PO(3, 'dve', lambda e, sl=sl, hh=hh: e.tensor_tensor(qinz[hh][hh * 64:(hh + 1) * 64, :], qT[hh * 64:(hh + 1) * 64, sl], Eq[hh * 64:(hh + 1) * 64, :], ALU.mult), reads=['qT', 'Eq'], writes=['qin'])
                    PO(3, 'dve', lambda e, sl=sl: e.tensor_tensor(kin, kT[:, sl], Ek, ALU.mult), reads=['kT', 'Ek'], writes=['kin'])
                    PO(3, 'pe', lambda e: e.transpose(pkt[:, 0:128], kin, self.identb), reads=['kin', 'identb'], writes=['ps2'])
                    for c in range(2):
                        PO(3, 'act', lambda e, c=c: e.copy(kintz[c][c * 64:(c + 1) * 64, :], pkt[c * 64:(c + 1) * 64, 0:128]), reads=['ps2'], writes=['kint'])
                    if isq:
                        for hh in range(2):
                            PO(4, 'pe', lambda e, hh=hh: e.matmul(pa3[:, hh, :], kin, qinz[hh], start=True, stop=True), reads=['kin', 'qin'], writes=['ps3'])
                        PO(4, 'dve', lambda e, d_=d_: e.tensor_tensor(am, pa3, gc[:, 2 + d_, :].unsqueeze(1).broadcast_to([128, 2, 128]), ALU.mult), reads=['ps3', 'gc'], writes=['am'])
                    for c in corder:
                        PO(5, 'pe', lambda e, c=c, tt=tt: e.matmul(pK3[:, c, :], kintz[c], vt[:, tt, :], start=True, stop=True), reads=['kint', 'vt'], writes=['ps4'])
                        di = (c * 64 + 63) if d_ == 0 else (c * 64)
                        PO(5, 'dve', lambda e, c=c, di=di: e.tensor_scalar(Kp[:, c, :], pK3[:, c, :], Eq[:, di:di + 1], None, ALU.mult), reads=['ps4', 'Eq'], writes=[f"Kp{c}"])
                    if isq:
                        for hh in range(2):
                            PO(6, 'pe', lambda e, hh=hh, tt=tt: e.matmul(pO3[hh], vt[:, tt, hh * 128:(hh + 1) * 128], am[:, hh, :], start=True, stop=False), reads=['vt', 'am'], writes=[kpO[hh]])
                    for ci, c in enumerate(corder):
                        if isq:
                            for hh in range(2):
                                PO(6, 'pe', lambda e, hh=hh, c=c, cur=cur, ci=ci: e.matmul(pO3[hh][:, c * 64:(c + 1) * 64], Sb[cur][:, hh * 128:(hh + 1) * 128], qinz[hh][:, c * 64:(c + 1) * 64], start=False, stop=(ci == 1)), reads=[f"Sb{cur}", 'qin'], writes=[kpO[hh]])
                        di = (c * 64 + 63) if d_ == 0 else (c * 64)
                        nxt = 1 - cur
                        PO(7, 'dve', lambda e, c=c, di=di, cur=cur, nxt=nxt: e.scalar_tensor_tensor(Sf[nxt], Sf[cur], Eq[:, di:di + 1], Kp[:, c, :], ALU.mult, ALU.add), reads=[f"Sf{cur}", 'Eq', f"Kp{c}"], writes=[f"Sf{nxt}"])
                        PO(7, 'act', lambda e, nxt=nxt: e.copy(Sb[nxt], Sf[nxt]), reads=[f"Sf{nxt}"], writes=[f"Sb{nxt}"])
                        cur = nxt
                    if not isq:
                        continue
                    if d_ == 1:
                        for hh in range(2):
                            PO(8, 'act', lambda e, sl=sl, hh=hh: e.copy(obuf[:, hh, sl], pO3[hh]), reads=[kpO[hh]], writes=['obuf'])
                    else:
                        for hh in range(2):
                            PO(8, 'dve', lambda e, hh=hh, sl=sl: e.tensor_tensor(ot, pO3[hh], obuf[:, hh, sl], ALU.add), reads=[kpO[hh], 'obuf'], writes=['ot'])
                            PO(8, 'act', lambda e: e.activation(sq, ot, AF.Square), reads=['ot'], writes=['sq'])
                            PO(8, 'pe', lambda e: e.matmul(pss[:, 0:128], onesM, sq, start=True, stop=True), reads=['onesM', 'sq'], writes=['ps6'])
                            PO(8, 'act', lambda e: e.activation(sd, pss[:, 0:128], AF.Sqrt, bias=EPS), reads=['ps6'], writes=['sd'])
                            PO(8, 'dve', lambda e: e.reciprocal(sd, sd), reads=['sd'], writes=['sd'])
                            PO(8, 'dve', lambda e: e.scalar_tensor_tensor(on, ot, gnorm[:, 0:1], sd, ALU.mult, ALU.mult), reads=['ot', 'gnorm', 'sd'], writes=['on'])
                            PO(8, 'act', lambda e, hh=hh, sl=sl: e.activation(sg, ogT[:, hh, sl], AF.Silu), reads=['ogT'], writes=['sg'])
                            PO(8, 'dve', lambda e, hh=hh, sl=sl: e.tensor_tensor(gst[:, hh, sl], on, sg, ALU.mult), reads=['on', 'sg'], writes=['gst'])
            for hh in range(2):
                P.dma('pool', self.S["oT"][2, (2 * pr + hh) * 128:(2 * pr + hh + 1) * 128, :], gst[:, hh, :], reads=['gst'], writes=['oT'])

    def load_w_bf16(self, dst, src3, nk, ncol, tmp, name):
        P = self.P
        for k in range(nk):
            P.dma('sp', tmp, src3[k * 128:(k + 1) * 128, :], writes=['wtmpm'])
            P.op('pool', lambda e, k=k: e.tensor_copy(dst[:, k, :], tmp), reads=['wtmpm'], writes=[name])

    def phase_merge(self, l):
        P = self.P
        self.new_phase()
        wo = [self.alloc(4 * D, BF16).rearrange("p (k n) -> p k n", k=4) for _ in range(3)]
        wout = self.alloc(8 * D, BF16).rearrange("p (k n) -> p k n", k=8)
        tmp = self.alloc(D)
        for br, nm in enumerate(["w_o_mla", "w_o_na", "w_o_gla"]):
            self.load_w_bf16(wo[br], self.I[nm][l], 4, D, tmp, f"wo{br}")
        self.load_w_bf16(wout, self.I["w_out"][l], 8, D, tmp, 'wout')
        g1 = [self.alloc(D) for _ in range(2)]
        for r in range(2):
            self.bvec(g1[r], self.modrow(l, r, 2), f"g1{r}")
        oTt = [[self.alloc(512, BF16).rearrange("p (k t) -> p k t", k=4) for _ in range(3)] for _ in range(2)]
        gts = [self.alloc(3 * D, BF16) for _ in range(2)]
        xts = [self.alloc(D) for _ in range(2)]
        y = self.alloc(D); tb = self.alloc(D); yb = self.alloc(D, BF16)
        yT = self.alloc(D, BF16).rearrange("p (k t) -> p k t", k=8)
        x1 = [self.alloc(D) for _ in range(2)]
        for idx, tt in enumerate(self.qtiles(l)):
            b = idx % 2
            sl = slice(tt * 128, (tt + 1) * 128)
            for br in range(3):
                P.dma('sp', oTt[b][br], self.S["oT"][br, :, sl].rearrange("(k p) t -> p k t", p=128), writes=[f"oTt{b}{br}"])
            P.dma('sp', gts[b], self.S["proj_tm"][sl, C_GATE:C_GATE + 3 * D], writes=[f"gts{b}"])
            P.dma('sp', xts[b], self.S["xs"][sl, :], writes=[f"xt{b}"])
            for br in range(3):
                pbk = (0, 2)[br % 2]
                for half in range(2):
                    pst = self.ps[pbk + half]
                    for k in range(4):
                        P.op('pe', lambda e, k=k, half=half, pst=pst, br=br, b=b: e.matmul(pst[:, :], oTt[b][br][:, k, :], wo[br][:, k, half * 512:(half + 1) * 512], start=(k == 0), stop=(k == 3)), reads=[f"oTt{b}{br}", f"wo{br}"], writes=[f"ps{pbk + half}"])
                    hs = slice(half * 512, (half + 1) * 512)
                    gs = slice(br * D + half * 512, br * D + (half + 1) * 512)
                    if br == 0:
                        P.op('dve', lambda e, pst=pst, hs=hs, gs=gs, b=b: e.tensor_tensor(y[:, hs], pst[:, :], gts[b][:, gs], ALU.mult), reads=[f"ps{pbk + half}", f"gts{b}"], writes=['y'])
                    else:
                        P.op('dve', lambda e, pst=pst, hs=hs, gs=gs, b=b: e.tensor_tensor(tb[:, hs], pst[:, :], gts[b][:, gs], ALU.mult), reads=[f"ps{pbk + half}", f"gts{b}"], writes=['tb'])
                        if br == 1:
                            P.op('pool', lambda e, hs=hs: e.tensor_tensor(y[:, hs], y[:, hs], tb[:, hs], ALU.add), reads=['tb', 'y'], writes=['y'])
                        else:
                            P.op('pool', lambda e, hs=hs: e.tensor_tensor(yb[:, hs], y[:, hs], tb[:, hs], ALU.add), reads=['tb', 'y'], writes=['yb'])
            pv = self.psb(4).rearrange("p (k t) -> p k t", k=8)
            for k in range(8):
                P.op('pe', lambda e, k=k: e.transpose(pv[:, k, :], yb[:, k * 128:(k + 1) * 128], self.identb), reads=['yb', 'identb'], writes=['ps4'])
            P.op('act', lambda e: e.copy(yT, pv), reads=['ps4'], writes=['yT'])
            for half in range(2):
                pst = self.ps[5 + half]
                for k in range(8):
                    P.op('pe', lambda e, k=k, half=half, pst=pst: e.matmul(pst[:, :], yT[:, k, :], wout[:, k, half * 512:(half + 1) * 512], start=(k == 0), stop=(k == 7)), reads=['yT', 'wout'], writes=[f"ps{5 + half}"])
                hs = slice(half * 512, (half + 1) * 512)
                r = 1 if tt < 2 else 0
                P.op('dve', lambda e, pst=pst, hs=hs, r=r: e.tensor_tensor(tb[:, hs], pst[:, :], g1[r][:, hs], ALU.mult), reads=[f"ps{5 + half}", f"g1{r}"], writes=['tb'])
                P.op('pool', lambda e, hs=hs, b=b: e.tensor_tensor(x1[b][:, hs], tb[:, hs], xts[b][:, hs], ALU.add), reads=['tb', f"xt{b}"], writes=[f"x1{b}"])
            P.dma('pool', self.S["xmid"][sl, :], x1[b], reads=[f"x1{b}"], writes=['xmid'])

    def phase_peer_prep(self, l):
        P = self.P
        self.new_phase()
        uf = [self.alloc(D) for _ in range(2)]
        ub = [self.alloc(D, BF16) for _ in range(2)]
        us = [self.alloc(D, BF16).rearrange("p (k e) -> p k e", k=8) for _ in range(2)]
        vf = [self.alloc(D) for _ in range(2)]
        vb = [self.alloc(D, BF16) for _ in range(2)]
        for et in range(128):
            b = et % 2
            es = slice(et * 128, (et + 1) * 128)
            P.dma('sp', uf[b], self.I["peer_u"][l, es, :], writes=[f"uf{b}"])
            P.op('dve', lambda e, b=b: e.tensor_copy(ub[b], uf[b]), reads=[f"uf{b}"], writes=[f"ub{b}"])
            pv = self.psb(b).rearrange("p (k e) -> p k e", k=8)
            for k in range(8):
                P.op('pe', lambda e, k=k, b=b, pv=pv: e.transpose(pv[:, k, :], ub[b][:, k * 128:(k + 1) * 128], self.identb), reads=[f"ub{b}", 'identb'], writes=[f"ps{b}"])
            P.op('act', lambda e, b=b, pv=pv: e.copy(us[b], pv), reads=[f"ps{b}"], writes=[f"us{b}"])
            P.dma('pool', self.S["uT"][:, :, es].rearrange("k p e -> p k e"), us[b], reads=[f"us{b}"], writes=['uT'])
            P.dma('sp', vf[b], self.I["peer_v"][l, es, :], writes=[f"vf{b}"])
            P.op('pool', lambda e, b=b: e.tensor_copy(vb[b], vf[b]), reads=[f"vf{b}"], writes=[f"vb{b}"])
            P.dma('pool', self.S["vb"][es, :], vb[b], reads=[f"vb{b}"], writes=['vbd'])

    def phase_peer(self, l):
        P = self.P
        self.new_phase()
        need_ctx = (l == 0)
        mv = self.load_modvecs(l, "norm2", 3, 4)
        g2 = [self.alloc(D) for _ in range(2)]
        for r in range(2):
            self.bvec(g2[r], self.modrow(l, r, 5), f"g2{r}")
        wqmem = self.alloc(8 * D)
        wq = wqmem.rearrange("p (k n) -> p k n", k=8)
        dE = [wqmem[:, i_ * 2048:(i_ + 1) * 2048].rearrange("p (h i j) -> p h i j", h=4, i=4) for i_ in range(4)]
        dEk = ['dE0', 'dE1', 'dE2', 'dE3']
        Dg = [[self.alloc(128, BF16) for _ in range(8)] for _ in range(2)]
        skn = self.alloc(128)
        skT = self.alloc(128)
        for p_ in range(2):
            P.dma('sp', skn[:, p_ * 64:(p_ + 1) * 64], self.I["peer_sub_keys"][l, p_, :, :], writes=['skn'])
        P.op('pe', lambda e: e.transpose(self.ps[0][:, 0:128], skn, self.identf), reads=['skn', 'identf'], writes=['ps0'])
        skTz = [self.alloc(128) for _ in range(2)]
        for a_ in range(2):
            P.op('pool', lambda e, a_=a_: e.memset(skTz[a_], 0.0), writes=['skT'])
        for a_ in range(2):
            P.op('act', lambda e, a_=a_: e.copy(skTz[a_][a_ * 64:(a_ + 1) * 64, :], self.ps[0][a_ * 64:(a_ + 1) * 64, 0:128]), reads=['ps0'], writes=['skT'])
        x1t = [self.alloc(D) for _ in range(2)]
        h2f = self.alloc(D); h2b = self.alloc(D, BF16); tmpf = self.alloc(D)
        h2Tb = self.alloc(8 * 256, BF16).rearrange("p (k t) -> p k t", k=8)
        h2Tf = self.alloc(8 * 256).rearrange("p (k t) -> p k t", k=8)
        qTf = self.alloc(8 * 256).rearrange("p (h t) -> p h t", h=8)
        s_tok = [self.alloc(2048).rearrange("p (h a n) -> p h a n", h=8, a=2) for _ in range(2)]
        s1n = [self.alloc(1024).rearrange("p (h n) -> p h n", h=8) for _ in range(2)]
        v16 = self.alloc(256).rearrange("p (h a n) -> p h a n", h=8, a=2)
        tmp128 = self.alloc(128)
        cand = self.alloc(2048).rearrange("p (h n) -> p h n", h=8)
        c24 = self.alloc(192).rearrange("p (h n) -> p h n", h=8)
        tmpc = self.alloc(256); tmpc2 = self.alloc(256); ec = self.alloc(256); junkc = self.alloc(256)
        sm = [self.alloc(64) for _ in range(2)]
        stt = self.alloc(4)
        mB = [self.alloc(2048, BF16).rearrange("p (h i j) -> p h i j", h=4, i=4) for _ in range(2)]
        uTs = [self.alloc(8 * 512, BF16).rearrange("p (k e) -> p k e", k=8) for _ in range(2)]
        vss = [self.alloc(4 * D, BF16).rearrange("p (c d) -> p c d", c=4) for _ in range(2)]
        ga_ = self.alloc(1024).rearrange("p (i t) -> p i t", i=4)
        Wt_ = self.alloc(1024, BF16).rearrange("p (i t) -> p i t", i=4)
        ga = [ga_, ga_]
        Wt = [Wt_, Wt_]
        tb = self.alloc(D); x2 = [tmpf, tmpf]
        groups = ([0] if need_ctx else []) + list(range(2, NT, 2))
        if PEER_GROUPS:
            groups = groups[:PEER_GROUPS]
        ecnt = 0
        wcnt = 0
        for g0 in groups:
            r = 1 if g0 < 2 else 0
            weff, kw, sh, ks = mv[r]
            for ti in range(2):
                tt = g0 + ti
                sl = slice(tt * 128, (tt + 1) * 128)
                ts = slice(ti * 128, (ti + 1) * 128)
                P.dma('sp', x1t[ti], self.S["xmid"][sl, :], writes=[f"x1t{ti}"])
                self.norm_mod(x1t[ti], f"x1t{ti}", h2f, 'h2f', weff, kw, sh, ks, tmpf, 'tmpf', stt, 'stt')
                P.op('act', lambda e: e.copy(h2b, h2f), reads=['h2f'], writes=['h2b'])
                pv = self.psb(0).rearrange("p (k t) -> p k t", k=8)
                for k in range(8):
                    P.op('pe', lambda e, k=k, pv=pv: e.transpose(pv[:, k, :], h2b[:, k * 128:(k + 1) * 128], self.identb), reads=['h2b', 'identb'], writes=['ps0'])
                P.op('act', lambda e, pv=pv, ts=ts: e.copy(h2Tb[:, :, ts], pv), reads=['ps0'], writes=['h2Tb'])
                for half in range(2):
                    pf = self.ps[1 + half][:, :].rearrange("p (k t) -> p k t", k=4)
                    for k4 in range(4):
                        k = half * 4 + k4
                        P.op('pe', lambda e, k=k, k4=k4, pf=pf: e.transpose(pf[:, k4, :], h2f[:, k * 128:(k + 1) * 128], self.identf), reads=['h2f', 'identf'], writes=[f"ps{1 + half}"])
                    P.op('dve', lambda e, pf=pf, half=half, ts=ts: e.tensor_copy(h2Tf[:, half * 4:(half + 1) * 4, ts], pf), reads=[f"ps{1 + half}"], writes=['h2Tf'])
            if PEER_STAGE < 2:
                continue
            P.dma('sp', wq, self.I["peer_w_q"][l].rearrange("(k p) n -> p k n", p=128), writes=['wq'] + dEk)
            for cc in range(8):
                pst = self.ps[cc % 2]
                for k in range(8):
                    P.op('pe', lambda e, k=k, cc=cc, pst=pst: e.matmul(pst[:, 0:256], wq[:, k, cc * 128:(cc + 1) * 128], h2Tf[:, k, :], start=(k == 0), stop=(k == 7)), reads=['wq', 'h2Tf'] + dEk, writes=[f"ps{cc % 2}"])
                P.op('act', lambda e, cc=cc, pst=pst: e.copy(qTf[:, cc, :], pst[:, 0:256]), reads=[f"ps{cc % 2}"], writes=['qTf'])
            for ti in (range(2) if PEER_STAGE >= 3 else []):
                ts = slice(ti * 128, (ti + 1) * 128)
                st_ = s_tok[ti]
                for h4 in range(4):
                    pst = self.ps[2 + (h4 % 2)]
                    ps4 = pst[:, :].rearrange("p (h a n) -> p h a n", h=2, a=2)
                    for hl in range(2):
                        h = h4 * 2 + hl
                        for a in range(2):
                            P.op('pe', lambda e, h=h, hl=hl, a=a, ps4=ps4, ts=ts: e.matmul(ps4[:, hl, a, :], qTf[:, h, ts], skTz[a], start=True, stop=True), reads=['qTf', 'skT'], writes=[f"ps{2 + (h4 % 2)}"])
                    P.op('act', lambda e, h4=h4, ps4=ps4, st_=st_: e.copy(st_[:, h4 * 2:h4 * 2 + 2, :, :], ps4), reads=[f"ps{2 + (h4 % 2)}"], writes=[f"stok{ti}"])
                sm_ = sm[ti]
                ksm = f"sm{ti}"
                if PEER_SUB < 2:
                    continue
                for h in range(8):
                    for a in range(2):
                        P.op('dve', lambda e, h=h, a=a, st_=st_: e.max(v16[:, h, a, 0:8], st_[:, h, a, :]), reads=[f"stok{ti}"], writes=['v16'])
                        P.op('dve', lambda e, h=h, a=a, st_=st_: e.match_replace(tmp128, v16[:, h, a, 0:8], st_[:, h, a, :], -1e30), reads=[f"stok{ti}", 'v16'], writes=['tmp128'])
                        P.op('dve', lambda e, h=h, a=a: e.max(v16[:, h, a, 8:16], tmp128), reads=['tmp128'], writes=['v16'])
                    if PEER_SUB < 3:
                        continue
                    ch = cand[:, h, :]
                    P.op('dve', lambda e, h=h, ch=ch: e.tensor_tensor(ch.rearrange("p (a b) -> p a b", a=16), v16[:, h, 0, :].unsqueeze(2).broadcast_to([128, 16, 16]), v16[:, h, 1, :].unsqueeze(1).broadcast_to([128, 16, 16]), ALU.add), reads=['v16'], writes=['cand'])
                    P.op('dve', lambda e, h=h, ch=ch: e.max(c24[:, h, 0:8], ch), reads=['cand'], writes=['c24'])
                    P.op('dve', lambda e, h=h, ch=ch: e.match_replace(tmpc, c24[:, h, 0:8], ch, -1e30), reads=['cand', 'c24'], writes=['tmpc'])
                    P.op('dve', lambda e, h=h: e.max(c24[:, h, 8:16], tmpc), reads=['tmpc'], writes=['c24'])
                    P.op('dve', lambda e, h=h: e.match_replace(tmpc2, c24[:, h, 8:16], tmpc, -1e30), reads=['tmpc', 'c24'], writes=['tmpc2'])
                    P.op('dve', lambda e, h=h: e.max(c24[:, h, 16:24], tmpc2), reads=['tmpc2'], writes=['c24'])
                if PEER_SUB < 4:
                    continue
                P.op('dve', lambda e, sm_=sm_: e.tensor_tensor(sm_[:, 0:8], c24[:, :, 15], c24[:, :, 16], ALU.add), reads=['c24'], writes=[ksm])
                P.op('dve', lambda e, sm_=sm_: e.tensor_scalar(sm_[:, 0:8], sm_[:, 0:8], 0.5, None, ALU.mult), reads=[ksm], writes=[ksm])
                P.op('dve', lambda e, sm_=sm_: e.tensor_scalar(sm_[:, 8:16], c24[:, :, 0], -1.0, None, ALU.mult), reads=['c24'], writes=[ksm])
                for h in range(8):
                    ch = cand[:, h, :]
                    P.op('act', lambda e, h=h, ch=ch, sm_=sm_: e.activation(ec, ch, AF.Exp, bias=sm_[:, 8 + h:9 + h]), reads=['cand', ksm], writes=['ec'])
                    P.op('dve', lambda e, h=h, ch=ch, sm_=sm_: e.scalar_tensor_tensor(junkc, ch, sm_[:, h:h + 1], ec, ALU.is_ge, ALU.mult, accum_out=sm_[:, 16 + h:17 + h]), reads=['cand', 'ec', ksm], writes=['junkc', ksm])
                P.op('act', lambda e, sm_=sm_: e.activation(sm_[:, 24:32], sm_[:, 16:24], AF.Ln), reads=[ksm], writes=[ksm])
                P.op('dve', lambda e, sm_=sm_: e.tensor_tensor(sm_[:, 32:40], sm_[:, 8:16], sm_[:, 24:32], ALU.subtract), reads=[ksm], writes=[ksm])
                P.op('dve', lambda e, sm_=sm_: e.tensor_tensor(sm_[:, 40:48], sm_[:, 0:8], sm_[:, 32:40], ALU.add), reads=[ksm], writes=[ksm])
                P.op('act', lambda e, sm_=sm_: e.activation(sm_[:, 48:56], sm_[:, 40:48], AF.Exp), reads=[ksm], writes=[ksm])
                for h in range(8):
                    P.op('dve', lambda e, h=h, st_=st_, sm_=sm_, ti=ti: e.tensor_scalar(s1n[ti][:, h, :], st_[:, h, 0, :], sm_[:, h:h + 1], None, ALU.subtract), reads=[f"stok{ti}", ksm], writes=[f"s1n{ti}"])
                    P.op('dve', lambda e, h=h, sm_=sm_, ti=ti: e.tensor_scalar(Dg[ti][h], self.identb, sm_[:, 48 + h:49 + h], None, ALU.mult), reads=['identb', ksm], writes=[f"Dg{ti}"])
            Gps = [self.ps[0], self.ps[1]]
            Aps = [self.ps[2], self.ps[3]]
            Ops = [[self.ps[4], self.ps[5]], [self.ps[6], self.ps[7]]]
            if PEER_STAGE < 4:
                continue
            for ib in range(32):
                wb_ = wcnt % 2
                wcnt += 1
                uT_, vs_ = uTs[wb_], vss[wb_]
                P.dma('sp', uT_, self.S["uT"][:, :, ib * 512:(ib + 1) * 512].rearrange("k p e -> p k e"), writes=[f"uT{wb_}"])
                P.dma('sp', vs_, self.S["vb"][ib * 512:(ib + 1) * 512, :].rearrange("(c j) d -> j c d", j=128), writes=[f"vs{wb_}"])
                for ti in range(2):
                    st_ = s_tok[ti]
                    for hb in range(2):
                        eb = ecnt % 2
                        ecnt += 1
                        d_, e_, m_ = dE[eb], dE[2 + eb], mB[hb]
                        kd, ke = dEk[eb], dEk[2 + eb]
                        P.op('dve', lambda e, d_=d_, st_=st_, hb=hb, ti=ti, ib=ib: e.tensor_tensor(d_, st_[:, hb * 4:(hb + 1) * 4, 1, :].unsqueeze(2).broadcast_to([128, 4, 4, 128]), s1n[ti][:, hb * 4:(hb + 1) * 4, ib * 4:(ib + 1) * 4].unsqueeze(3).broadcast_to([128, 4, 4, 128]), ALU.add), reads=[f"stok{ti}", f"s1n{ti}"], writes=[kd])
                        P.op('act', lambda e, d_=d_, e_=e_: e.activation(e_, d_, AF.Exp), reads=[kd], writes=[ke])
                        P.op('dve', lambda e, d_=d_, e_=e_, m_=m_: e.scalar_tensor_tensor(m_, d_, 0.0, e_, ALU.is_ge, ALU.mult), reads=[kd, ke], writes=[f"mB{hb}"])
                    for il in range(4):
                        G_ = Gps[il // 2]
                        for h in range(8):
                            P.op('pe', lambda e, G_=G_, il=il, ti=ti, h=h: e.matmul(G_[:, (il % 2) * 256 + ti * 128:(il % 2) * 256 + (ti + 1) * 128], mB[h // 4][:, h % 4, il, :], Dg[ti][h], start=(h == 0), stop=(h == 7)), reads=[f"mB{h // 4}", f"Dg{ti}"], writes=[f"ps{il // 2}"])
                for il in range(4):
                    A_ = Aps[il // 2]
                    for k in range(8):
                        P.op('pe', lambda e, A_=A_, il=il, k=k, uT_=uT_: e.matmul(A_[:, (il % 2) * 256:(il % 2 + 1) * 256], uT_[:, k, il * 128:(il + 1) * 128], h2Tb[:, k, :], start=(k == 0), stop=(k == 7)), reads=[f"uT{wb_}", 'h2Tb'], writes=[f"ps{2 + il // 2}"])
                gb = 0
                for hf in range(2):
                    P.op('act', lambda e, hf=hf, gb=gb: e.activation(ga[gb][:, hf * 2:hf * 2 + 2, :], Aps[hf][:, :].rearrange("p (i t) -> p i t", i=2), AF.Gelu_apprx_tanh), reads=[f"ps{2 + hf}"], writes=[f"ga{gb}"])
                    P.op('dve', lambda e, hf=hf, gb=gb: e.tensor_tensor(Wt[gb][:, hf * 2:hf * 2 + 2, :], Gps[hf][:, :].rearrange("p (i t) -> p i t", i=2), ga[gb][:, hf * 2:hf * 2 + 2, :], ALU.mult), reads=[f"ps{hf}", f"ga{gb}"], writes=[f"Wt{gb}"])
                for il in range(4):
                    i = ib * 4 + il
                    for ti in range(2):
                        for half in range(2):
                            P.op('pe', lambda e, il=il, ti=ti, half=half, i=i, gb=gb, vs_=vs_: e.matmul(Ops[ti][half][:, :], Wt[gb][:, il, ti * 128:(ti + 1) * 128], vs_[:, il, half * 512:(half + 1) * 512], start=(i == 0), stop=(i == 127)), reads=[f"Wt{gb}", f"vs{wb_}"], writes=[f"ps{4 + 2 * ti + half}"])
            for ti in range(2):
                tt = g0 + ti
                sl = slice(tt * 128, (tt + 1) * 128)
                for half in range(2):
                    hs = slice(half * 512, (half + 1) * 512)
                    P.op('dve', lambda e, ti=ti, half=half, hs=hs, r=r: e.tensor_tensor(tb[:, hs], Ops[ti][half][:, :], g2[r][:, hs], ALU.mult), reads=[f"ps{4 + 2 * ti + half}", f"g2{r}"], writes=['tb'])
                    P.op('pool', lambda e, ti=ti, hs=hs: e.tensor_tensor(x2[ti][:, hs], tb[:, hs], x1t[ti][:, hs], ALU.add), reads=['tb', f"x1t{ti}"], writes=["tmpf"])
                P.dma('pool', self.S["xs"][sl, :], x2[ti], reads=["tmpf"], writes=['xs'])

    def phase_final(self):
        P = self.P
        self.new_phase()
        fn = self.alloc(D)
        self.bvec(fn, self.I["final_norm"].rearrange("(o d) -> o d", o=1), 'fn')
        xts = [self.alloc(D) for _ in range(2)]
        os_ = [self.alloc(D) for _ in range(2)]
        tmpf = self.alloc(D)
        stt = self.alloc(4 * NT)
        for tt in range(2, NT):
            b = tt % 2
            st = stt[:, 4 * tt:4 * tt + 4]
            kst = f"st{tt}"
            P.dma('sp', xts[b], self.S["xs"][tt * 128:(tt + 1) * 128, :], writes=[f"xt{b}"])
            P.op('act', lambda e, b=b, st=st: e.activation(tmpf, xts[b], AF.Square, accum_out=st[:, 0:1]), reads=[f"xt{b}"], writes=['tmpf', kst])
            P.op('act', lambda e, st=st: e.activation(st[:, 1:2], st[:, 0:1], AF.Sqrt, bias=EPS, scale=1.0 / D), reads=[kst], writes=[kst])
            P.op('dve', lambda e, st=st: e.reciprocal(st[:, 2:3], st[:, 1:2]), reads=[kst], writes=[kst])
            P.op('dve', lambda e, b=b, st=st: e.scalar_tensor_tensor(os_[b], xts[b], st[:, 2:3], fn, ALU.mult, ALU.mult), reads=[f"xt{b}", kst, 'fn'], writes=[f"os{b}"])
            P.dma('pool', self.out[(tt - 2) * 128:(tt - 1) * 128, :], os_[b], reads=[f"os{b}"], writes=['out'])


def _consts():
    t = np.arange(NLAT)
    rows = (t // 64).astype(np.float32)
    cols = (t % 64).astype(np.float32)
    inv = (10000.0 ** (-np.arange(0, 16, 2, dtype=np.float32) / 16)).astype(np.float32)
    ar = rows[:, None] * inv
    ac = cols[:, None] * inv
    ang = np.concatenate([ar, ar, ac, ac], axis=-1).astype(np.float32)
    cos = np.cos(ang).astype(np.float32)
    sin = np.sin(ang).astype(np.float32)
    sgn = np.concatenate([-np.ones(8), np.ones(8), -np.ones(8), np.ones(8)]).astype(np.float32)
    sinS = sin * sgn
    k_cos = np.concatenate([np.ones((NCTX, 32), np.float32), cos], 0)
    k_sin = np.concatenate([np.zeros((NCTX, 32), np.float32), sinS], 0)
    q_cosT = np.ascontiguousarray((k_cos * np.float32(SC_MLA)).T)
    q_sinT = np.ascontiguousarray((k_sin * np.float32(SC_MLA)).T)
    s = np.arange(128)
    same = (s[:, None] // 64) == (s[None, :] // 64)
    le = s[:, None] <= s[None, :]
    ge = s[:, None] >= s[None, :]
    gla_c = np.stack([np.where(same & le, -1.0 / 16, 0.0), np.where(same & ge, -1.0 / 16, 0.0),
                      np.where(same & le, 1.0, 0.0), np.where(same & ge, 1.0, 0.0)]).astype(np.float32)
    return dict(k_cos=k_cos, k_sin=k_sin, q_cosT=q_cosT, q_sinT=q_sinT, gla_c=gla_c)


def _na_index():
    dr = np.zeros((128, 5, 7, 128), np.int64)
    dc = np.zeros((128, 5, 7, 128), np.int64)
    valid = np.zeros((128, 5, 7, 128), bool)
    kp = np.arange(128)[:, None]
    qi = np.arange(128)[None, :]
    for pc, m in enumerate([0, 1, 2, 30, 31]):
        kb = min(max(m - 2, 0), 27)
        for j in range(5):
            kr = 2 * (kb + j) + kp // 64
            wk = kp % 64
            r = 2 * m + qi // 64
            wq = qi % 64
            rs = np.clip(r - 4, 0, 56)
            cs = np.clip(wq - 8, 0, 48)
            ok = (kr >= rs) & (kr < rs + 8) & (wk >= cs) & (wk < cs + 16)
            valid[:, pc, j, :] = ok
            dr[:, pc, j, :] = np.clip(kr - r + 7, 0, 14)
            dc[:, pc, j, :] = np.clip(wk - wq, -15, 15) + 15
    valid[:, :, 5:7, :] = True
    return dr, dc, valid


def _prep_shared(inp):
    sh = {}
    w_in = inp["w_in"]
    kr = w_in[:, :, 640:672]
    swap = np.concatenate([kr[..., 8:16], kr[..., 0:8], kr[..., 24:32], kr[..., 16:24]], -1)
    sh["w_in_x"] = np.ascontiguousarray(np.concatenate([w_in, swap], -1))
    wuq = inp["mla_w_uq"].reshape(2, 384, 8, 96)
    rp = wuq[..., 64:96]
    rsw = np.concatenate([rp[..., 8:16], rp[..., 0:8], rp[..., 24:32], rp[..., 16:24]], -1)
    z = np.zeros((2, 384, 8, 64), np.float32)
    sh["w_uq_x"] = np.ascontiguousarray(np.concatenate([wuq, z, rsw], -1).reshape(2, 384, 1536))
    dr, dc, valid = _na_index()
    rpb = inp["na_rpb"]
    nb = rpb[:, :, dr, dc]
    nb = np.where(valid[None, None], nb, np.float32(0.0)).astype(np.float32)
    nb[:, :, :, :, 5:7, :] = 0.0
    sh["na_bias"] = np.ascontiguousarray(nb.reshape(2, 8, 128, 4480))
    sh["na_mask"] = np.ascontiguousarray(np.where(valid, 0.0, -1e30).astype(np.float32).reshape(128, 4480))
    sh.update(_consts())
    for k in ["w_ada", "b_ada", "norm1", "mla_q_norm", "mla_kv_norm", "mla_w_ukv", "gla_w_gk_fwd", "gla_b_gk_fwd",
              "gla_w_gk_bwd", "gla_b_gk_bwd", "gla_norm", "w_o_mla", "w_o_na", "w_o_gla", "w_out", "norm2", "peer_w_q",
              "peer_sub_keys", "peer_u", "peer_v", "final_norm"]:
        sh[k] = np.ascontiguousarray(inp[k], dtype=np.float32)
    return sh


_CACHE = {}


def build_nc(nlayers=2, dbg=False, phases=None, scr_in=()):
    nc = bass.Bass("TRN2", target_bir_lowering=False)
    with contextlib.ExitStack() as st:
        b = Builder(nc, st, nlayers=nlayers, dbg=dbg, phases=phases, scr_in=scr_in)
        b.build()
    return nc, b


def kernel(**inp):
    inp = {k: np.asarray(v) for k, v in inp.items()}
    sh = _prep_shared(inp)
    nc, b = build_nc()
    in_maps = []
    ncores = 4
    for c in range(ncores):
        m = dict(sh)
        m["x"] = np.ascontiguousarray(inp["x"][c], dtype=np.float32)
        m["ctx"] = np.ascontiguousarray(inp["ctx"][c], dtype=np.float32)
        m["cvecs"] = np.ascontiguousarray(np.stack([inp["c"][c], inp["c_ctx"]]), dtype=np.float32)
        in_maps.append({k: v for k, v in m.items() if k in b.I})
    res = run_bass_kernel_spmd(nc, in_maps, core_ids=list(range(ncores)))
    return np.stack([np.asarray(r["out"], dtype=np.float32) for r in res.results], 0)
```

```python
import contextlib
import types
import numpy as np
import concourse.bass as bass
import concourse.mybir as mybir
from concourse.bass_utils import run_bass_kernel_spmd

F32 = mybir.dt.float32
BF16 = mybir.dt.bfloat16
AF = mybir.ActivationFunctionType
ALU = mybir.AluOpType

SEM_LIMIT = 16000
DMA_LIMIT = 900
DMA_R = 6
SAMESYNC = True

D = 1024
NCTX = 256
NLAT = 4096
NTOK = NCTX + NLAT
NT = NTOK // 128
EPS = 1e-6
INX = 6880
C_GATE = 3776
FM_ROWS = 2080
FM_NAQ, FM_NAK, FM_GQ, FM_GK, FM_OG, FM_LOW = 0, 512, 1024, 1280, 1536, 2048
SC_MLA = 96 ** -0.5
NEXP = 16384


def _freeze(fn):
    if fn.__closure__:
        cells = []
        for c in fn.__closure__:
            try:
                cells.append(types.CellType(c.cell_contents))
            except ValueError:
                cells.append(c)
        fn = types.FunctionType(fn.__code__, fn.__globals__, fn.__name__, fn.__defaults__, tuple(cells))
    return fn


class Prog:
    ENG = ['pe', 'act', 'dve', 'pool', 'sp']

    def __init__(self, nc, stack):
        self.nc = nc
        self.stack = stack
        self.stream = {e: [] for e in self.ENG}
        self.sems = {}
        self.ecount = {e: 0 for e in self.ENG}
        self.known = {e: {} for e in self.ENG}
        self.evclock = {}
        self.lastw = {}
        self.readers = {}
        self.dslot = {}
        self.dcount = {}
        self.nops = 0

    def sem(self, key):
        if key not in self.sems:
            self.sems[key] = self.stack.enter_context(self.nc.semaphore("s" + "_".join(str(k) for k in key)))
        return self.sems[key]

    def _deps(self, eng, reads, writes, samesync):
        deps = {}

        def add(ev):
            sk, v = ev
            if not samesync and sk[0] == 'e' and sk[1] == eng:
                return
            if deps.get(sk, 0) < v:
                deps[sk] = v
        for k in reads:
            if k in self.lastw:
                add(self.lastw[k])
        for k in writes:
            if k in self.lastw:
                add(self.lastw[k])
            for ev in self.readers.get(k, ()):
                add(ev)
        kn = self.known[eng]
        waits = []
        for sk, v in deps.items():
            if kn.get(sk, 0) >= v:
                continue
            waits.append((sk, v))
        for sk, v in waits:
            if kn.get(sk, 0) < v:
                kn[sk] = v
            for sk2, v2 in self.evclock.get((sk, v), {}).items():
                if kn.get(sk2, 0) < v2:
                    kn[sk2] = v2
        return waits

    def _commit(self, ev, reads, writes):
        for k in reads:
            self.readers.setdefault(k, []).append(ev)
        for k in writes:
            self.lastw[k] = ev
            self.readers[k] = []

    def op(self, eng, fn, reads=(), writes=()):
        fn = _freeze(fn)
        waits = self._deps(eng, reads, writes, SAMESYNC and eng != 'pe')
        self.ecount[eng] += 1
        n = self.ecount[eng]
        sk = ('e', eng, (n - 1) // SEM_LIMIT)
        ev = (sk, (n - 1) % SEM_LIMIT + 1)
        self.evclock[ev] = dict(self.known[eng])
        self.stream[eng].append((waits, fn, ev, 1))
        self._commit(ev, reads, writes)
        self.nops += 1
        return ev

    def dma(self, q, out, in_, reads=(), writes=(), **kw):
        waits = self._deps(q, reads, writes, True)
        i = self.dslot.get(q, 0)
        self.dslot[q] = (i + 1) % DMA_R
        c = self.dcount.get((q, i), 0)
        ep, cc = divmod(c, DMA_LIMIT)
        sk = ('d', q, i, ep)
        kn = self.known[q]
        if cc > 0:
            if kn.get(sk, 0) < cc * 16:
                waits.append((sk, cc * 16))
                kn[sk] = cc * 16
        elif ep > 0:
            skp = ('d', q, i, ep - 1)
            if kn.get(skp, 0) < DMA_LIMIT * 16:
                waits.append((skp, DMA_LIMIT * 16))
                kn[skp] = DMA_LIMIT * 16
        self.dcount[(q, i)] = c + 1
        ev = (sk, (cc + 1) * 16)
        self.evclock[ev] = dict(kn)
        fn = lambda e, out=out, in_=in_, kw=kw: e.dma_start(out=out, in_=in_, **kw)
        self.stream[q].append((waits, fn, ev, 16))
        self._commit(ev, reads, writes)
        self.nops += 1
        return ev

    def _all_events(self):
        evs = []
        for (q, i), c in self.dcount.items():
            ep, cc = divmod(c, DMA_LIMIT)
            if cc == 0:
                ep, cc = ep - 1, DMA_LIMIT
            evs.append((('d', q, i, ep), cc * 16))
        for e in self.ENG:
            n = self.ecount[e]
            if n:
                evs.append((('e', e, (n - 1) // SEM_LIMIT), (n - 1) % SEM_LIMIT + 1))
        return evs

    def barrier(self):
        evs = self._all_events()
        for eng in self.ENG:
            kn = self.known[eng]
            waits = []
            for sk, v in evs:
                if sk[0] == 'e' and sk[1] == eng:
                    continue
                if kn.get(sk, 0) < v:
                    waits.append((sk, v))
                    kn[sk] = v
            if waits:
                self.stream[eng].append((waits, None, None, 0))
        self.lastw = {}
        self.readers = {}

    def emit(self):
        nc = self.nc
        for e in self.ENG:
            for waits, fn, ev, inc in self.stream[e]:
                for sk, v in waits:
                    self.sem(sk)
                if ev is not None:
                    self.sem(ev[0])
        with nc.Block() as block:
            def replay(name, e):
                for waits, fn, ev, inc in self.stream[name]:
                    for sk, v in waits:
                        e.wait_ge(self.sems[sk], v)
                    if fn is not None:
                        fn(e).then_inc(self.sems[ev[0]], inc)

            @block.sync
            def _(e):
                replay('sp', e)

            @block.tensor
            def _(e):
                replay('pe', e)

            @block.scalar
            def _(e):
                replay('act', e)

            @block.vector
            def _(e):
                replay('dve', e)

            @block.gpsimd
            def _(e):
                replay('pool', e)


ARENA_WORDS = 51000
GLA_STAGE = 99
GLA_NT = 0
MLA_STAGE = 99
PEER_GROUPS = 0
PEER_STAGE = 99
PEER_SUB = 99
MLA_SKIP = set()
GLA_HH = 2


class LazyIn(dict):
    def __init__(self, nc):
        super().__init__()
        self.nc = nc
        self.shapes = {}

    def __missing__(self, name):
        sh = self.shapes[name]
        shape, dt = sh[0], sh[1]
        kind = sh[2] if len(sh) > 2 else "ExternalInput"
        ap = self.nc.dram_tensor(name, shape, dt, kind=kind).ap()
        self[name] = ap
        return ap


class Builder:
    def __init__(self, nc, st, nlayers=2, dbg=False, phases=None, scr_in=()):
        self.phases = phases
        self.scr_in = set(scr_in)
        self.nc = nc
        self.P = Prog(nc, st)
        self.arena = st.enter_context(nc.sbuf_tensor("arena", [128, ARENA_WORDS], F32))
        self.top = 0
        self.ps = [st.enter_context(nc.psum_tensor(f"ps{i}", [128, 512], F32)) for i in range(8)]
        self.nlayers = nlayers
        self.dbg = dbg
        self.uid = 0
        self.I = LazyIn(nc)
        self.S = LazyIn(nc)

    def alloc(self, n, dt=F32):
        w = n if dt == F32 else (n + 1) // 2
        assert self.top + w <= ARENA_WORDS, (self.top, w)
        a = self.arena[:, self.top:self.top + w]
        self.top += w
        return a if dt == F32 else a.bitcast(dt)

    def key(self, s):
        self.uid += 1
        return f"{s}#{self.uid}"

    def psb(self, i):
        return self.ps[i][:, :].bitcast(BF16)

    def inp(self, name, shape, dt=F32):
        self.I.shapes[name] = (list(shape), dt)

    def scratch(self, name, shape, dt):
        kind = "ExternalInput" if name in self.scr_in else ("ExternalOutput" if self.dbg else "Internal")
        self.S.shapes[name] = (list(shape), dt, kind)

    def new_phase(self):
        self.P.barrier()
        self.top = self.persist_top

    def declare(self):
        L = 2
        self.inp("x", [NLAT, D]); self.inp("ctx", [NCTX, D]); self.inp("cvecs", [2, D])
        self.inp("w_ada", [L, D, 6 * D]); self.inp("b_ada", [L, 6 * D]); self.inp("norm1", [L, D])
        self.inp("w_in_x", [L, D, INX]); self.inp("mla_q_norm", [L, 384]); self.inp("w_uq_x", [L, 384, 1536])
        self.inp("mla_kv_norm", [L, 256]); self.inp("mla_w_ukv", [L, 256, 1024])
        self.inp("na_bias", [L, 8, 128, 4480])
        self.inp("gla_w_gk_fwd", [L, 16, 256]); self.inp("gla_b_gk_fwd", [L, 256])
        self.inp("gla_w_gk_bwd", [L, 16, 256]); self.inp("gla_b_gk_bwd", [L, 256]); self.inp("gla_norm", [L, 128])
        self.inp("w_o_mla", [L, 512, D]); self.inp("w_o_na", [L, 512, D]); self.inp("w_o_gla", [L, 512, D])
        self.inp("w_out", [L, D, D]); self.inp("norm2", [L, D]); self.inp("peer_w_q", [L, D, D])
        self.inp("peer_sub_keys", [L, 2, 128, 64]); self.inp("peer_u", [L, NEXP, D]); self.inp("peer_v", [L, NEXP, D])
        self.inp("final_norm", [D])
        self.inp("k_cos", [NTOK, 32]); self.inp("k_sin", [NTOK, 32])
        self.inp("q_cosT", [32, NTOK]); self.inp("q_sinT", [32, NTOK])
        self.inp("na_mask", [128, 4480]); self.inp("gla_c", [4, 128, 128])
        self.out = self.nc.dram_tensor("out", [NLAT, D], F32, kind="ExternalOutput").ap()
        self.scratch("xs", [NTOK, D], F32); self.scratch("xmid", [NTOK, D], F32)
        self.scratch("mod", [L, 2, 6 * D], F32)
        self.scratch("proj_tm", [NTOK, INX], BF16); self.scratch("proj_fm", [FM_ROWS, NTOK], BF16)
        self.scratch("oT", [3, 512, NTOK], BF16)
        self.scratch("uT", [8, 128, NEXP], BF16); self.scratch("vb", [NEXP, D], BF16)

    def build(self):
        P = self.P
        self.declare()
        self.identf = self.alloc(128); self.identb = self.alloc(128, BF16)
        self.onesf = self.alloc(128); self.onesb = self.alloc(128, BF16)
        P.op('pool', lambda e: e.memset(self.identf, 1.0), writes=['identf'])
        P.op('pool', lambda e: e.affine_select(self.identf, self.identf, [[-1, 128]], ALU.is_equal, 0.0, base=0, channel_multiplier=1), reads=['identf'], writes=['identf'])
        P.op('dve', lambda e: e.tensor_copy(self.identb, self.identf), reads=['identf'], writes=['identb'])
        P.op('pool', lambda e: e.memset(self.onesf, 1.0), writes=['onesf'])
        P.op('pool', lambda e: e.memset(self.onesb, 1.0), writes=['onesb'])
        self.persist_top = self.top
        if "xs" not in self.scr_in:
            P.dma('sp', self.S["xs"][0:NCTX, :], self.I["ctx"][:, :], writes=['xs'])
            P.dma('sp', self.S["xs"][NCTX:NTOK, :], self.I["x"][:, :], writes=['xs'])
        P.barrier()
        allp = ["mod", "AP", "mla", "na", "gla", "merge", "peer_prep", "peer"]
        for l in range(self.nlayers):
            for ph in allp:
                if self.phases is None or ph in self.phases:
                    getattr(self, "phase_" + ph)(l)
        if self.phases is None or "final" in self.phases:
            self.phase_final()
        P.barrier()
        P.emit()

    def bvec(self, dst, src_row, key):
        self.P.dma('sp', dst, src_row.partition_broadcast(128), writes=[key])

    def qtiles(self, l):
        return list(range(0, NT)) if l == 0 else list(range(2, NT))

    def phase_mod(self, l):
        P = self.P
        self.new_phase()
        sc = self.alloc(16).rearrange("p (r k) -> p r k", r=2)
        brow = self.alloc(6144)
        modsb = self.alloc(6144)
        wb = [self.alloc(4096).rearrange("p (k n) -> p k n", k=8) for _ in range(2)]
        for r in range(2):
            P.dma('sp', sc[:, r, :], self.I["cvecs"][r, :].rearrange("(k p) -> p k", p=128), writes=['sc'], allow_slow_non_contiguous=True)
        P.op('act', lambda e: e.activation(sc, sc, AF.Silu), reads=['sc'], writes=['sc'])
        P.dma('sp', brow[0:1, :], self.I["b_ada"][l:l + 1, :], writes=['brow'])
        for g in range(12):
            w = wb[g % 2]
            wk = f"modw{g % 2}"
            P.dma('sp', w, self.I["w_ada"][l, :, g * 512:(g + 1) * 512].rearrange("(k p) n -> p k n", p=128), writes=[wk])
            pk = f"ps{g % 2}"
            pst = self.ps[g % 2]
            for k in range(8):
                P.op('pe', lambda e, k=k, w=w, pst=pst: e.matmul(pst[0:2, :], sc[:, :, k], w[:, k, :], start=(k == 0), stop=False), reads=['sc', wk], writes=[pk])
            P.op('pe', lambda e, g=g, pst=pst: e.matmul(pst[0:2, :], self.onesf[0:1, 0:2], brow[0:1, g * 512:(g + 1) * 512], start=False, stop=True), reads=['brow', 'onesf'], writes=[pk])
            P.op('act', lambda e, g=g, pst=pst: e.copy(modsb[0:2, g * 512:(g + 1) * 512], pst[0:2, :]), reads=[pk], writes=['modsb'])
        P.dma('sp', self.S["mod"][l, :, :], modsb[0:2, :], reads=['modsb'], writes=['mod'])

    def modrow(self, l, r, i):
        return self.S["mod"][l, r:r + 1, i * D:(i + 1) * D]

    def load_modvecs(self, l, norm_name, i_sh, i_sc):
        P = self.P
        res = []
        nrm = self.alloc(D)
        self.bvec(nrm, self.I[norm_name][l:l + 1, :], 'nrm')
        for r in range(2):
            weff = self.alloc(D); sh = self.alloc(D)
            kw, ks = self.key('weff'), self.key('sh')
            self.bvec(weff, self.modrow(l, r, i_sc), kw)
            self.bvec(sh, self.modrow(l, r, i_sh), ks)
            P.op('dve', lambda e, weff=weff: e.scalar_tensor_tensor(weff, weff, 1.0, nrm, ALU.add, ALU.mult), reads=[kw, 'nrm'], writes=[kw])
            res.append((weff, kw, sh, ks))
        return res

    def norm_mod(self, xt, kx, dst, kd, weff, kw, sh, ks, tmpf, kt, st, kst):
        P = self.P
        P.op('act', lambda e: e.activation(tmpf, xt, AF.Square, accum_out=st[:, 0:1]), reads=[kx], writes=[kt, kst])
        P.op('act', lambda e: e.activation(st[:, 1:2], st[:, 0:1], AF.Sqrt, bias=EPS, scale=1.0 / D), reads=[kst], writes=[kst])
        P.op('dve', lambda e: e.reciprocal(st[:, 2:3], st[:, 1:2]), reads=[kst], writes=[kst])
        P.op('dve', lambda e: e.scalar_tensor_tensor(tmpf, xt, st[:, 2:3], weff, ALU.mult, ALU.mult), reads=[kx, kst, kw], writes=[kt])
        P.op('dve', lambda e: e.tensor_tensor(dst, tmpf, sh, ALU.add), reads=[kt, ks], writes=[kd])

    def phase_AP(self, l):
        P = self.P
        self.new_phase()
        hT = self.alloc(8 * NTOK, BF16).rearrange("p (k t) -> p k t", k=8)
        m0 = self.top
        mv = self.load_modvecs(l, "norm1", 0, 1)
        xts = [self.alloc(D) for _ in range(2)]
        hbs = [self.alloc(D, BF16) for _ in range(2)]
        tmpf = self.alloc(D)
        stt = self.alloc(4 * NT)
        for tt in range(NT):
            b = tt % 2
            xt, hb = xts[b], hbs[b]
            kx, kh, kp = f"xt{b}", f"hb{b}", f"ps{b}"
            st = stt[:, 4 * tt:4 * tt + 4]
            kst = f"st{tt}"
            P.dma('sp', xt, self.S["xs"][tt * 128:(tt + 1) * 128, :], writes=[kx])
            weff, kw, sh, ks = mv[1 if tt < 2 else 0]
            self.norm_mod(xt, kx, hb, kh, weff, kw, sh, ks, tmpf, 'tmpf', st, kst)
            pv = self.psb(b).rearrange("p (k t) -> p k t", k=8)
            for k in range(8):
                P.op('pe', lambda e, k=k, hb=hb, pv=pv: e.transpose(pv[:, k, :], hb[:, k * 128:(k + 1) * 128], self.identb), reads=[kh, 'identb'], writes=[kp])
            P.op('act', lambda e, tt=tt, pv=pv: e.copy(hT[:, :, tt * 128:(tt + 1) * 128], pv), reads=[kp], writes=[f"hT{tt}"])
        hkeys = [f"hT{tt}" for tt in range(NT)]
        P.barrier()
        for tt in range(NT):
            P.lastw[f"hT{tt}"] = None
        P.lastw = {}
        self.top = m0
        groups = [(i * 512, 512) for i in range(7)] + [(3584, 192)] + [(C_GATE + i * 512, 512) for i in range(6)] + [(6848, 32)]
        wfs = [self.alloc(4096).rearrange("p (k n) -> p k n", k=8) for _ in range(2)]
        wbs = [self.alloc(4096, BF16).rearrange("p (k n) -> p k n", k=8) for _ in range(2)]
        stg = [self.alloc(512, BF16) for _ in range(3)]
        cnt = 0
        for gi, (c0, ncol) in enumerate(groups):
            b = gi % 2
            wf, wbb = wfs[b], wbs[b]
            P.dma('sp', wf[:, :, 0:ncol], self.I["w_in_x"][l, :, c0:c0 + ncol].rearrange("(k p) n -> p k n", p=128), writes=[f"wf{b}"])
            P.op('pool', lambda e, wf=wf, wbb=wbb, ncol=ncol: e.tensor_copy(wbb[:, :, 0:ncol], wf[:, :, 0:ncol]), reads=[f"wf{b}"], writes=[f"wb{b}"])
            func = AF.Sigmoid if (C_GATE <= c0 < 6848) else AF.Copy
            for tt in range(NT):
                pb = cnt % 3
                sb_ = cnt % 3
                cnt += 1
                pst = self.ps[pb]
                for k in range(8):
                    P.op('pe', lambda e, k=k, tt=tt, pst=pst, wbb=wbb, ncol=ncol: e.matmul(pst[:, 0:ncol], hT[:, k, tt * 128:(tt + 1) * 128], wbb[:, k, 0:ncol], start=(k == 0), stop=(k == 7)), reads=[f"wb{b}"], writes=[f"ps{pb}"])
                s_ = stg[sb_]
                P.op('act', lambda e, pst=pst, s_=s_, ncol=ncol, func=func: e.activation(s_[:, 0:ncol], pst[:, 0:ncol], func), reads=[f"ps{pb}"], writes=[f"stg{sb_}"])
                P.dma('pool', self.S["proj_tm"][tt * 128:(tt + 1) * 128, c0:c0 + ncol], s_[:, 0:ncol], reads=[f"stg{sb_}"], writes=['proj_tm'])
        fm = [(672 + 128 * i, 128, 0.125, FM_NAQ + 128 * i) for i in range(4)] + [(1184 + 128 * i, 128, 1.0, FM_NAK + 128 * i) for i in range(4)] + \
             [(2208 + 128 * i, 128, 0.125, FM_GQ + 128 * i) for i in range(2)] + [(2464 + 128 * i, 128, 1.0, FM_GK + 128 * i) for i in range(2)] + \
             [(3232 + 128 * i, 128, 1.0, FM_OG + 128 * i) for i in range(4)] + [(3744, 32, 1.0, FM_LOW)]
        tgs = [(0, 256)] + [(256 + 512 * i, 512) for i in range(8)]
        for gi, (c0, nr, scl, r0) in enumerate(fm):
            b = gi % 2
            wf, wbb = wfs[b], wbs[b]
            P.dma('sp', wf[:, :, 0:nr], self.I["w_in_x"][l, :, c0:c0 + nr].rearrange("(k p) n -> p k n", p=128), writes=[f"wf{b}"])
            P.op('pool', lambda e, wf=wf, wbb=wbb, nr=nr: e.tensor_copy(wbb[:, :, 0:nr], wf[:, :, 0:nr]), reads=[f"wf{b}"], writes=[f"wb{b}"])
            for (t0, n) in tgs:
                pb = cnt % 3
                cnt += 1
                pst = self.ps[pb]
                s_ = stg[pb]
                for k in range(8):
                    P.op('pe', lambda e, k=k, pst=pst, wbb=wbb, nr=nr, t0=t0, n=n: e.matmul(pst[0:nr, 0:n], wbb[:, k, 0:nr], hT[:, k, t0:t0 + n], start=(k == 0), stop=(k == 7)), reads=[f"wb{b}"], writes=[f"ps{pb}"])
                P.op('act', lambda e, pst=pst, s_=s_, nr=nr, n=n, scl=scl: e.activation(s_[0:nr, 0:n], pst[0:nr, 0:n], AF.Copy, scale=scl), reads=[f"ps{pb}"], writes=[f"stg{pb}"])
                P.dma('pool', self.S["proj_fm"][r0:r0 + nr, t0:t0 + n], s_[0:nr, 0:n], reads=[f"stg{pb}"], writes=['proj_fm'])

    def attn_fin(self, O, n, kO, dst, kdst, rec, krec):
        P = self.P
        P.op('dve', lambda e: e.reciprocal(rec[0:64, 0:n], O[64:128, 0:n]), reads=[kO], writes=[krec])
        P.op('dve', lambda e: e.tensor_tensor(dst, O[0:64, 0:n], rec[0:64, 0:n], ALU.mult), reads=[kO, krec], writes=[kdst])

    def phase_mla(self, l):
        P = self.P
        self.new_phase()
        need_ctx = (l == 0)
        cqnT = self.alloc(3 * NTOK, BF16).rearrange("p (k t) -> p k t", k=3)
        ckvnT = self.alloc(2 * NTOK, BF16).rearrange("p (k t) -> p k t", k=2)
        kTs = [self.alloc(NTOK, BF16) for _ in range(2)]
        qTs = [self.alloc(NTOK, BF16) for _ in range(2)]
        Vps = [self.alloc(NT * 128, BF16).rearrange("p (t c) -> p t c", c=128) for _ in range(2)]
        ost1 = self.alloc(NTOK, BF16)
        osts = [ost1, ost1]
        wuq = self.alloc(3 * 1536, BF16).rearrange("p (k n) -> p k n", k=3)
        wukv = self.alloc(2 * 1024, BF16).rearrange("p (k n) -> p k n", k=2)
        cosT = self.alloc(NTOK); sinT = self.alloc(NTOK)
        kcs = self.alloc(NT * 32).rearrange("p (t c) -> p t c", c=32)
        ksn = self.alloc(NT * 32).rearrange("p (t c) -> p t c", c=32)
        PTs = [self.alloc(512, BF16) for _ in range(3)]
        rec = self.alloc(512); tq1 = self.alloc(512); tq2 = self.alloc(512)
        qn = self.alloc(3); kvn = self.alloc(2)
        m0 = self.top
        wtmp = self.alloc(3 * 1536).rearrange("p (k n) -> p k n", k=3)
        P.dma('sp', wtmp, self.I["w_uq_x"][l].rearrange("(k p) n -> p k n", p=128), writes=['wtmp'])
        P.dma('sp', qn, self.I["mla_q_norm"][l, :].rearrange("(k p) -> p k", p=128), writes=['qn'], allow_slow_non_contiguous=True)
        for k in range(3):
            P.op('dve', lambda e, k=k: e.tensor_scalar(wuq[:, k, :], wtmp[:, k, :], qn[:, k:k + 1], None, ALU.mult), reads=['wtmp', 'qn'], writes=['wuq'])
        wtmp2 = wtmp.rearrange("p k n -> p (k n)")[:, 0:2048].rearrange("p (k n) -> p k n", k=2)
        P.dma('sp', wtmp2, self.I["mla_w_ukv"][l].rearrange("(k p) n -> p k n", p=128), writes=['wtmp'])
        P.dma('sp', kvn, self.I["mla_kv_norm"][l, :].rearrange("(k p) -> p k", p=128), writes=['kvn'], allow_slow_non_contiguous=True)
        for k in range(2):
            P.op('dve', lambda e, k=k: e.tensor_scalar(wukv[:, k, :], wtmp2[:, k, :], kvn[:, k:k + 1], None, ALU.mult), reads=['wtmp', 'kvn'], writes=['wukv'])
        P.dma('sp', cosT[64:96, :], self.I["q_cosT"][:, :], writes=['cosT'])
        P.dma('sp', sinT[64:96, :], self.I["q_sinT"][:, :], writes=['sinT'])
        P.dma('sp', kcs, self.I["k_cos"].rearrange("(t p) c -> p t c", p=128), writes=['kcs'])
        P.dma('sp', ksn, self.I["k_sin"].rearrange("(t p) c -> p t c", p=128), writes=['ksn'])
        for b in range(2):
            P.op('pool', lambda e, b=b: e.memset(Vps[b][:, :, 64:128], 1.0), writes=[f"Vp{b}"])
        if MLA_STAGE < 1:
            return
        pjs = [self.alloc(704, BF16) for _ in range(2)]
        junk = self.alloc(384)
        stt = self.alloc(8 * NT)
        cqn = [self.alloc(384, BF16) for _ in range(2)]
        ckvn = [self.alloc(256, BF16) for _ in range(2)]
        kro = [self.alloc(128, BF16) for _ in range(2)]
        for b_ in range(2):
            P.op('pool', lambda e, b_=b_: e.memset(kro[b_], 0.0), writes=[f"kro{b_}"])
        t1 = self.alloc(32); t2 = self.alloc(32)
        for tt in range(NT):
            b = tt % 2
            pj = pjs[b]
            kpj = f"pj{b}"
            st = stt[:, 8 * tt:8 * tt + 8]
            kst = f"st{tt}"
            P.dma('sp', pj[:, 0:672], self.S["proj_tm"][tt * 128:(tt + 1) * 128, 0:672], writes=[kpj])
            P.dma('sp', pj[:, 672:704], self.S["proj_tm"][tt * 128:(tt + 1) * 128, 6848:6880], writes=[kpj])
            if 'sq' not in MLA_SKIP: P.op('act', lambda e, pj=pj, st=st: e.activation(junk[:, 0:384], pj[:, 0:384], AF.Square, accum_out=st[:, 0:1]), reads=[kpj], writes=['junk', kst])
            if 'sq' not in MLA_SKIP: P.op('act', lambda e, pj=pj, st=st: e.activation(junk[:, 0:256], pj[:, 384:640], AF.Square, accum_out=st[:, 1:2]), reads=[kpj], writes=['junk', kst])
            if 'sq' not in MLA_SKIP: P.op('act', lambda e, st=st: e.activation(st[:, 2:3], st[:, 0:1], AF.Sqrt, bias=EPS, scale=1.0 / 384), reads=[kst], writes=[kst])
            if 'sq' not in MLA_SKIP: P.op('act', lambda e, st=st: e.activation(st[:, 3:4], st[:, 1:2], AF.Sqrt, bias=EPS, scale=1.0 / 256), reads=[kst], writes=[kst])
            if 'sq' not in MLA_SKIP: P.op('dve', lambda e, st=st: e.reciprocal(st[:, 4:6], st[:, 2:4]), reads=[kst], writes=[kst])
            if 'norm' not in MLA_SKIP: P.op('dve', lambda e, pj=pj, st=st, b=b: e.tensor_scalar(cqn[b], pj[:, 0:384], st[:, 4:5], None, ALU.mult), reads=[kpj, kst], writes=[f"cqn{b}"])
            if 'norm' not in MLA_SKIP: P.op('dve', lambda e, pj=pj, st=st, b=b: e.tensor_scalar(ckvn[b], pj[:, 384:640], st[:, 5:6], None, ALU.mult), reads=[kpj, kst], writes=[f"ckvn{b}"])
            if 'rope' not in MLA_SKIP: P.op('dve', lambda e, pj=pj, tt=tt: e.tensor_tensor(t1, pj[:, 640:672], kcs[:, tt, :], ALU.mult), reads=[kpj, 'kcs'], writes=['t1'])
            if 'rope' not in MLA_SKIP: P.op('dve', lambda e, pj=pj, tt=tt: e.tensor_tensor(t2, pj[:, 672:704], ksn[:, tt, :], ALU.mult), reads=[kpj, 'ksn'], writes=['t2'])
            if 'rope' not in MLA_SKIP: P.op('dve', lambda e, b=b: e.tensor_tensor(kro[b][:, 64:96], t1, t2, ALU.add), reads=['t1', 't2'], writes=[f"kro{b}"])
            pv = self.psb(b).rearrange("p (k t) -> p k t", k=8)
            kp = f"ps{b}"
            for k in range(3):
                if 'tr' not in MLA_SKIP: P.op('pe', lambda e, k=k, b=b, pv=pv: e.transpose(pv[:, k, :], cqn[b][:, k * 128:(k + 1) * 128], self.identb), reads=[f"cqn{b}"], writes=[kp])
            for k in range(2):
                if 'tr' not in MLA_SKIP: P.op('pe', lambda e, k=k, b=b, pv=pv: e.transpose(pv[:, 3 + k, :], ckvn[b][:, k * 128:(k + 1) * 128], self.identb), reads=[f"ckvn{b}"], writes=[kp])
            if 'tr' not in MLA_SKIP: P.op('pe', lambda e, b=b, pv=pv: e.transpose(pv[:, 5, :], kro[b], self.identb), reads=[f"kro{b}"], writes=[kp])
            sl = slice(tt * 128, (tt + 1) * 128)
            if 'tr' not in MLA_SKIP: P.op('act', lambda e, pv=pv, sl=sl: e.copy(cqnT[:, :, sl], pv[:, 0:3, :]), reads=[kp], writes=['cqnT'])
            if 'tr' not in MLA_SKIP: P.op('act', lambda e, pv=pv, sl=sl: e.copy(ckvnT[:, :, sl], pv[:, 3:5, :]), reads=[kp], writes=['ckvnT'])
            if 'kc' not in MLA_SKIP: P.op('act', lambda e, pv=pv, sl=sl: e.copy(kTs[0][64:96, sl], pv[64:96, 5, :]), reads=[kp], writes=['kT0r'])
            if 'kc' not in MLA_SKIP: P.op('act', lambda e, pv=pv, sl=sl: e.copy(kTs[1][64:96, sl], pv[64:96, 5, :]), reads=[kp], writes=['kT1r'])
        if MLA_STAGE < 2:
            return
        self.top = m0
        tgs = [(0, 256)] + [(256 + 512 * i, 512) for i in range(8)]
        qgs = ([(0, 256, [0, 1])] if need_ctx else []) + [(256 + 512 * i, 512, list(range(NT))) for i in range(8)]
        scnt = 0
        ocnt = 0
        for h in range(8):
            b = h % 2
            kT, qT, Vp, ost = kTs[b], qTs[b], Vps[b], osts[b]
            kkT, kqT, kVp, kost = f"kT{b}", f"qT{b}", f"Vp{b}", "ost"
            for (t0, n, _) in qgs:
                p1, p2 = self.ps[5], self.ps[6]
                for k in range(3):
                    P.op('pe', lambda e, k=k, t0=t0, n=n, h=h: e.matmul(p1[0:96, 0:n], wuq[:, k, h * 192:h * 192 + 96], cqnT[:, k, t0:t0 + n], start=(k == 0), stop=(k == 2)), reads=['wuq', 'cqnT'], writes=['ps5'])
                for k in range(3):
                    P.op('pe', lambda e, k=k, t0=t0, n=n, h=h: e.matmul(p2[0:96, 0:n], wuq[:, k, h * 192 + 96:h * 192 + 192], cqnT[:, k, t0:t0 + n], start=(k == 0), stop=(k == 2)), reads=['wuq', 'cqnT'], writes=['ps6'])
                P.op('act', lambda e, t0=t0, n=n, qT=qT: e.activation(qT[0:64, t0:t0 + n], p1[0:64, 0:n], AF.Copy, scale=SC_MLA), reads=['ps5'], writes=[kqT])
                P.op('dve', lambda e, t0=t0, n=n: e.tensor_tensor(tq1[64:96, 0:n], p1[64:96, 0:n], cosT[64:96, t0:t0 + n], ALU.mult), reads=['ps5', 'cosT'], writes=['tq1'])
                P.op('dve', lambda e, t0=t0, n=n: e.tensor_tensor(tq2[64:96, 0:n], p2[64:96, 0:n], sinT[64:96, t0:t0 + n], ALU.mult), reads=['ps6', 'sinT'], writes=['tq2'])
                P.op('dve', lambda e, t0=t0, n=n, qT=qT: e.tensor_tensor(qT[64:96, t0:t0 + n], tq1[64:96, 0:n], tq2[64:96, 0:n], ALU.add), reads=['tq1', 'tq2'], writes=[kqT])
            for (t0, n) in tgs:
                p3 = self.ps[7]
                for k in range(2):
                    P.op('pe', lambda e, k=k, t0=t0, n=n, h=h: e.matmul(p3[0:64, 0:n], wukv[:, k, h * 128:h * 128 + 64], ckvnT[:, k, t0:t0 + n], start=(k == 0), stop=(k == 1)), reads=['wukv', 'ckvnT'], writes=['ps7'])
                P.op('act', lambda e, t0=t0, n=n, kT=kT: e.copy(kT[0:64, t0:t0 + n], p3[0:64, 0:n]), reads=['ps7'], writes=[kkT])
            for t8 in range(0, NT, 8):
                p3 = self.ps[7]
                nt8 = min(8, NT - t8)
                pv3 = p3[:, :].rearrange("p (t c) -> p t c", c=64)
                for j in range(nt8):
                    tt = t8 + j
                    for k in range(2):
                        P.op('pe', lambda e, k=k, j=j, tt=tt, h=h: e.matmul(pv3[:, j, :], ckvnT[:, k, tt * 128:(tt + 1) * 128], wukv[:, k, h * 128 + 64:h * 128 + 128], start=(k == 0), stop=(k == 1)), reads=['wukv', 'ckvnT'], writes=['ps7'])
                P.op('dve', lambda e, t8=t8, nt8=nt8, Vp=Vp: e.tensor_copy(Vp[:, t8:t8 + nt8, 0:64], pv3[:, 0:nt8, :]), reads=['ps7'], writes=[kVp])
            for (t0, n, kts) in (qgs if MLA_STAGE >= 3 else []):
                ob = 3 + (ocnt % 2)
                ocnt += 1
                O = self.ps[ob]
                kO = f"ps{ob}"
                pend = []
                nk = len(kts)

                def qk(i):
                    nonlocal scnt
                    sb_ = scnt % 3
                    scnt += 1
                    kt = kts[i]
                    S_ = self.ps[sb_]
                    P.op('pe', lambda e, S_=S_, kt=kt: e.matmul(S_[:, 0:n], kT[0:96, kt * 128:(kt + 1) * 128], qT[0:96, t0:t0 + n], start=True, stop=True), reads=[kkT, kqT, 'kT0r', 'kT1r'], writes=[f"ps{sb_}"])
                    PT = PTs[sb_]
                    P.op('act', lambda e, S_=S_, PT=PT: e.activation(PT[:, 0:n], S_[:, 0:n], AF.Exp), reads=[f"ps{sb_}"], writes=[f"PT{sb_}"])
                    return sb_

                def pvm(i, sb_):
                    kt = kts[i]
                    PT = PTs[sb_]
                    P.op('pe', lambda e, PT=PT, kt=kt, i=i: e.matmul(O[:, 0:n], Vp[:, kt, :], PT[:, 0:n], start=(i == 0), stop=(i == nk - 1)), reads=[kVp, f"PT{sb_}"], writes=[kO])
                q = []
                for i in range(nk):
                    q.append((i, qk(i)))
                    if len(q) > 2:
                        pvm(*q.pop(0))
                while q:
                    pvm(*q.pop(0))
                self.attn_fin(O, n, kO, ost[0:64, t0:t0 + n], kost, rec, 'rec')
            P.dma('pool', self.S["oT"][0, h * 64:(h + 1) * 64, :], ost[0:64, :], reads=[kost], writes=['oT'])

    def phase_na(self, l):
        P = self.P
        self.new_phase()
        need_ctx = (l == 0)
        nmask = self.alloc(4480)
        P.dma('sp', nmask, self.I["na_mask"][:, :], writes=['nmask'])
        kTs = [self.alloc(NTOK, BF16) for _ in range(2)]
        qTs = [self.alloc(NTOK, BF16) for _ in range(2)]
        Vps = [self.alloc(NT * 128, BF16).rearrange("p (t c) -> p t c", c=128) for _ in range(2)]
        osts = [self.alloc(NTOK, BF16) for _ in range(2)]
        biasm = [self.alloc(4480).rearrange("p (a j q) -> p a j q", a=5, j=7) for _ in range(2)]
        T1s = [self.alloc(896).rearrange("p (j q) -> p j q", j=7) for _ in range(2)]
        PTs = [self.alloc(896, BF16).rearrange("p (j q) -> p j q", j=7) for _ in range(2)]
        PTc = self.alloc(512, BF16).rearrange("p (j q) -> p j q", j=2)
        rec = self.alloc(256)
        for b in range(2):
            P.op('pool', lambda e, b=b: e.memset(Vps[b][:, :, 64:128], 1.0), writes=[f"Vp{b}"])
        cnt = 0
        for h in range(8):
            b = h % 2
            kT, qT, Vp, ost, bm = kTs[b], qTs[b], Vps[b], osts[b], biasm[b]
            kkT, kqT, kVp, kost, kbm = f"kT{b}", f"qT{b}", f"Vp{b}", f"ost{b}", f"bm{b}"
            P.dma('sp', qT[0:64, :], self.S["proj_fm"][FM_NAQ + h * 64:FM_NAQ + (h + 1) * 64, :], writes=[kqT])
            P.dma('sp', kT[0:64, :], self.S["proj_fm"][FM_NAK + h * 64:FM_NAK + (h + 1) * 64, :], writes=[kkT])
            P.dma('sp', Vp[:, :, 0:64], self.S["proj_tm"][:, 1696 + h * 64:1696 + (h + 1) * 64].rearrange("(t p) c -> p t c", p=128), writes=[kVp])
            bmf = bm.rearrange("p a j q -> p (a j q)")
            P.dma('sp', bmf, self.I["na_bias"][l, h, :, :], writes=[kbm])
            P.op('pool', lambda e, bmf=bmf: e.tensor_tensor(bmf, bmf, nmask, ALU.add), reads=[kbm, 'nmask'], writes=[kbm])
            for m in range(32):
                pc = 0 if m == 0 else 1 if m == 1 else 3 if m == 30 else 4 if m == 31 else 2
                kb = min(max(m - 2, 0), 27)
                kts = [2 + kb + j for j in range(5)] + [0, 1]
                s = cnt % 2
                cnt += 1
                SA, SB = self.ps[2 * s], self.ps[2 * s + 1]
                kSA, kSB = f"ps{2 * s}", f"ps{2 * s + 1}"
                q0 = (2 + m) * 128
                for j, kt in enumerate(kts):
                    dst = SA[:, j * 128:(j + 1) * 128] if j < 4 else SB[:, (j - 4) * 128:(j - 3) * 128]
                    P.op('pe', lambda e, dst=dst, kt=kt, q0=q0: e.matmul(dst, kT[0:64, kt * 128:(kt + 1) * 128], qT[0:64, q0:q0 + 128], start=True, stop=True), reads=[kkT, kqT], writes=[kSA if j < 4 else kSB])
                T1, PT = T1s[s], PTs[s]
                P.op('dve', lambda e, T1=T1, SA=SA, pc=pc: e.tensor_tensor(T1[:, 0:4, :], SA[:, :].rearrange("p (j q) -> p j q", j=4), bm[:, pc, 0:4, :], ALU.add), reads=[kSA, kbm], writes=[f"T1{s}"])
                P.op('dve', lambda e, T1=T1, SB=SB, pc=pc: e.tensor_tensor(T1[:, 4:7, :], SB[:, 0:384].rearrange("p (j q) -> p j q", j=3), bm[:, pc, 4:7, :], ALU.add), reads=[kSB, kbm], writes=[f"T1{s}"])
                P.op('act', lambda e, T1=T1, PT=PT: e.activation(PT, T1, AF.Exp), reads=[f"T1{s}"], writes=[f"PT{s}"])
                ob = 4 + s
                O = self.ps[ob]
                for j, kt in enumerate(kts):
                    P.op('pe', lambda e, O=O, kt=kt, j=j, PT=PT: e.matmul(O[:, 0:128], Vp[:, kt, :], PT[:, j, :], start=(j == 0), stop=(j == 6)), reads=[kVp, f"PT{s}"], writes=[f"ps{ob}"])
                self.attn_fin(O, 128, f"ps{ob}", ost[0:64, q0:q0 + 128], kost, rec, 'rec')
            if need_ctx:
                S_ = self.ps[6]
                for jt in range(2):
                    P.op('pe', lambda e, jt=jt: e.matmul(S_[:, jt * 256:(jt + 1) * 256], kT[0:64, jt * 128:(jt + 1) * 128], qT[0:64, 0:256], start=True, stop=True), reads=[kkT, kqT], writes=['ps6'])
                P.op('act', lambda e: e.activation(PTc, S_[:, :].rearrange("p (j q) -> p j q", j=2), AF.Exp), reads=['ps6'], writes=['PTc'])
                O = self.ps[7]
                for jt in range(2):
                    P.op('pe', lambda e, jt=jt: e.matmul(O[:, 0:256], Vp[:, jt, :], PTc[:, jt, :], start=(jt == 0), stop=(jt == 1)), reads=[kVp, 'PTc'], writes=['ps7'])
                self.attn_fin(O, 256, 'ps7', ost[0:64, 0:256], kost, rec, 'rec')
            P.dma('pool', self.S["oT"][1, h * 64:(h + 1) * 64, :], ost[0:64, :], reads=[kost], writes=['oT'])

    def phase_gla(self, l):
        P = self.P
        self.new_phase()
        STG = GLA_STAGE

        def PO(stg, *a, **k):
            if STG >= stg:
                P.op(*a, **k)

        need_ctx = (l == 0)
        gc = self.alloc(512).rearrange("p (a t) -> p a t", a=4)
        P.dma('sp', gc, self.I["gla_c"].rearrange("a p t -> p a t"), writes=['gc'])
        onesM = self.alloc(128)
        P.op('pool', lambda e: e.memset(onesM, 1.0 / 128), writes=['onesM'])
        gnorm = self.alloc(1)
        P.dma('sp', gnorm, self.I["gla_norm"][l, :].rearrange("(p o) -> p o", o=1), writes=['gnorm'])
        wgk = [self.alloc(256) for _ in range(2)]
        brow = [self.alloc(256) for _ in range(2)]
        lowb = [self.alloc(NTOK, BF16) for _ in range(2)]
        low = [self.alloc(NTOK) for _ in range(2)]
        for d_, nm in enumerate(["fwd", "bwd"]):
            P.dma('sp', wgk[d_][0:16, :], self.I[f"gla_w_gk_{nm}"][l, :, :], writes=[f"wgk{d_}"])
            P.dma('sp', brow[d_][0:1, :], self.I[f"gla_b_gk_{nm}"][l:l + 1, :], writes=[f"brow{d_}"])
            P.dma('sp', lowb[d_][0:16, :], self.S["proj_fm"][FM_LOW + 16 * d_:FM_LOW + 16 * d_ + 16, :], writes=[f"lowb{d_}"])
            P.op('dve', lambda e, d_=d_: e.tensor_copy(low[d_][0:16, :], lowb[d_][0:16, :]), reads=[f"lowb{d_}"], writes=[f"low{d_}"])
        qT = self.alloc(NTOK, BF16); kT = self.alloc(NTOK, BF16)
        vt = self.alloc(NT * 256, BF16).rearrange("p (t c) -> p t c", c=256)
        obuf = self.alloc(2 * NTOK, BF16).rearrange("p (h t) -> p h t", h=2)
        ogT = self.alloc(2 * NTOK, BF16).rearrange("p (h t) -> p h t", h=2)
        gst = self.alloc(2 * NTOK, BF16).rearrange("p (h t) -> p h t", h=2)
        Sf = [self.alloc(256) for _ in range(2)]
        Sb = [self.alloc(256, BF16) for _ in range(2)]
        e1s = [self.alloc(128) for _ in range(2)]; sps = [self.alloc(128) for _ in range(2)]
        Eqs = [self.alloc(128) for _ in range(2)]; Eks = [self.alloc(128) for _ in range(2)]
        kins = [self.alloc(128, BF16) for _ in range(2)]
        qinzs = [[self.alloc(128, BF16) for _ in range(2)] for _ in range(2)]
        kintzs = [[self.alloc(128, BF16) for _ in range(2)] for _ in range(2)]
        for q_ in range(2):
            for i_ in range(2):
                P.op('pool', lambda e, i_=i_, q_=q_: e.memset(qinzs[q_][i_], 0.0), writes=[f'qin{q_}'])
                P.op('pool', lambda e, i_=i_, q_=q_: e.memset(kintzs[q_][i_], 0.0), writes=[f'kint{q_}'])
        ams = [self.alloc(256, BF16).rearrange("p (h t) -> p h t", h=2) for _ in range(2)]
        Kps = [self.alloc(512).rearrange("p (c n) -> p c n", c=2) for _ in range(2)]
        ot = self.alloc(128); sq = self.alloc(128); sd = self.alloc(128); on = self.alloc(128); sg = self.alloc(128)
        pzs = [self.ps[0][:, 0:128], self.ps[0][:, 128:256]]
        pbs = [self.ps[0][:, 256:384], self.ps[0][:, 384:512]]
        pkts = [self.psb(1)[:, 0:128], self.psb(1)[:, 128:256]]
        pa3s = [self.ps[2][:, 0:256].rearrange("p (h t) -> p h t", h=2), self.ps[2][:, 256:512].rearrange("p (h t) -> p h t", h=2)]
        pK3s = [self.ps[3][:, :].rearrange("p (c n) -> p c n", c=2), self.ps[4][:, :].rearrange("p (c n) -> p c n", c=2)]
        pO3 = [self.ps[5][:, 0:128], self.ps[6][:, 0:128]]
        kpO = ['pO0', 'pO1']
        pss = self.ps[7]
        tcnt = 0
        for pr in range(2):
            P.dma('sp', qT, self.S["proj_fm"][FM_GQ + pr * 128:FM_GQ + (pr + 1) * 128, :], writes=['qT'])
            P.dma('sp', kT, self.S["proj_fm"][FM_GK + pr * 128:FM_GK + (pr + 1) * 128, :], writes=['kT'])
            P.dma('sp', vt, self.S["proj_tm"][:, 2720 + pr * 256:2720 + (pr + 1) * 256].rearrange("(t p) c -> p t c", p=128), writes=['vt'])
            for hh in range(2):
                P.dma('sp', ogT[:, hh, :], self.S["proj_fm"][FM_OG + (2 * pr + hh) * 128:FM_OG + (2 * pr + hh + 1) * 128, :], writes=['ogT'])
            for d_ in (1, 0):
                tiles = [1, 0] + list(range(NT - 1, 1, -1)) if d_ == 1 else list(range(NT))
                corder = (1, 0) if d_ == 1 else (0, 1)
                cur = 0
                P.op('pool', lambda e: e.memset(Sf[0], 0.0), writes=['Sf0'])
                P.op('pool', lambda e: e.memset(Sb[0], 0.0), writes=['Sb0'])
                if GLA_NT:
                    tiles = tiles[:GLA_NT]
                state = {'cur': 0}

                def emitA(tt, q_, d_=d_, pr=pr):
                    sl = slice(tt * 128, (tt + 1) * 128)
                    isq = need_ctx or tt >= 2
                    e1, sp_, Eq, Ek, kin, qinz, kintz, am, Kp = e1s[q_], sps[q_], Eqs[q_], Eks[q_], kins[q_], qinzs[q_], kintzs[q_], ams[q_], Kps[q_]
                    pz, pb_, pkt, pa3, pK3 = pzs[q_], pbs[q_], pkts[q_], pa3s[q_], pK3s[q_]
                    Q = str(q_)
                    PO(1, 'pe', lambda e, sl=sl, d_=d_, pr=pr: e.matmul(pz, low[d_][0:16, sl], wgk[d_][0:16, pr * 128:(pr + 1) * 128], start=True, stop=False), reads=[f"low{d_}", f"wgk{d_}"], writes=[('ps0_' + Q)])
                    PO(1, 'pe', lambda e, d_=d_, pr=pr: e.matmul(pz, self.onesf[0:1, :], brow[d_][0:1, pr * 128:(pr + 1) * 128], start=False, stop=True), reads=[f"brow{d_}", 'onesf'], writes=[('ps0_' + Q)])
                    PO(1, 'act', lambda e: e.activation(e1, pz, AF.Exp, scale=-1.0), reads=[('ps0_' + Q)], writes=[('e1_' + Q)])
                    PO(1, 'act', lambda e: e.activation(sp_, e1, AF.Ln, bias=1.0), reads=[('e1_' + Q)], writes=[('sp_' + Q)])
                    PO(2, 'pe', lambda e, d_=d_: e.matmul(pb_, sp_, gc[:, d_, :], start=True, stop=True), reads=[('sp_' + Q), 'gc'], writes=[('ps1_' + Q)])
                    PO(2, 'act', lambda e: e.activation(Eq, pb_, AF.Exp), reads=[('ps1_' + Q)], writes=[('Eq_' + Q)])
                    PO(2, 'act', lambda e: e.activation(Ek, pb_, AF.Exp, scale=-1.0), reads=[('ps1_' + Q)], writes=[('Ek_' + Q)])
                    for hh in range(2):
                        PO(3, 'dve', lambda e, sl=sl, hh=hh: e.tensor_tensor(qinz[hh][hh * 64:(hh + 1) * 64, :], qT[hh * 64:(hh + 1) * 64, sl], Eq[hh * 64:(hh + 1) * 64, :], ALU.mult), reads=['qT', ('Eq_' + Q)], writes=[('qin_' + Q)])
                    PO(3, 'dve', lambda e, sl=sl: e.tensor_tensor(kin, kT[:, sl], Ek, ALU.mult), reads=['kT', ('Ek_' + Q)], writes=[('kin_' + Q)])
                    PO(3, 'pe', lambda e: e.transpose(pkt, kin, self.identb), reads=[('kin_' + Q), 'identb'], writes=[('ps2_' + Q)])
                    for c in range(2):
                        PO(3, 'act', lambda e, c=c: e.copy(kintz[c][c * 64:(c + 1) * 64, :], pkt[c * 64:(c + 1) * 64, :]), reads=[('ps2_' + Q)], writes=[('kint_' + Q)])
                    if isq:
                        for hh in range(2):
                            PO(4, 'pe', lambda e, hh=hh: e.matmul(pa3[:, hh, :], kin, qinz[hh], start=True, stop=True), reads=[('kin_' + Q), ('qin_' + Q)], writes=[('ps3_' + Q)])
                        PO(4, 'dve', lambda e, d_=d_: e.tensor_tensor(am, pa3, gc[:, 2 + d_, :].unsqueeze(1).broadcast_to([128, 2, 128]), ALU.mult), reads=[('ps3_' + Q), 'gc'], writes=[('am_' + Q)])
                    for c in corder:
                        PO(5, 'pe', lambda e, c=c, tt=tt: e.matmul(pK3[:, c, :], kintz[c], vt[:, tt, :], start=True, stop=True), reads=[('kint_' + Q), 'vt'], writes=[('ps4_' + Q)])
                        di = (c * 64 + 63) if d_ == 0 else (c * 64)
                        PO(5, 'dve', lambda e, c=c, di=di: e.tensor_scalar(Kp[:, c, :], pK3[:, c, :], Eq[:, di:di + 1], None, ALU.mult), reads=[('ps4_' + Q), ('Eq_' + Q)], writes=[f"Kp{c}_" + Q])

                def emitB(tt, q_, d_=d_, pr=pr, corder=corder):
                    sl = slice(tt * 128, (tt + 1) * 128)
                    isq = need_ctx or tt >= 2
                    e1, sp_, Eq, Ek, kin, qinz, kintz, am, Kp = e1s[q_], sps[q_], Eqs[q_], Eks[q_], kins[q_], qinzs[q_], kintzs[q_], ams[q_], Kps[q_]
                    pz, pb_, pkt, pa3, pK3 = pzs[q_], pbs[q_], pkts[q_], pa3s[q_], pK3s[q_]
                    Q = str(q_)
                    cur = state['cur']
                    if isq:
                        for hh in range(2):
                            PO(6, 'pe', lambda e, hh=hh, tt=tt: e.matmul(pO3[hh], vt[:, tt, hh * 128:(hh + 1) * 128], am[:, hh, :], start=True, stop=False), reads=['vt', ('am_' + Q)], writes=[kpO[hh]])
                    for ci, c in enumerate(corder):
                        if isq:
                            for hh in range(2):
                                PO(6, 'pe', lambda e, hh=hh, c=c, cur=cur, ci=ci: e.matmul(pO3[hh][:, c * 64:(c + 1) * 64], Sb[cur][:, hh * 128:(hh + 1) * 128], qinz[hh][:, c * 64:(c + 1) * 64], start=False, stop=(ci == 1)), reads=[f"Sb{cur}", ('qin_' + Q)], writes=[kpO[hh]])
                        di = (c * 64 + 63) if d_ == 0 else (c * 64)
                        nxt = 1 - cur
                        PO(7, 'dve', lambda e, c=c, di=di, cur=cur, nxt=nxt: e.scalar_tensor_tensor(Sf[nxt], Sf[cur], Eq[:, di:di + 1], Kp[:, c, :], ALU.mult, ALU.add), reads=[f"Sf{cur}", ('Eq_' + Q), f"Kp{c}_" + Q], writes=[f"Sf{nxt}"])
                        PO(7, 'act', lambda e, nxt=nxt: e.copy(Sb[nxt], Sf[nxt]), reads=[f"Sf{nxt}"], writes=[f"Sb{nxt}"])
                        cur = nxt
                        state['cur'] = cur
                    if not isq:
                        return
                    if d_ == 1:
                        for hh in range(2):
                            PO(8, 'act', lambda e, sl=sl, hh=hh: e.copy(obuf[:, hh, sl], pO3[hh]), reads=[kpO[hh]], writes=['obuf'])
                    else:
                        for hh in range(2):
                            PO(8, 'dve', lambda e, hh=hh, sl=sl: e.tensor_tensor(ot, pO3[hh], obuf[:, hh, sl], ALU.add), reads=[kpO[hh], 'obuf'], writes=['ot'])
                            PO(8, 'act', lambda e: e.activation(sq, ot, AF.Square), reads=['ot'], writes=['sq'])
                            PO(8, 'pe', lambda e: e.matmul(pss[:, 0:128], onesM, sq, start=True, stop=True), reads=['onesM', 'sq'], writes=['ps7'])
                            PO(8, 'act', lambda e: e.activation(sd, pss[:, 0:128], AF.Sqrt, bias=EPS), reads=['ps7'], writes=['sd'])
                            PO(8, 'dve', lambda e: e.reciprocal(sd, sd), reads=['sd'], writes=['sd'])
                            PO(8, 'dve', lambda e: e.scalar_tensor_tensor(on, ot, gnorm[:, 0:1], sd, ALU.mult, ALU.mult), reads=['ot', 'gnorm', 'sd'], writes=['on'])
                            PO(8, 'act', lambda e, hh=hh, sl=sl: e.activation(sg, ogT[:, hh, sl], AF.Silu), reads=['ogT'], writes=['sg'])
                            PO(8, 'dve', lambda e, hh=hh, sl=sl: e.tensor_tensor(gst[:, hh, sl], on, sg, ALU.mult), reads=['on', 'sg'], writes=['gst'])

                prev = None
                for ti_, tt in enumerate(tiles):
                    emitA(tt, ti_ % 2)
                    if prev is not None:
                        emitB(*prev)
                    prev = (tt, ti_ % 2)
                emitB(*prev)
            for hh in range(2):
                P.dma('pool', self.S["oT"][2, (2 * pr + hh) * 128:(2 * pr + hh + 1) * 128, :], gst[:, hh, :], reads=['gst'], writes=['oT'])

    def load_w_bf16(self, dst, src3, nk, ncol, tmp, name):
        P = self.P
        for k in range(nk):
            P.dma('sp', tmp, src3[k * 128:(k + 1) * 128, :], writes=['wtmpm'])
            P.op('pool', lambda e, k=k: e.tensor_copy(dst[:, k, :], tmp), reads=['wtmpm'], writes=[name])

    def phase_merge(self, l):
        P = self.P
        self.new_phase()
        wo = [self.alloc(4 * D, BF16).rearrange("p (k n) -> p k n", k=4) for _ in range(3)]
        wout = self.alloc(8 * D, BF16).rearrange("p (k n) -> p k n", k=8)
        tmp = self.alloc(D)
        for br, nm in enumerate(["w_o_mla", "w_o_na", "w_o_gla"]):
            self.load_w_bf16(wo[br], self.I[nm][l], 4, D, tmp, f"wo{br}")
        self.load_w_bf16(wout, self.I["w_out"][l], 8, D, tmp, 'wout')
        g1 = [self.alloc(D) for _ in range(2)]
        for r in range(2):
            self.bvec(g1[r], self.modrow(l, r, 2), f"g1{r}")
        oTt = [[self.alloc(512, BF16).rearrange("p (k t) -> p k t", k=4) for _ in range(3)] for _ in range(2)]
        gts = [self.alloc(3 * D, BF16) for _ in range(2)]
        xts = [self.alloc(D) for _ in range(2)]
        y = self.alloc(D); tb = self.alloc(D); yb = self.alloc(D, BF16)
        yT = self.alloc(D, BF16).rearrange("p (k t) -> p k t", k=8)
        x1 = [self.alloc(D) for _ in range(2)]
        for idx, tt in enumerate(self.qtiles(l)):
            b = idx % 2
            sl = slice(tt * 128, (tt + 1) * 128)
            for br in range(3):
                P.dma('sp', oTt[b][br], self.S["oT"][br, :, sl].rearrange("(k p) t -> p k t", p=128), writes=[f"oTt{b}{br}"])
            P.dma('sp', gts[b], self.S["proj_tm"][sl, C_GATE:C_GATE + 3 * D], writes=[f"gts{b}"])
            P.dma('sp', xts[b], self.S["xs"][sl, :], writes=[f"xt{b}"])
            for br in range(3):
                pbk = (0, 2)[br % 2]
                for half in range(2):
                    pst = self.ps[pbk + half]
                    for k in range(4):
                        P.op('pe', lambda e, k=k, half=half, pst=pst, br=br, b=b: e.matmul(pst[:, :], oTt[b][br][:, k, :], wo[br][:, k, half * 512:(half + 1) * 512], start=(k == 0), stop=(k == 3)), reads=[f"oTt{b}{br}", f"wo{br}"], writes=[f"ps{pbk + half}"])
                    hs = slice(half * 512, (half + 1) * 512)
                    gs = slice(br * D + half * 512, br * D + (half + 1) * 512)
                    if br == 0:
                        P.op('dve', lambda e, pst=pst, hs=hs, gs=gs, b=b: e.tensor_tensor(y[:, hs], pst[:, :], gts[b][:, gs], ALU.mult), reads=[f"ps{pbk + half}", f"gts{b}"], writes=['y'])
                    else:
                        P.op('dve', lambda e, pst=pst, hs=hs, gs=gs, b=b: e.tensor_tensor(tb[:, hs], pst[:, :], gts[b][:, gs], ALU.mult), reads=[f"ps{pbk + half}", f"gts{b}"], writes=['tb'])
                        if br == 1:
                            P.op('pool', lambda e, hs=hs: e.tensor_tensor(y[:, hs], y[:, hs], tb[:, hs], ALU.add), reads=['tb', 'y'], writes=['y'])
                        else:
                            P.op('pool', lambda e, hs=hs: e.tensor_tensor(yb[:, hs], y[:, hs], tb[:, hs], ALU.add), reads=['tb', 'y'], writes=['yb'])
            pv = self.psb(4).rearrange("p (k t) -> p k t", k=8)
            for k in range(8):
                P.op('pe', lambda e, k=k: e.transpose(pv[:, k, :], yb[:, k * 128:(k + 1) * 128], self.identb), reads=['yb', 'identb'], writes=['ps4'])
            P.op('act', lambda e: e.copy(yT, pv), reads=['ps4'], writes=['yT'])
            for half in range(2):
                pst = self.ps[5 + half]
                for k in range(8):
                    P.op('pe', lambda e, k=k, half=half, pst=pst: e.matmul(pst[:, :], yT[:, k, :], wout[:, k, half * 512:(half + 1) * 512], start=(k == 0), stop=(k == 7)), reads=['yT', 'wout'], writes=[f"ps{5 + half}"])
                hs = slice(half * 512, (half + 1) * 512)
                r = 1 if tt < 2 else 0
                P.op('dve', lambda e, pst=pst, hs=hs, r=r: e.tensor_tensor(tb[:, hs], pst[:, :], g1[r][:, hs], ALU.mult), reads=[f"ps{5 + half}", f"g1{r}"], writes=['tb'])
                P.op('pool', lambda e, hs=hs, b=b: e.tensor_tensor(x1[b][:, hs], tb[:, hs], xts[b][:, hs], ALU.add), reads=['tb', f"xt{b}"], writes=[f"x1{b}"])
            P.dma('pool', self.S["xmid"][sl, :], x1[b], reads=[f"x1{b}"], writes=['xmid'])

    def phase_peer_prep(self, l):
        P = self.P
        self.new_phase()
        uf = [self.alloc(D) for _ in range(2)]
        ub = [self.alloc(D, BF16) for _ in range(2)]
        us = [self.alloc(D, BF16).rearrange("p (k e) -> p k e", k=8) for _ in range(2)]
        vf = [self.alloc(D) for _ in range(2)]
        vb = [self.alloc(D, BF16) for _ in range(2)]
        for et in range(128):
            b = et % 2
            es = slice(et * 128, (et + 1) * 128)
            P.dma('sp', uf[b], self.I["peer_u"][l, es, :], writes=[f"uf{b}"])
            P.op('dve', lambda e, b=b: e.tensor_copy(ub[b], uf[b]), reads=[f"uf{b}"], writes=[f"ub{b}"])
            pv = self.psb(b).rearrange("p (k e) -> p k e", k=8)
            for k in range(8):
                P.op('pe', lambda e, k=k, b=b, pv=pv: e.transpose(pv[:, k, :], ub[b][:, k * 128:(k + 1) * 128], self.identb), reads=[f"ub{b}", 'identb'], writes=[f"ps{b}"])
            P.op('act', lambda e, b=b, pv=pv: e.copy(us[b], pv), reads=[f"ps{b}"], writes=[f"us{b}"])
            P.dma('pool', self.S["uT"][:, :, es].rearrange("k p e -> p k e"), us[b], reads=[f"us{b}"], writes=['uT'])
            P.dma('sp', vf[b], self.I["peer_v"][l, es, :], writes=[f"vf{b}"])
            P.op('pool', lambda e, b=b: e.tensor_copy(vb[b], vf[b]), reads=[f"vf{b}"], writes=[f"vb{b}"])
            P.dma('pool', self.S["vb"][es, :], vb[b], reads=[f"vb{b}"], writes=['vbd'])

    def phase_peer(self, l):
        P = self.P
        self.new_phase()
        need_ctx = (l == 0)
        mv = self.load_modvecs(l, "norm2", 3, 4)
        g2 = [self.alloc(D) for _ in range(2)]
        for r in range(2):
            self.bvec(g2[r], self.modrow(l, r, 5), f"g2{r}")
        wqmem = self.alloc(8 * D)
        wq = wqmem.rearrange("p (k n) -> p k n", k=8)
        dE = [wqmem[:, i_ * 2048:(i_ + 1) * 2048].rearrange("p (h i j) -> p h i j", h=4, i=4) for i_ in range(4)]
        dEk = ['dE0', 'dE1', 'dE2', 'dE3']
        Dg = [[self.alloc(128, BF16) for _ in range(8)] for _ in range(2)]
        skn = self.alloc(128)
        skT = self.alloc(128)
        for p_ in range(2):
            P.dma('sp', skn[:, p_ * 64:(p_ + 1) * 64], self.I["peer_sub_keys"][l, p_, :, :], writes=['skn'])
        P.op('pe', lambda e: e.transpose(self.ps[0][:, 0:128], skn, self.identf), reads=['skn', 'identf'], writes=['ps0'])
        skTz = [self.alloc(128) for _ in range(2)]
        for a_ in range(2):
            P.op('pool', lambda e, a_=a_: e.memset(skTz[a_], 0.0), writes=['skT'])
        for a_ in range(2):
            P.op('act', lambda e, a_=a_: e.copy(skTz[a_][a_ * 64:(a_ + 1) * 64, :], self.ps[0][a_ * 64:(a_ + 1) * 64, 0:128]), reads=['ps0'], writes=['skT'])
        x1t = [self.alloc(D) for _ in range(2)]
        h2f = self.alloc(D); h2b = self.alloc(D, BF16); tmpf = self.alloc(D)
        h2Tb = self.alloc(8 * 256, BF16).rearrange("p (k t) -> p k t", k=8)
        h2Tf = self.alloc(8 * 256).rearrange("p (k t) -> p k t", k=8)
        qTf = self.alloc(8 * 256).rearrange("p (h t) -> p h t", h=8)
        s_tok = [self.alloc(2048).rearrange("p (h a n) -> p h a n", h=8, a=2) for _ in range(2)]
        s1n = [self.alloc(1024).rearrange("p (h n) -> p h n", h=8) for _ in range(2)]
        v16 = self.alloc(256).rearrange("p (h a n) -> p h a n", h=8, a=2)
        tmp128 = self.alloc(128)
        cand = self.alloc(2048).rearrange("p (h n) -> p h n", h=8)
        c24 = self.alloc(192).rearrange("p (h n) -> p h n", h=8)
        tmpc = self.alloc(256); tmpc2 = self.alloc(256); ec = self.alloc(256); junkc = self.alloc(256)
        sm = [self.alloc(64) for _ in range(2)]
        stt = self.alloc(4)
        mB = [self.alloc(2048, BF16).rearrange("p (h i j) -> p h i j", h=4, i=4) for _ in range(2)]
        uTs = [self.alloc(8 * 512, BF16).rearrange("p (k e) -> p k e", k=8) for _ in range(2)]
        vss = [self.alloc(4 * D, BF16).rearrange("p (c d) -> p c d", c=4) for _ in range(2)]
        ga_ = self.alloc(1024).rearrange("p (i t) -> p i t", i=4)
        Wt_ = self.alloc(1024, BF16).rearrange("p (i t) -> p i t", i=4)
        ga = [ga_, ga_]
        Wt = [Wt_, Wt_]
        tb = self.alloc(D); x2 = [tmpf, tmpf]
        groups = ([0] if need_ctx else []) + list(range(2, NT, 2))
        if PEER_GROUPS:
            groups = groups[:PEER_GROUPS]
        ecnt = 0
        wcnt = 0
        for g0 in groups:
            r = 1 if g0 < 2 else 0
            weff, kw, sh, ks = mv[r]
            for ti in range(2):
                tt = g0 + ti
                sl = slice(tt * 128, (tt + 1) * 128)
                ts = slice(ti * 128, (ti + 1) * 128)
                P.dma('sp', x1t[ti], self.S["xmid"][sl, :], writes=[f"x1t{ti}"])
                self.norm_mod(x1t[ti], f"x1t{ti}", h2f, 'h2f', weff, kw, sh, ks, tmpf, 'tmpf', stt, 'stt')
                P.op('act', lambda e: e.copy(h2b, h2f), reads=['h2f'], writes=['h2b'])
                pv = self.psb(0).rearrange("p (k t) -> p k t", k=8)
                for k in range(8):
                    P.op('pe', lambda e, k=k, pv=pv: e.transpose(pv[:, k, :], h2b[:, k * 128:(k + 1) * 128], self.identb), reads=['h2b', 'identb'], writes=['ps0'])
                P.op('act', lambda e, pv=pv, ts=ts: e.copy(h2Tb[:, :, ts], pv), reads=['ps0'], writes=['h2Tb'])
                for half in range(2):
                    pf = self.ps[1 + half][:, :].rearrange("p (k t) -> p k t", k=4)
                    for k4 in range(4):
                        k = half * 4 + k4
                        P.op('pe', lambda e, k=k, k4=k4, pf=pf: e.transpose(pf[:, k4, :], h2f[:, k * 128:(k + 1) * 128], self.identf), reads=['h2f', 'identf'], writes=[f"ps{1 + half}"])
                    P.op('dve', lambda e, pf=pf, half=half, ts=ts: e.tensor_copy(h2Tf[:, half * 4:(half + 1) * 4, ts], pf), reads=[f"ps{1 + half}"], writes=['h2Tf'])
            if PEER_STAGE < 2:
                continue
            P.dma('sp', wq, self.I["peer_w_q"][l].rearrange("(k p) n -> p k n", p=128), writes=['wq'] + dEk)
            for cc in range(8):
                pst = self.ps[cc % 2]
                for k in range(8):
                    P.op('pe', lambda e, k=k, cc=cc, pst=pst: e.matmul(pst[:, 0:256], wq[:, k, cc * 128:(cc + 1) * 128], h2Tf[:, k, :], start=(k == 0), stop=(k == 7)), reads=['wq', 'h2Tf'] + dEk, writes=[f"ps{cc % 2}"])
                P.op('act', lambda e, cc=cc, pst=pst: e.copy(qTf[:, cc, :], pst[:, 0:256]), reads=[f"ps{cc % 2}"], writes=['qTf'])
            for ti in (range(2) if PEER_STAGE >= 3 else []):
                ts = slice(ti * 128, (ti + 1) * 128)
                st_ = s_tok[ti]
                for h4 in range(4):
                    pst = self.ps[2 + (h4 % 2)]
                    ps4 = pst[:, :].rearrange("p (h a n) -> p h a n", h=2, a=2)
                    for hl in range(2):
                        h = h4 * 2 + hl
                        for a in range(2):
                            P.op('pe', lambda e, h=h, hl=hl, a=a, ps4=ps4, ts=ts: e.matmul(ps4[:, hl, a, :], qTf[:, h, ts], skTz[a], start=True, stop=True), reads=['qTf', 'skT'], writes=[f"ps{2 + (h4 % 2)}"])
                    P.op('act', lambda e, h4=h4, ps4=ps4, st_=st_: e.copy(st_[:, h4 * 2:h4 * 2 + 2, :, :], ps4), reads=[f"ps{2 + (h4 % 2)}"], writes=[f"stok{ti}"])
                sm_ = sm[ti]
                ksm = f"sm{ti}"
                if PEER_SUB < 2:
                    continue
                for h in range(8):
                    for a in range(2):
                        P.op('dve', lambda e, h=h, a=a, st_=st_: e.max(v16[:, h, a, 0:8], st_[:, h, a, :]), reads=[f"stok{ti}"], writes=['v16'])
                        P.op('dve', lambda e, h=h, a=a, st_=st_: e.match_replace(tmp128, v16[:, h, a, 0:8], st_[:, h, a, :], -1e30), reads=[f"stok{ti}", 'v16'], writes=['tmp128'])
                        P.op('dve', lambda e, h=h, a=a: e.max(v16[:, h, a, 8:16], tmp128), reads=['tmp128'], writes=['v16'])
                    if PEER_SUB < 3:
                        continue
                    ch = cand[:, h, :]
                    P.op('dve', lambda e, h=h, ch=ch: e.tensor_tensor(ch.rearrange("p (a b) -> p a b", a=16), v16[:, h, 0, :].unsqueeze(2).broadcast_to([128, 16, 16]), v16[:, h, 1, :].unsqueeze(1).broadcast_to([128, 16, 16]), ALU.add), reads=['v16'], writes=['cand'])
                    P.op('dve', lambda e, h=h, ch=ch: e.max(c24[:, h, 0:8], ch), reads=['cand'], writes=['c24'])
                    P.op('dve', lambda e, h=h, ch=ch: e.match_replace(tmpc, c24[:, h, 0:8], ch, -1e30), reads=['cand', 'c24'], writes=['tmpc'])
                    P.op('dve', lambda e, h=h: e.max(c24[:, h, 8:16], tmpc), reads=['tmpc'], writes=['c24'])
                    P.op('dve', lambda e, h=h: e.match_replace(tmpc2, c24[:, h, 8:16], tmpc, -1e30), reads=['tmpc', 'c24'], writes=['tmpc2'])
                    P.op('dve', lambda e, h=h: e.max(c24[:, h, 16:24], tmpc2), reads=['tmpc2'], writes=['c24'])
                if PEER_SUB < 4:
                    continue
                P.op('dve', lambda e, sm_=sm_: e.tensor_tensor(sm_[:, 0:8], c24[:, :, 15], c24[:, :, 16], ALU.add), reads=['c24'], writes=[ksm])
                P.op('dve', lambda e, sm_=sm_: e.tensor_scalar(sm_[:, 0:8], sm_[:, 0:8], 0.5, None, ALU.mult), reads=[ksm], writes=[ksm])
                P.op('dve', lambda e, sm_=sm_: e.tensor_scalar(sm_[:, 8:16], c24[:, :, 0], -1.0, None, ALU.mult), reads=['c24'], writes=[ksm])
                for h in range(8):
                    ch = cand[:, h, :]
                    P.op('act', lambda e, h=h, ch=ch, sm_=sm_: e.activation(ec, ch, AF.Exp, bias=sm_[:, 8 + h:9 + h]), reads=['cand', ksm], writes=['ec'])
                    P.op('dve', lambda e, h=h, ch=ch, sm_=sm_: e.scalar_tensor_tensor(junkc, ch, sm_[:, h:h + 1], ec, ALU.is_ge, ALU.mult, accum_out=sm_[:, 16 + h:17 + h]), reads=['cand', 'ec', ksm], writes=['junkc', ksm])
                P.op('act', lambda e, sm_=sm_: e.activation(sm_[:, 24:32], sm_[:, 16:24], AF.Ln), reads=[ksm], writes=[ksm])
                P.op('dve', lambda e, sm_=sm_: e.tensor_tensor(sm_[:, 32:40], sm_[:, 8:16], sm_[:, 24:32], ALU.subtract), reads=[ksm], writes=[ksm])
                P.op('dve', lambda e, sm_=sm_: e.tensor_tensor(sm_[:, 40:48], sm_[:, 0:8], sm_[:, 32:40], ALU.add), reads=[ksm], writes=[ksm])
                P.op('act', lambda e, sm_=sm_: e.activation(sm_[:, 48:56], sm_[:, 40:48], AF.Exp), reads=[ksm], writes=[ksm])
                for h in range(8):
                    P.op('dve', lambda e, h=h, st_=st_, sm_=sm_, ti=ti: e.tensor_scalar(s1n[ti][:, h, :], st_[:, h, 0, :], sm_[:, h:h + 1], None, ALU.subtract), reads=[f"stok{ti}", ksm], writes=[f"s1n{ti}"])
                    P.op('dve', lambda e, h=h, sm_=sm_, ti=ti: e.tensor_scalar(Dg[ti][h], self.identb, sm_[:, 48 + h:49 + h], None, ALU.mult), reads=['identb', ksm], writes=[f"Dg{ti}"])
            Gps = [self.ps[0], self.ps[1]]
            Aps = [self.ps[2], self.ps[3]]
            Ops = [[self.ps[4], self.ps[5]], [self.ps[6], self.ps[7]]]
            if PEER_STAGE < 4:
                continue
            for ib in range(32):
                wb_ = wcnt % 2
                wcnt += 1
                uT_, vs_ = uTs[wb_], vss[wb_]
                P.dma('sp', uT_, self.S["uT"][:, :, ib * 512:(ib + 1) * 512].rearrange("k p e -> p k e"), writes=[f"uT{wb_}"])
                P.dma('sp', vs_, self.S["vb"][ib * 512:(ib + 1) * 512, :].rearrange("(c j) d -> j c d", j=128), writes=[f"vs{wb_}"])
                for ti in range(2):
                    st_ = s_tok[ti]
                    for hb in range(2):
                        eb = ecnt % 2
                        ecnt += 1
                        d_, e_, m_ = dE[eb], dE[2 + eb], mB[hb]
                        kd, ke = dEk[eb], dEk[2 + eb]
                        P.op('dve', lambda e, d_=d_, st_=st_, hb=hb, ti=ti, ib=ib: e.tensor_tensor(d_, st_[:, hb * 4:(hb + 1) * 4, 1, :].unsqueeze(2).broadcast_to([128, 4, 4, 128]), s1n[ti][:, hb * 4:(hb + 1) * 4, ib * 4:(ib + 1) * 4].unsqueeze(3).broadcast_to([128, 4, 4, 128]), ALU.add), reads=[f"stok{ti}", f"s1n{ti}"], writes=[kd])
                        P.op('act', lambda e, d_=d_, e_=e_: e.activation(e_, d_, AF.Exp), reads=[kd], writes=[ke])
                        P.op('dve', lambda e, d_=d_, e_=e_, m_=m_: e.scalar_tensor_tensor(m_, d_, 0.0, e_, ALU.is_ge, ALU.mult), reads=[kd, ke], writes=[f"mB{hb}"])
                    for il in range(4):
                        G_ = Gps[il // 2]
                        for h in range(8):
                            P.op('pe', lambda e, G_=G_, il=il, ti=ti, h=h: e.matmul(G_[:, (il % 2) * 256 + ti * 128:(il % 2) * 256 + (ti + 1) * 128], mB[h // 4][:, h % 4, il, :], Dg[ti][h], start=(h == 0), stop=(h == 7)), reads=[f"mB{h // 4}", f"Dg{ti}"], writes=[f"ps{il // 2}"])
                for il in range(4):
                    A_ = Aps[il // 2]
                    for k in range(8):
                        P.op('pe', lambda e, A_=A_, il=il, k=k, uT_=uT_: e.matmul(A_[:, (il % 2) * 256:(il % 2 + 1) * 256], uT_[:, k, il * 128:(il + 1) * 128], h2Tb[:, k, :], start=(k == 0), stop=(k == 7)), reads=[f"uT{wb_}", 'h2Tb'], writes=[f"ps{2 + il // 2}"])
                gb = 0
                for hf in range(2):
                    P.op('act', lambda e, hf=hf, gb=gb: e.activation(ga[gb][:, hf * 2:hf * 2 + 2, :], Aps[hf][:, :].rearrange("p (i t) -> p i t", i=2), AF.Gelu_apprx_tanh), reads=[f"ps{2 + hf}"], writes=[f"ga{gb}"])
                    P.op('dve', lambda e, hf=hf, gb=gb: e.tensor_tensor(Wt[gb][:, hf * 2:hf * 2 + 2, :], Gps[hf][:, :].rearrange("p (i t) -> p i t", i=2), ga[gb][:, hf * 2:hf * 2 + 2, :], ALU.mult), reads=[f"ps{hf}", f"ga{gb}"], writes=[f"Wt{gb}"])
                for il in range(4):
                    i = ib * 4 + il
                    for ti in range(2):
                        for half in range(2):
                            P.op('pe', lambda e, il=il, ti=ti, half=half, i=i, gb=gb, vs_=vs_: e.matmul(Ops[ti][half][:, :], Wt[gb][:, il, ti * 128:(ti + 1) * 128], vs_[:, il, half * 512:(half + 1) * 512], start=(i == 0), stop=(i == 127)), reads=[f"Wt{gb}", f"vs{wb_}"], writes=[f"ps{4 + 2 * ti + half}"])
            for ti in range(2):
                tt = g0 + ti
                sl = slice(tt * 128, (tt + 1) * 128)
                for half in range(2):
                    hs = slice(half * 512, (half + 1) * 512)
                    P.op('dve', lambda e, ti=ti, half=half, hs=hs, r=r: e.tensor_tensor(tb[:, hs], Ops[ti][half][:, :], g2[r][:, hs], ALU.mult), reads=[f"ps{4 + 2 * ti + half}", f"g2{r}"], writes=['tb'])
                    P.op('pool', lambda e, ti=ti, hs=hs: e.tensor_tensor(x2[ti][:, hs], tb[:, hs], x1t[ti][:, hs], ALU.add), reads=['tb', f"x1t{ti}"], writes=["tmpf"])
                P.dma('pool', self.S["xs"][sl, :], x2[ti], reads=["tmpf"], writes=['xs'])

    def phase_final(self):
        P = self.P
        self.new_phase()
        fn = self.alloc(D)
        self.bvec(fn, self.I["final_norm"].rearrange("(o d) -> o d", o=1), 'fn')
        xts = [self.alloc(D) for _ in range(2)]
        os_ = [self.alloc(D) for _ in range(2)]
        tmpf = self.alloc(D)
        stt = self.alloc(4 * NT)
        for tt in range(2, NT):
            b = tt % 2
            st = stt[:, 4 * tt:4 * tt + 4]
            kst = f"st{tt}"
            P.dma('sp', xts[b], self.S["xs"][tt * 128:(tt + 1) * 128, :], writes=[f"xt{b}"])
            P.op('act', lambda e, b=b, st=st: e.activation(tmpf, xts[b], AF.Square, accum_out=st[:, 0:1]), reads=[f"xt{b}"], writes=['tmpf', kst])
            P.op('act', lambda e, st=st: e.activation(st[:, 1:2], st[:, 0:1], AF.Sqrt, bias=EPS, scale=1.0 / D), reads=[kst], writes=[kst])
            P.op('dve', lambda e, st=st: e.reciprocal(st[:, 2:3], st[:, 1:2]), reads=[kst], writes=[kst])
            P.op('dve', lambda e, b=b, st=st: e.scalar_tensor_tensor(os_[b], xts[b], st[:, 2:3], fn, ALU.mult, ALU.mult), reads=[f"xt{b}", kst, 'fn'], writes=[f"os{b}"])
            P.dma('pool', self.out[(tt - 2) * 128:(tt - 1) * 128, :], os_[b], reads=[f"os{b}"], writes=['out'])


def _consts():
    t = np.arange(NLAT)
    rows = (t // 64).astype(np.float32)
    cols = (t % 64).astype(np.float32)
    inv = (10000.0 ** (-np.arange(0, 16, 2, dtype=np.float32) / 16)).astype(np.float32)
    ar = rows[:, None] * inv
    ac = cols[:, None] * inv
    ang = np.concatenate([ar, ar, ac, ac], axis=-1).astype(np.float32)
    cos = np.cos(ang).astype(np.float32)
    sin = np.sin(ang).astype(np.float32)
    sgn = np.concatenate([-np.ones(8), np.ones(8), -np.ones(8), np.ones(8)]).astype(np.float32)
    sinS = sin * sgn
    k_cos = np.concatenate([np.ones((NCTX, 32), np.float32), cos], 0)
    k_sin = np.concatenate([np.zeros((NCTX, 32), np.float32), sinS], 0)
    q_cosT = np.ascontiguousarray((k_cos * np.float32(SC_MLA)).T)
    q_sinT = np.ascontiguousarray((k_sin * np.float32(SC_MLA)).T)
    s = np.arange(128)
    same = (s[:, None] // 64) == (s[None, :] // 64)
    le = s[:, None] <= s[None, :]
    ge = s[:, None] >= s[None, :]
    gla_c = np.stack([np.where(same & le, -1.0 / 16, 0.0), np.where(same & ge, -1.0 / 16, 0.0),
                      np.where(same & le, 1.0, 0.0), np.where(same & ge, 1.0, 0.0)]).astype(np.float32)
    return dict(k_cos=k_cos, k_sin=k_sin, q_cosT=q_cosT, q_sinT=q_sinT, gla_c=gla_c)


def _na_index():
    dr = np.zeros((128, 5, 7, 128), np.int64)
    dc = np.zeros((128, 5, 7, 128), np.int64)
    valid = np.zeros((128, 5, 7, 128), bool)
    kp = np.arange(128)[:, None]
    qi = np.arange(128)[None, :]
    for pc, m in enumerate([0, 1, 2, 30, 31]):
        kb = min(max(m - 2, 0), 27)
        for j in range(5):
            kr = 2 * (kb + j) + kp // 64
            wk = kp % 64
            r = 2 * m + qi // 64
            wq = qi % 64
            rs = np.clip(r - 4, 0, 56)
            cs = np.clip(wq - 8, 0, 48)
            ok = (kr >= rs) & (kr < rs + 8) & (wk >= cs) & (wk < cs + 16)
            valid[:, pc, j, :] = ok
            dr[:, pc, j, :] = np.clip(kr - r + 7, 0, 14)
            dc[:, pc, j, :] = np.clip(wk - wq, -15, 15) + 15
    valid[:, :, 5:7, :] = True
    return dr, dc, valid


def _prep_shared(inp):
    sh = {}
    w_in = inp["w_in"]
    kr = w_in[:, :, 640:672]
    swap = np.concatenate([kr[..., 8:16], kr[..., 0:8], kr[..., 24:32], kr[..., 16:24]], -1)
    sh["w_in_x"] = np.ascontiguousarray(np.concatenate([w_in, swap], -1))
    wuq = inp["mla_w_uq"].reshape(2, 384, 8, 96)
    rp = wuq[..., 64:96]
    rsw = np.concatenate([rp[..., 8:16], rp[..., 0:8], rp[..., 24:32], rp[..., 16:24]], -1)
    z = np.zeros((2, 384, 8, 64), np.float32)
    sh["w_uq_x"] = np.ascontiguousarray(np.concatenate([wuq, z, rsw], -1).reshape(2, 384, 1536))
    dr, dc, valid = _na_index()
    rpb = inp["na_rpb"]
    nb = rpb[:, :, dr, dc]
    nb = np.where(valid[None, None], nb, np.float32(0.0)).astype(np.float32)
    nb[:, :, :, :, 5:7, :] = 0.0
    sh["na_bias"] = np.ascontiguousarray(nb.reshape(2, 8, 128, 4480))
    sh["na_mask"] = np.ascontiguousarray(np.where(valid, 0.0, -1e30).astype(np.float32).reshape(128, 4480))
    sh.update(_consts())
    for k in ["w_ada", "b_ada", "norm1", "mla_q_norm", "mla_kv_norm", "mla_w_ukv", "gla_w_gk_fwd", "gla_b_gk_fwd",
              "gla_w_gk_bwd", "gla_b_gk_bwd", "gla_norm", "w_o_mla", "w_o_na", "w_o_gla", "w_out", "norm2", "peer_w_q",
              "peer_sub_keys", "peer_u", "peer_v", "final_norm"]:
        sh[k] = np.ascontiguousarray(inp[k], dtype=np.float32)
    return sh


_CACHE = {}


def build_nc(nlayers=2, dbg=False, phases=None, scr_in=()):
    nc = bass.Bass("TRN2", target_bir_lowering=False)
    with contextlib.ExitStack() as st:
        b = Builder(nc, st, nlayers=nlayers, dbg=dbg, phases=phases, scr_in=scr_in)
        b.build()
    return nc, b


def kernel(**inp):
    inp = {k: np.asarray(v) for k, v in inp.items()}
    sh = _prep_shared(inp)
    nc, b = build_nc()
    in_maps = []
    ncores = 4
    for c in range(ncores):
        m = dict(sh)
        m["x"] = np.ascontiguousarray(inp["x"][c], dtype=np.float32)
        m["ctx"] = np.ascontiguousarray(inp["ctx"][c], dtype=np.float32)
        m["cvecs"] = np.ascontiguousarray(np.stack([inp["c"][c], inp["c_ctx"]]), dtype=np.float32)
        in_maps.append({k: v for k, v in m.items() if k in b.I})
    res = run_bass_kernel_spmd(nc, in_maps, core_ids=list(range(ncores)))
    return np.stack([np.asarray(r["out"], dtype=np.float32) for r in res.results], 0)
```

```python
import contextlib
import types
import numpy as np
import concourse.bass as bass
import concourse.mybir as mybir
from concourse.bass_utils import run_bass_kernel_spmd

F32 = mybir.dt.float32
BF16 = mybir.dt.bfloat16
AF = mybir.ActivationFunctionType
ALU = mybir.AluOpType

SEM_LIMIT = 16000
DMA_LIMIT = 900
DMA_R = 6
SAMESYNC = True

D = 1024
NCTX = 256
NLAT = 4096
NTOK = NCTX + NLAT
NT = NTOK // 128
EPS = 1e-6
INX = 6880
C_GATE = 3776
FM_ROWS = 2080
FM_NAQ, FM_NAK, FM_GQ, FM_GK, FM_OG, FM_LOW = 0, 512, 1024, 1280, 1536, 2048
SC_MLA = 96 ** -0.5
NEXP = 16384


def _freeze(fn):
    if fn.__closure__:
        cells = []
        for c in fn.__closure__:
            try:
                cells.append(types.CellType(c.cell_contents))
            except ValueError:
                cells.append(c)
        fn = types.FunctionType(fn.__code__, fn.__globals__, fn.__name__, fn.__defaults__, tuple(cells))
    return fn


class Prog:
    ENG = ['pe', 'act', 'dve', 'pool', 'sp']

    def __init__(self, nc, stack):
        self.nc = nc
        self.stack = stack
        self.stream = {e: [] for e in self.ENG}
        self.sems = {}
        self.ecount = {e: 0 for e in self.ENG}
        self.known = {e: {} for e in self.ENG}
        self.evclock = {}
        self.lastw = {}
        self.readers = {}
        self.dslot = {}
        self.dcount = {}
        self.nops = 0

    def sem(self, key):
        if key not in self.sems:
            self.sems[key] = self.stack.enter_context(self.nc.semaphore("s" + "_".join(str(k) for k in key)))
        return self.sems[key]

    def _deps(self, eng, reads, writes, samesync):
        deps = {}

        def add(ev):
            sk, v = ev
            if not samesync and sk[0] == 'e' and sk[1] == eng:
                return
            if deps.get(sk, 0) < v:
                deps[sk] = v
        for k in reads:
            if k in self.lastw:
                add(self.lastw[k])
        for k in writes:
            if k in self.lastw:
                add(self.lastw[k])
            for ev in self.readers.get(k, ()):
                add(ev)
        kn = self.known[eng]
        waits = []
        for sk, v in deps.items():
            if kn.get(sk, 0) >= v:
                continue
            waits.append((sk, v))
        for sk, v in waits:
            if kn.get(sk, 0) < v:
                kn[sk] = v
            for sk2, v2 in self.evclock.get((sk, v), {}).items():
                if kn.get(sk2, 0) < v2:
                    kn[sk2] = v2
        return waits

    def _commit(self, ev, reads, writes):
        for k in reads:
            self.readers.setdefault(k, []).append(ev)
        for k in writes:
            self.lastw[k] = ev
            self.readers[k] = []

    def op(self, eng, fn, reads=(), writes=()):
        fn = _freeze(fn)
        waits = self._deps(eng, reads, writes, SAMESYNC and eng != 'pe')
        self.ecount[eng] += 1
        n = self.ecount[eng]
        sk = ('e', eng, (n - 1) // SEM_LIMIT)
        ev = (sk, (n - 1) % SEM_LIMIT + 1)
        self.evclock[ev] = dict(self.known[eng])
        self.stream[eng].append((waits, fn, ev, 1))
        self._commit(ev, reads, writes)
        self.nops += 1
        return ev

    def dma(self, q, out, in_, reads=(), writes=(), **kw):
        waits = self._deps(q, reads, writes, True)
        i = self.dslot.get(q, 0)
        self.dslot[q] = (i + 1) % DMA_R
        c = self.dcount.get((q, i), 0)
        ep, cc = divmod(c, DMA_LIMIT)
        sk = ('d', q, i, ep)
        kn = self.known[q]
        if cc > 0:
            if kn.get(sk, 0) < cc * 16:
                waits.append((sk, cc * 16))
                kn[sk] = cc * 16
        elif ep > 0:
            skp = ('d', q, i, ep - 1)
            if kn.get(skp, 0) < DMA_LIMIT * 16:
                waits.append((skp, DMA_LIMIT * 16))
                kn[skp] = DMA_LIMIT * 16
        self.dcount[(q, i)] = c + 1
        ev = (sk, (cc + 1) * 16)
        self.evclock[ev] = dict(kn)
        fn = lambda e, out=out, in_=in_, kw=kw: e.dma_start(out=out, in_=in_, **kw)
        self.stream[q].append((waits, fn, ev, 16))
        self._commit(ev, reads, writes)
        self.nops += 1
        return ev

    def _all_events(self):
        evs = []
        for (q, i), c in self.dcount.items():
            ep, cc = divmod(c, DMA_LIMIT)
            if cc == 0:
                ep, cc = ep - 1, DMA_LIMIT
            evs.append((('d', q, i, ep), cc * 16))
        for e in self.ENG:
            n = self.ecount[e]
            if n:
                evs.append((('e', e, (n - 1) // SEM_LIMIT), (n - 1) % SEM_LIMIT + 1))
        return evs

    def barrier(self):
        evs = self._all_events()
        for eng in self.ENG:
            kn = self.known[eng]
            waits = []
            for sk, v in evs:
                if sk[0] == 'e' and sk[1] == eng:
                    continue
                if kn.get(sk, 0) < v:
                    waits.append((sk, v))
                    kn[sk] = v
            if waits:
                self.stream[eng].append((waits, None, None, 0))
        self.lastw = {}
        self.readers = {}

    def emit(self):
        nc = self.nc
        for e in self.ENG:
            for waits, fn, ev, inc in self.stream[e]:
                for sk, v in waits:
                    self.sem(sk)
                if ev is not None:
                    self.sem(ev[0])
        with nc.Block() as block:
            def replay(name, e):
                for waits, fn, ev, inc in self.stream[name]:
                    for sk, v in waits:
                        e.wait_ge(self.sems[sk], v)
                    if fn is not None:
                        fn(e).then_inc(self.sems[ev[0]], inc)

            @block.sync
            def _(e):
                replay('sp', e)

            @block.tensor
            def _(e):
                replay('pe', e)

            @block.scalar
            def _(e):
                replay('act', e)

            @block.vector
            def _(e):
                replay('dve', e)

            @block.gpsimd
            def _(e):
                replay('pool', e)


ARENA_WORDS = 51000
GLA_STAGE = 99
GLA_NT = 0
MLA_STAGE = 99
PEER_GROUPS = 0
PEER_STAGE = 99
PEER_SUB = 99
MLA_SKIP = set()
GLA_HH = 2


class LazyIn(dict):
    def __init__(self, nc):
        super().__init__()
        self.nc = nc
        self.shapes = {}

    def __missing__(self, name):
        sh = self.shapes[name]
        shape, dt = sh[0], sh[1]
        kind = sh[2] if len(sh) > 2 else "ExternalInput"
        ap = self.nc.dram_tensor(name, shape, dt, kind=kind).ap()
        self[name] = ap
        return ap


class Builder:
    def __init__(self, nc, st, nlayers=2, dbg=False, phases=None, scr_in=()):
        self.phases = phases
        self.scr_in = set(scr_in)
        self.nc = nc
        self.P = Prog(nc, st)
        self.arena = st.enter_context(nc.sbuf_tensor("arena", [128, ARENA_WORDS], F32))
        self.top = 0
        self.ps = [st.enter_context(nc.psum_tensor(f"ps{i}", [128, 512], F32)) for i in range(8)]
        self.nlayers = nlayers
        self.dbg = dbg
        self.uid = 0
        self.I = LazyIn(nc)
        self.S = LazyIn(nc)

    def alloc(self, n, dt=F32):
        w = n if dt == F32 else (n + 1) // 2
        assert self.top + w <= ARENA_WORDS, (self.top, w)
        a = self.arena[:, self.top:self.top + w]
        self.top += w
        return a if dt == F32 else a.bitcast(dt)

    def key(self, s):
        self.uid += 1
        return f"{s}#{self.uid}"

    def psb(self, i):
        return self.ps[i][:, :].bitcast(BF16)

    def inp(self, name, shape, dt=F32):
        self.I.shapes[name] = (list(shape), dt)

    def scratch(self, name, shape, dt):
        kind = "ExternalInput" if name in self.scr_in else ("ExternalOutput" if self.dbg else "Internal")
        self.S.shapes[name] = (list(shape), dt, kind)

    def new_phase(self):
        self.P.barrier()
        self.top = self.persist_top

    def declare(self):
        L = 2
        self.inp("x", [NLAT, D]); self.inp("ctx", [NCTX, D]); self.inp("cvecs", [2, D])
        self.inp("w_ada", [L, D, 6 * D]); self.inp("b_ada", [L, 6 * D]); self.inp("norm1", [L, D])
        self.inp("w_in_x", [L, D, INX]); self.inp("mla_q_norm", [L, 384]); self.inp("w_uq_x", [L, 384, 1536])
        self.inp("mla_kv_norm", [L, 256]); self.inp("mla_w_ukv", [L, 256, 1024])
        self.inp("na_bias", [L, 8, 128, 4480])
        self.inp("gla_w_gk_fwd", [L, 16, 256]); self.inp("gla_b_gk_fwd", [L, 256])
        self.inp("gla_w_gk_bwd", [L, 16, 256]); self.inp("gla_b_gk_bwd", [L, 256]); self.inp("gla_norm", [L, 128])
        self.inp("w_o_mla", [L, 512, D]); self.inp("w_o_na", [L, 512, D]); self.inp("w_o_gla", [L, 512, D])
        self.inp("w_out", [L, D, D]); self.inp("norm2", [L, D]); self.inp("peer_w_q", [L, D, D])
        self.inp("peer_sub_keys", [L, 2, 128, 64]); self.inp("peer_u", [L, NEXP, D]); self.inp("peer_v", [L, NEXP, D])
        self.inp("final_norm", [D])
        self.inp("k_cos", [NTOK, 32]); self.inp("k_sin", [NTOK, 32])
        self.inp("q_cosT", [32, NTOK]); self.inp("q_sinT", [32, NTOK])
        self.inp("na_mask", [128, 4480]); self.inp("gla_c", [4, 128, 128])
        self.out = self.nc.dram_tensor("out", [NLAT, D], F32, kind="ExternalOutput").ap()
        self.scratch("xs", [NTOK, D], F32); self.scratch("xmid", [NTOK, D], F32)
        self.scratch("mod", [L, 2, 6 * D], F32)
        self.scratch("proj_tm", [NTOK, INX], BF16); self.scratch("proj_fm", [FM_ROWS, NTOK], BF16)
        self.scratch("oT", [3, 512, NTOK], BF16)
        self.scratch("uT", [8, 128, NEXP], BF16); self.scratch("vb", [NEXP, D], BF16)

    def build(self):
        P = self.P
        self.declare()
        self.identf = self.alloc(128); self.identb = self.alloc(128, BF16)
        self.onesf = self.alloc(128); self.onesb = self.alloc(128, BF16)
        P.op('pool', lambda e: e.memset(self.identf, 1.0), writes=['identf'])
        P.op('pool', lambda e: e.affine_select(self.identf, self.identf, [[-1, 128]], ALU.is_equal, 0.0, base=0, channel_multiplier=1), reads=['identf'], writes=['identf'])
        P.op('dve', lambda e: e.tensor_copy(self.identb, self.identf), reads=['identf'], writes=['identb'])
        P.op('pool', lambda e: e.memset(self.onesf, 1.0), writes=['onesf'])
        P.op('pool', lambda e: e.memset(self.onesb, 1.0), writes=['onesb'])
        self.persist_top = self.top
        if "xs" not in self.scr_in:
            P.dma('sp', self.S["xs"][0:NCTX, :], self.I["ctx"][:, :], writes=['xs'])
            P.dma('sp', self.S["xs"][NCTX:NTOK, :], self.I["x"][:, :], writes=['xs'])
        P.barrier()
        allp = ["mod", "AP", "mla", "na", "gla", "merge", "peer_prep", "peer"]
        for l in range(self.nlayers):
            for ph in allp:
                if self.phases is None or ph in self.phases:
                    getattr(self, "phase_" + ph)(l)
        if self.phases is None or "final" in self.phases:
            self.phase_final()
        P.barrier()
        P.emit()

    def bvec(self, dst, src_row, key):
        self.P.dma('sp', dst, src_row.partition_broadcast(128), writes=[key])

    def qtiles(self, l):
        return list(range(0, NT)) if l == 0 else list(range(2, NT))

    def phase_mod(self, l):
        P = self.P
        self.new_phase()
        sc = self.alloc(16).rearrange("p (r k) -> p r k", r=2)
        brow = self.alloc(6144)
        modsb = self.alloc(6144)
        wb = [self.alloc(4096).rearrange("p (k n) -> p k n", k=8) for _ in range(2)]
        for r in range(2):
            P.dma('sp', sc[:, r, :], self.I["cvecs"][r, :].rearrange("(k p) -> p k", p=128), writes=['sc'], allow_slow_non_contiguous=True)
        P.op('act', lambda e: e.activation(sc, sc, AF.Silu), reads=['sc'], writes=['sc'])
        P.dma('sp', brow[0:1, :], self.I["b_ada"][l:l + 1, :], writes=['brow'])
        for g in range(12):
            w = wb[g % 2]
            wk = f"modw{g % 2}"
            P.dma('sp', w, self.I["w_ada"][l, :, g * 512:(g + 1) * 512].rearrange("(k p) n -> p k n", p=128), writes=[wk])
            pk = f"ps{g % 2}"
            pst = self.ps[g % 2]
            for k in range(8):
                P.op('pe', lambda e, k=k, w=w, pst=pst: e.matmul(pst[0:2, :], sc[:, :, k], w[:, k, :], start=(k == 0), stop=False), reads=['sc', wk], writes=[pk])
            P.op('pe', lambda e, g=g, pst=pst: e.matmul(pst[0:2, :], self.onesf[0:1, 0:2], brow[0:1, g * 512:(g + 1) * 512], start=False, stop=True), reads=['brow', 'onesf'], writes=[pk])
            P.op('act', lambda e, g=g, pst=pst: e.copy(modsb[0:2, g * 512:(g + 1) * 512], pst[0:2, :]), reads=[pk], writes=['modsb'])
        P.dma('sp', self.S["mod"][l, :, :], modsb[0:2, :], reads=['modsb'], writes=['mod'])

    def modrow(self, l, r, i):
        return self.S["mod"][l, r:r + 1, i * D:(i + 1) * D]

    def load_modvecs(self, l, norm_name, i_sh, i_sc):
        P = self.P
        res = []
        nrm = self.alloc(D)
        self.bvec(nrm, self.I[norm_name][l:l + 1, :], 'nrm')
        for r in range(2):
            weff = self.alloc(D); sh = self.alloc(D)
            kw, ks = self.key('weff'), self.key('sh')
            self.bvec(weff, self.modrow(l, r, i_sc), kw)
            self.bvec(sh, self.modrow(l, r, i_sh), ks)
            P.op('dve', lambda e, weff=weff: e.scalar_tensor_tensor(weff, weff, 1.0, nrm, ALU.add, ALU.mult), reads=[kw, 'nrm'], writes=[kw])
            res.append((weff, kw, sh, ks))
        return res

    def norm_mod(self, xt, kx, dst, kd, weff, kw, sh, ks, tmpf, kt, st, kst):
        P = self.P
        P.op('act', lambda e: e.activation(tmpf, xt, AF.Square, accum_out=st[:, 0:1]), reads=[kx], writes=[kt, kst])
        P.op('act', lambda e: e.activation(st[:, 1:2], st[:, 0:1], AF.Sqrt, bias=EPS, scale=1.0 / D), reads=[kst], writes=[kst])
        P.op('dve', lambda e: e.reciprocal(st[:, 2:3], st[:, 1:2]), reads=[kst], writes=[kst])
        P.op('dve', lambda e: e.scalar_tensor_tensor(tmpf, xt, st[:, 2:3], weff, ALU.mult, ALU.mult), reads=[kx, kst, kw], writes=[kt])
        P.op('dve', lambda e: e.tensor_tensor(dst, tmpf, sh, ALU.add), reads=[kt, ks], writes=[kd])

    def phase_AP(self, l):
        P = self.P
        self.new_phase()
        hT = self.alloc(8 * NTOK, BF16).rearrange("p (k t) -> p k t", k=8)
        m0 = self.top
        mv = self.load_modvecs(l, "norm1", 0, 1)
        xts = [self.alloc(D) for _ in range(2)]
        hbs = [self.alloc(D, BF16) for _ in range(2)]
        tmpf = self.alloc(D)
        stt = self.alloc(4 * NT)
        for tt in range(NT):
            b = tt % 2
            xt, hb = xts[b], hbs[b]
            kx, kh, kp = f"xt{b}", f"hb{b}", f"ps{b}"
            st = stt[:, 4 * tt:4 * tt + 4]
            kst = f"st{tt}"
            P.dma('sp', xt, self.S["xs"][tt * 128:(tt + 1) * 128, :], writes=[kx])
            weff, kw, sh, ks = mv[1 if tt < 2 else 0]
            self.norm_mod(xt, kx, hb, kh, weff, kw, sh, ks, tmpf, 'tmpf', st, kst)
            pv = self.psb(b).rearrange("p (k t) -> p k t", k=8)
            for k in range(8):
                P.op('pe', lambda e, k=k, hb=hb, pv=pv: e.transpose(pv[:, k, :], hb[:, k * 128:(k + 1) * 128], self.identb), reads=[kh, 'identb'], writes=[kp])
            P.op('act', lambda e, tt=tt, pv=pv: e.copy(hT[:, :, tt * 128:(tt + 1) * 128], pv), reads=[kp], writes=[f"hT{tt}"])
        hkeys = [f"hT{tt}" for tt in range(NT)]
        P.barrier()
        for tt in range(NT):
            P.lastw[f"hT{tt}"] = None
        P.lastw = {}
        self.top = m0
        groups = [(i * 512, 512) for i in range(7)] + [(3584, 192)] + [(C_GATE + i * 512, 512) for i in range(6)] + [(6848, 32)]
        wfs = [self.alloc(4096).rearrange("p (k n) -> p k n", k=8) for _ in range(2)]
        wbs = [self.alloc(4096, BF16).rearrange("p (k n) -> p k n", k=8) for _ in range(2)]
        stg = [self.alloc(512, BF16) for _ in range(3)]
        cnt = 0
        for gi, (c0, ncol) in enumerate(groups):
            b = gi % 2
            wf, wbb = wfs[b], wbs[b]
            P.dma('sp', wf[:, :, 0:ncol], self.I["w_in_x"][l, :, c0:c0 + ncol].rearrange("(k p) n -> p k n", p=128), writes=[f"wf{b}"])
            P.op('pool', lambda e, wf=wf, wbb=wbb, ncol=ncol: e.tensor_copy(wbb[:, :, 0:ncol], wf[:, :, 0:ncol]), reads=[f"wf{b}"], writes=[f"wb{b}"])
            func = AF.Sigmoid if (C_GATE <= c0 < 6848) else AF.Copy
            for tt in range(NT):
                pb = cnt % 3
                sb_ = cnt % 3
                cnt += 1
                pst = self.ps[pb]
                for k in range(8):
                    P.op('pe', lambda e, k=k, tt=tt, pst=pst, wbb=wbb, ncol=ncol: e.matmul(pst[:, 0:ncol], hT[:, k, tt * 128:(tt + 1) * 128], wbb[:, k, 0:ncol], start=(k == 0), stop=(k == 7)), reads=[f"wb{b}"], writes=[f"ps{pb}"])
                s_ = stg[sb_]
                P.op('act', lambda e, pst=pst, s_=s_, ncol=ncol, func=func: e.activation(s_[:, 0:ncol], pst[:, 0:ncol], func), reads=[f"ps{pb}"], writes=[f"stg{sb_}"])
                P.dma('pool', self.S["proj_tm"][tt * 128:(tt + 1) * 128, c0:c0 + ncol], s_[:, 0:ncol], reads=[f"stg{sb_}"], writes=['proj_tm'])
        fm = [(672 + 128 * i, 128, 0.125, FM_NAQ + 128 * i) for i in range(4)] + [(1184 + 128 * i, 128, 1.0, FM_NAK + 128 * i) for i in range(4)] + \
             [(2208 + 128 * i, 128, 0.125, FM_GQ + 128 * i) for i in range(2)] + [(2464 + 128 * i, 128, 1.0, FM_GK + 128 * i) for i in range(2)] + \
             [(3232 + 128 * i, 128, 1.0, FM_OG + 128 * i) for i in range(4)] + [(3744, 32, 1.0, FM_LOW)]
        tgs = [(0, 256)] + [(256 + 512 * i, 512) for i in range(8)]
        for gi, (c0, nr, scl, r0) in enumerate(fm):
            b = gi % 2
            wf, wbb = wfs[b], wbs[b]
            P.dma('sp', wf[:, :, 0:nr], self.I["w_in_x"][l, :, c0:c0 + nr].rearrange("(k p) n -> p k n", p=128), writes=[f"wf{b}"])
            P.op('pool', lambda e, wf=wf, wbb=wbb, nr=nr: e.tensor_copy(wbb[:, :, 0:nr], wf[:, :, 0:nr]), reads=[f"wf{b}"], writes=[f"wb{b}"])
            for (t0, n) in tgs:
                pb = cnt % 3
                cnt += 1
                pst = self.ps[pb]
                s_ = stg[pb]
                for k in range(8):
                    P.op('pe', lambda e, k=k, pst=pst, wbb=wbb, nr=nr, t0=t0, n=n: e.matmul(pst[0:nr, 0:n], wbb[:, k, 0:nr], hT[:, k, t0:t0 + n], start=(k == 0), stop=(k == 7)), reads=[f"wb{b}"], writes=[f"ps{pb}"])
                P.op('act', lambda e, pst=pst, s_=s_, nr=nr, n=n, scl=scl: e.activation(s_[0:nr, 0:n], pst[0:nr, 0:n], AF.Copy, scale=scl), reads=[f"ps{pb}"], writes=[f"stg{pb}"])
                P.dma('pool', self.S["proj_fm"][r0:r0 + nr, t0:t0 + n], s_[0:nr, 0:n], reads=[f"stg{pb}"], writes=['proj_fm'])

    def attn_fin(self, O, n, kO, dst, kdst, rec, krec):
        P = self.P
        P.op('dve', lambda e: e.reciprocal(rec[0:64, 0:n], O[64:128, 0:n]), reads=[kO], writes=[krec])
        P.op('dve', lambda e: e.tensor_tensor(dst, O[0:64, 0:n], rec[0:64, 0:n], ALU.mult), reads=[kO, krec], writes=[kdst])

    def phase_mla(self, l):
        P = self.P
        self.new_phase()
        need_ctx = (l == 0)
        cqnT = self.alloc(3 * NTOK, BF16).rearrange("p (k t) -> p k t", k=3)
        ckvnT = self.alloc(2 * NTOK, BF16).rearrange("p (k t) -> p k t", k=2)
        kTs = [self.alloc(NTOK, BF16) for _ in range(2)]
        qTs = [self.alloc(NTOK, BF16) for _ in range(2)]
        Vps = [self.alloc(NT * 128, BF16).rearrange("p (t c) -> p t c", c=128) for _ in range(2)]
        ost1 = self.alloc(NTOK, BF16)
        osts = [ost1, ost1]
        wuq = self.alloc(3 * 1536, BF16).rearrange("p (k n) -> p k n", k=3)
        wukv = self.alloc(2 * 1024, BF16).rearrange("p (k n) -> p k n", k=2)
        cosT = self.alloc(NTOK); sinT = self.alloc(NTOK)
        kcs = self.alloc(NT * 32).rearrange("p (t c) -> p t c", c=32)
        ksn = self.alloc(NT * 32).rearrange("p (t c) -> p t c", c=32)
        PTs = [self.alloc(512, BF16) for _ in range(3)]
        rec = self.alloc(512); tq1 = self.alloc(512); tq2 = self.alloc(512)
        qn = self.alloc(3); kvn = self.alloc(2)
        m0 = self.top
        wtmp = self.alloc(3 * 1536).rearrange("p (k n) -> p k n", k=3)
        P.dma('sp', wtmp, self.I["w_uq_x"][l].rearrange("(k p) n -> p k n", p=128), writes=['wtmp'])
        P.dma('sp', qn, self.I["mla_q_norm"][l, :].rearrange("(k p) -> p k", p=128), writes=['qn'], allow_slow_non_contiguous=True)
        for k in range(3):
            P.op('dve', lambda e, k=k: e.tensor_scalar(wuq[:, k, :], wtmp[:, k, :], qn[:, k:k + 1], None, ALU.mult), reads=['wtmp', 'qn'], writes=['wuq'])
        wtmp2 = wtmp.rearrange("p k n -> p (k n)")[:, 0:2048].rearrange("p (k n) -> p k n", k=2)
        P.dma('sp', wtmp2, self.I["mla_w_ukv"][l].rearrange("(k p) n -> p k n", p=128), writes=['wtmp'])
        P.dma('sp', kvn, self.I["mla_kv_norm"][l, :].rearrange("(k p) -> p k", p=128), writes=['kvn'], allow_slow_non_contiguous=True)
        for k in range(2):
            P.op('dve', lambda e, k=k: e.tensor_scalar(wukv[:, k, :], wtmp2[:, k, :], kvn[:, k:k + 1], None, ALU.mult), reads=['wtmp', 'kvn'], writes=['wukv'])
        P.dma('sp', cosT[64:96, :], self.I["q_cosT"][:, :], writes=['cosT'])
        P.dma('sp', sinT[64:96, :], self.I["q_sinT"][:, :], writes=['sinT'])
        P.dma('sp', kcs, self.I["k_cos"].rearrange("(t p) c -> p t c", p=128), writes=['kcs'])
        P.dma('sp', ksn, self.I["k_sin"].rearrange("(t p) c -> p t c", p=128), writes=['ksn'])
        for b in range(2):
            P.op('pool', lambda e, b=b: e.memset(Vps[b][:, :, 64:128], 1.0), writes=[f"Vp{b}"])
        if MLA_STAGE < 1:
            return
        pjs = [self.alloc(704, BF16) for _ in range(2)]
        junk = self.alloc(384)
        stt = self.alloc(8 * NT)
        cqn = [self.alloc(384, BF16) for _ in range(2)]
        ckvn = [self.alloc(256, BF16) for _ in range(2)]
        kro = [self.alloc(128, BF16) for _ in range(2)]
        for b_ in range(2):
            P.op('pool', lambda e, b_=b_: e.memset(kro[b_], 0.0), writes=[f"kro{b_}"])
        t1 = self.alloc(32); t2 = self.alloc(32)
        for tt in range(NT):
            b = tt % 2
            pj = pjs[b]
            kpj = f"pj{b}"
            st = stt[:, 8 * tt:8 * tt + 8]
            kst = f"st{tt}"
            P.dma('sp', pj[:, 0:672], self.S["proj_tm"][tt * 128:(tt + 1) * 128, 0:672], writes=[kpj])
            P.dma('sp', pj[:, 672:704], self.S["proj_tm"][tt * 128:(tt + 1) * 128, 6848:6880], writes=[kpj])
            if 'sq' not in MLA_SKIP: P.op('act', lambda e, pj=pj, st=st: e.activation(junk[:, 0:384], pj[:, 0:384], AF.Square, accum_out=st[:, 0:1]), reads=[kpj], writes=['junk', kst])
            if 'sq' not in MLA_SKIP: P.op('act', lambda e, pj=pj, st=st: e.activation(junk[:, 0:256], pj[:, 384:640], AF.Square, accum_out=st[:, 1:2]), reads=[kpj], writes=['junk', kst])
            if 'sq' not in MLA_SKIP: P.op('act', lambda e, st=st: e.activation(st[:, 2:3], st[:, 0:1], AF.Sqrt, bias=EPS, scale=1.0 / 384), reads=[kst], writes=[kst])
            if 'sq' not in MLA_SKIP: P.op('act', lambda e, st=st: e.activation(st[:, 3:4], st[:, 1:2], AF.Sqrt, bias=EPS, scale=1.0 / 256), reads=[kst], writes=[kst])
            if 'sq' not in MLA_SKIP: P.op('dve', lambda e, st=st: e.reciprocal(st[:, 4:6], st[:, 2:4]), reads=[kst], writes=[kst])
            if 'norm' not in MLA_SKIP: P.op('dve', lambda e, pj=pj, st=st, b=b: e.tensor_scalar(cqn[b], pj[:, 0:384], st[:, 4:5], None, ALU.mult), reads=[kpj, kst], writes=[f"cqn{b}"])
            if 'norm' not in MLA_SKIP: P.op('dve', lambda e, pj=pj, st=st, b=b: e.tensor_scalar(ckvn[b], pj[:, 384:640], st[:, 5:6], None, ALU.mult), reads=[kpj, kst], writes=[f"ckvn{b}"])
            if 'rope' not in MLA_SKIP: P.op('dve', lambda e, pj=pj, tt=tt: e.tensor_tensor(t1, pj[:, 640:672], kcs[:, tt, :], ALU.mult), reads=[kpj, 'kcs'], writes=['t1'])
            if 'rope' not in MLA_SKIP: P.op('dve', lambda e, pj=pj, tt=tt: e.tensor_tensor(t2, pj[:, 672:704], ksn[:, tt, :], ALU.mult), reads=[kpj, 'ksn'], writes=['t2'])
            if 'rope' not in MLA_SKIP: P.op('dve', lambda e, b=b: e.tensor_tensor(kro[b][:, 64:96], t1, t2, ALU.add), reads=['t1', 't2'], writes=[f"kro{b}"])
            pv = self.psb(b).rearrange("p (k t) -> p k t", k=8)
            kp = f"ps{b}"
            for k in range(3):
                if 'tr' not in MLA_SKIP: P.op('pe', lambda e, k=k, b=b, pv=pv: e.transpose(pv[:, k, :], cqn[b][:, k * 128:(k + 1) * 128], self.identb), reads=[f"cqn{b}"], writes=[kp])
            for k in range(2):
                if 'tr' not in MLA_SKIP: P.op('pe', lambda e, k=k, b=b, pv=pv: e.transpose(pv[:, 3 + k, :], ckvn[b][:, k * 128:(k + 1) * 128], self.identb), reads=[f"ckvn{b}"], writes=[kp])
            if 'tr' not in MLA_SKIP: P.op('pe', lambda e, b=b, pv=pv: e.transpose(pv[:, 5, :], kro[b], self.identb), reads=[f"kro{b}"], writes=[kp])
            sl = slice(tt * 128, (tt + 1) * 128)
            if 'tr' not in MLA_SKIP: P.op('act', lambda e, pv=pv, sl=sl: e.copy(cqnT[:, :, sl], pv[:, 0:3, :]), reads=[kp], writes=['cqnT'])
            if 'tr' not in MLA_SKIP: P.op('act', lambda e, pv=pv, sl=sl: e.copy(ckvnT[:, :, sl], pv[:, 3:5, :]), reads=[kp], writes=['ckvnT'])
            if 'kc' not in MLA_SKIP: P.op('act', lambda e, pv=pv, sl=sl: e.copy(kTs[0][64:96, sl], pv[64:96, 5, :]), reads=[kp], writes=['kT0r'])
            if 'kc' not in MLA_SKIP: P.op('act', lambda e, pv=pv, sl=sl: e.copy(kTs[1][64:96, sl], pv[64:96, 5, :]), reads=[kp], writes=['kT1r'])
        if MLA_STAGE < 2:
            return
        self.top = m0
        tgs = [(0, 256)] + [(256 + 512 * i, 512) for i in range(8)]
        qgs = ([(0, 256, [0, 1])] if need_ctx else []) + [(256 + 512 * i, 512, list(range(NT))) for i in range(8)]
        scnt = 0
        ocnt = 0
        for h in range(8):
            b = h % 2
            kT, qT, Vp, ost = kTs[b], qTs[b], Vps[b], osts[b]
            kkT, kqT, kVp, kost = f"kT{b}", f"qT{b}", f"Vp{b}", "ost"
            for (t0, n, _) in qgs:
                p1, p2 = self.ps[5], self.ps[6]
                for k in range(3):
                    P.op('pe', lambda e, k=k, t0=t0, n=n, h=h: e.matmul(p1[0:96, 0:n], wuq[:, k, h * 192:h * 192 + 96], cqnT[:, k, t0:t0 + n], start=(k == 0), stop=(k == 2)), reads=['wuq', 'cqnT'], writes=['ps5'])
                for k in range(3):
                    P.op('pe', lambda e, k=k, t0=t0, n=n, h=h: e.matmul(p2[0:96, 0:n], wuq[:, k, h * 192 + 96:h * 192 + 192], cqnT[:, k, t0:t0 + n], start=(k == 0), stop=(k == 2)), reads=['wuq', 'cqnT'], writes=['ps6'])
                P.op('act', lambda e, t0=t0, n=n, qT=qT: e.activation(qT[0:64, t0:t0 + n], p1[0:64, 0:n], AF.Copy, scale=SC_MLA), reads=['ps5'], writes=[kqT])
                P.op('dve', lambda e, t0=t0, n=n: e.tensor_tensor(tq1[64:96, 0:n], p1[64:96, 0:n], cosT[64:96, t0:t0 + n], ALU.mult), reads=['ps5', 'cosT'], writes=['tq1'])
                P.op('dve', lambda e, t0=t0, n=n: e.tensor_tensor(tq2[64:96, 0:n], p2[64:96, 0:n], sinT[64:96, t0:t0 + n], ALU.mult), reads=['ps6', 'sinT'], writes=['tq2'])
                P.op('dve', lambda e, t0=t0, n=n, qT=qT: e.tensor_tensor(qT[64:96, t0:t0 + n], tq1[64:96, 0:n], tq2[64:96, 0:n], ALU.add), reads=['tq1', 'tq2'], writes=[kqT])
            for (t0, n) in tgs:
                p3 = self.ps[7]
                for k in range(2):
                    P.op('pe', lambda e, k=k, t0=t0, n=n, h=h: e.matmul(p3[0:64, 0:n], wukv[:, k, h * 128:h * 128 + 64], ckvnT[:, k, t0:t0 + n], start=(k == 0), stop=(k == 1)), reads=['wukv', 'ckvnT'], writes=['ps7'])
                P.op('act', lambda e, t0=t0, n=n, kT=kT: e.copy(kT[0:64, t0:t0 + n], p3[0:64, 0:n]), reads=['ps7'], writes=[kkT])
            for t8 in range(0, NT, 8):
                p3 = self.ps[7]
                nt8 = min(8, NT - t8)
                pv3 = p3[:, :].rearrange("p (t c) -> p t c", c=64)
                for j in range(nt8):
                    tt = t8 + j
                    for k in range(2):
                        P.op('pe', lambda e, k=k, j=j, tt=tt, h=h: e.matmul(pv3[:, j, :], ckvnT[:, k, tt * 128:(tt + 1) * 128], wukv[:, k, h * 128 + 64:h * 128 + 128], start=(k == 0), stop=(k == 1)), reads=['wukv', 'ckvnT'], writes=['ps7'])
                P.op('dve', lambda e, t8=t8, nt8=nt8, Vp=Vp: e.tensor_copy(Vp[:, t8:t8 + nt8, 0:64], pv3[:, 0:nt8, :]), reads=['ps7'], writes=[kVp])
            for (t0, n, kts) in (qgs if MLA_STAGE >= 3 else []):
                ob = 3 + (ocnt % 2)
                ocnt += 1
                O = self.ps[ob]
                kO = f"ps{ob}"
                pend = []
                nk = len(kts)

                def qk(i):
                    nonlocal scnt
                    sb_ = scnt % 3
                    scnt += 1
                    kt = kts[i]
                    S_ = self.ps[sb_]
                    P.op('pe', lambda e, S_=S_, kt=kt: e.matmul(S_[:, 0:n], kT[0:96, kt * 128:(kt + 1) * 128], qT[0:96, t0:t0 + n], start=True, stop=True), reads=[kkT, kqT, 'kT0r', 'kT1r'], writes=[f"ps{sb_}"])
                    PT = PTs[sb_]
                    P.op('act', lambda e, S_=S_, PT=PT: e.activation(PT[:, 0:n], S_[:, 0:n], AF.Exp), reads=[f"ps{sb_}"], writes=[f"PT{sb_}"])
                    return sb_

                def pvm(i, sb_):
                    kt = kts[i]
                    PT = PTs[sb_]
                    P.op('pe', lambda e, PT=PT, kt=kt, i=i: e.matmul(O[:, 0:n], Vp[:, kt, :], PT[:, 0:n], start=(i == 0), stop=(i == nk - 1)), reads=[kVp, f"PT{sb_}"], writes=[kO])
                q = []
                for i in range(nk):
                    q.append((i, qk(i)))
                    if len(q) > 2:
                        pvm(*q.pop(0))
                while q:
                    pvm(*q.pop(0))
                self.attn_fin(O, n, kO, ost[0:64, t0:t0 + n], kost, rec, 'rec')
            P.dma('pool', self.S["oT"][0, h * 64:(h + 1) * 64, :], ost[0:64, :], reads=[kost], writes=['oT'])

    def phase_na(self, l):
        P = self.P
        self.new_phase()
        need_ctx = (l == 0)
        nmask = self.alloc(4480)
        P.dma('sp', nmask, self.I["na_mask"][:, :], writes=['nmask'])
        kTs = [self.alloc(NTOK, BF16) for _ in range(2)]
        qTs = [self.alloc(NTOK, BF16) for _ in range(2)]
        Vps = [self.alloc(NT * 128, BF16).rearrange("p (t c) -> p t c", c=128) for _ in range(2)]
        osts = [self.alloc(NTOK, BF16) for _ in range(2)]
        biasm = [self.alloc(4480).rearrange("p (a j q) -> p a j q", a=5, j=7) for _ in range(2)]
        T1s = [self.alloc(896).rearrange("p (j q) -> p j q", j=7) for _ in range(2)]
        PTs = [self.alloc(896, BF16).rearrange("p (j q) -> p j q", j=7) for _ in range(2)]
        PTc = self.alloc(512, BF16).rearrange("p (j q) -> p j q", j=2)
        rec = self.alloc(256)
        for b in range(2):
            P.op('pool', lambda e, b=b: e.memset(Vps[b][:, :, 64:128], 1.0), writes=[f"Vp{b}"])
        cnt = 0
        for h in range(8):
            b = h % 2
            kT, qT, Vp, ost, bm = kTs[b], qTs[b], Vps[b], osts[b], biasm[b]
            kkT, kqT, kVp, kost, kbm = f"kT{b}", f"qT{b}", f"Vp{b}", f"ost{b}", f"bm{b}"
            P.dma('sp', qT[0:64, :], self.S["proj_fm"][FM_NAQ + h * 64:FM_NAQ + (h + 1) * 64, :], writes=[kqT])
            P.dma('sp', kT[0:64, :], self.S["proj_fm"][FM_NAK + h * 64:FM_NAK + (h + 1) * 64, :], writes=[kkT])
            P.dma('sp', Vp[:, :, 0:64], self.S["proj_tm"][:, 1696 + h * 64:1696 + (h + 1) * 64].rearrange("(t p) c -> p t c", p=128), writes=[kVp])
            bmf = bm.rearrange("p a j q -> p (a j q)")
            P.dma('sp', bmf, self.I["na_bias"][l, h, :, :], writes=[kbm])
            P.op('pool', lambda e, bmf=bmf: e.tensor_tensor(bmf, bmf, nmask, ALU.add), reads=[kbm, 'nmask'], writes=[kbm])
            for m in range(32):
                pc = 0 if m == 0 else 1 if m == 1 else 3 if m == 30 else 4 if m == 31 else 2
                kb = min(max(m - 2, 0), 27)
                kts = [2 + kb + j for j in range(5)] + [0, 1]
                s = cnt % 2
                cnt += 1
                SA, SB = self.ps[2 * s], self.ps[2 * s + 1]
                kSA, kSB = f"ps{2 * s}", f"ps{2 * s + 1}"
                q0 = (2 + m) * 128
                for j, kt in enumerate(kts):
                    dst = SA[:, j * 128:(j + 1) * 128] if j < 4 else SB[:, (j - 4) * 128:(j - 3) * 128]
                    P.op('pe', lambda e, dst=dst, kt=kt, q0=q0: e.matmul(dst, kT[0:64, kt * 128:(kt + 1) * 128], qT[0:64, q0:q0 + 128], start=True, stop=True), reads=[kkT, kqT], writes=[kSA if j < 4 else kSB])
                T1, PT = T1s[s], PTs[s]
                P.op('dve', lambda e, T1=T1, SA=SA, pc=pc: e.tensor_tensor(T1[:, 0:4, :], SA[:, :].rearrange("p (j q) -> p j q", j=4), bm[:, pc, 0:4, :], ALU.add), reads=[kSA, kbm], writes=[f"T1{s}"])
                P.op('dve', lambda e, T1=T1, SB=SB, pc=pc: e.tensor_tensor(T1[:, 4:7, :], SB[:, 0:384].rearrange("p (j q) -> p j q", j=3), bm[:, pc, 4:7, :], ALU.add), reads=[kSB, kbm], writes=[f"T1{s}"])
                P.op('act', lambda e, T1=T1, PT=PT: e.activation(PT, T1, AF.Exp), reads=[f"T1{s}"], writes=[f"PT{s}"])
                ob = 4 + s
                O = self.ps[ob]
                for j, kt in enumerate(kts):
                    P.op('pe', lambda e, O=O, kt=kt, j=j, PT=PT: e.matmul(O[:, 0:128], Vp[:, kt, :], PT[:, j, :], start=(j == 0), stop=(j == 6)), reads=[kVp, f"PT{s}"], writes=[f"ps{ob}"])
                self.attn_fin(O, 128, f"ps{ob}", ost[0:64, q0:q0 + 128], kost, rec, 'rec')
            if need_ctx:
                S_ = self.ps[6]
                for jt in range(2):
                    P.op('pe', lambda e, jt=jt: e.matmul(S_[:, jt * 256:(jt + 1) * 256], kT[0:64, jt * 128:(jt + 1) * 128], qT[0:64, 0:256], start=True, stop=True), reads=[kkT, kqT], writes=['ps6'])
                P.op('act', lambda e: e.activation(PTc, S_[:, :].rearrange("p (j q) -> p j q", j=2), AF.Exp), reads=['ps6'], writes=['PTc'])
                O = self.ps[7]
                for jt in range(2):
                    P.op('pe', lambda e, jt=jt: e.matmul(O[:, 0:256], Vp[:, jt, :], PTc[:, jt, :], start=(jt == 0), stop=(jt == 1)), reads=[kVp, 'PTc'], writes=['ps7'])
                self.attn_fin(O, 256, 'ps7', ost[0:64, 0:256], kost, rec, 'rec')
            P.dma('pool', self.S["oT"][1, h * 64:(h + 1) * 64, :], ost[0:64, :], reads=[kost], writes=['oT'])

    def phase_gla(self, l):
        P = self.P
        self.new_phase()
        STG = GLA_STAGE

        def PO(stg, *a, **k):
            if STG >= stg:
                P.op(*a, **k)

        need_ctx = (l == 0)
        gc = self.alloc(512).rearrange("p (a t) -> p a t", a=4)
        P.dma('sp', gc, self.I["gla_c"].rearrange("a p t -> p a t"), writes=['gc'])
        onesM = self.alloc(128)
        P.op('pool', lambda e: e.memset(onesM, 1.0 / 128), writes=['onesM'])
        gnorm = self.alloc(1)
        P.dma('sp', gnorm, self.I["gla_norm"][l, :].rearrange("(p o) -> p o", o=1), writes=['gnorm'])
        wgk = [self.alloc(256) for _ in range(2)]
        brow = [self.alloc(256) for _ in range(2)]
        lowb = [self.alloc(NTOK, BF16) for _ in range(2)]
        low = [self.alloc(NTOK) for _ in range(2)]
        for d_, nm in enumerate(["fwd", "bwd"]):
            P.dma('sp', wgk[d_][0:16, :], self.I[f"gla_w_gk_{nm}"][l, :, :], writes=[f"wgk{d_}"])
            P.dma('sp', brow[d_][0:1, :], self.I[f"gla_b_gk_{nm}"][l:l + 1, :], writes=[f"brow{d_}"])
            P.dma('sp', lowb[d_][0:16, :], self.S["proj_fm"][FM_LOW + 16 * d_:FM_LOW + 16 * d_ + 16, :], writes=[f"lowb{d_}"])
            P.op('dve', lambda e, d_=d_: e.tensor_copy(low[d_][0:16, :], lowb[d_][0:16, :]), reads=[f"lowb{d_}"], writes=[f"low{d_}"])
        qT = self.alloc(NTOK, BF16); kT = self.alloc(NTOK, BF16)
        vt = self.alloc(NT * 256, BF16).rearrange("p (t c) -> p t c", c=256)
        obuf = self.alloc(2 * NTOK, BF16).rearrange("p (h t) -> p h t", h=2)
        ogT = self.alloc(2 * NTOK, BF16).rearrange("p (h t) -> p h t", h=2)
        gst = self.alloc(2 * NTOK, BF16).rearrange("p (h t) -> p h t", h=2)
        Sf = [self.alloc(256) for _ in range(2)]
        Sb = [self.alloc(256, BF16) for _ in range(2)]
        e1s = [self.alloc(128) for _ in range(2)]; sps = [self.alloc(128) for _ in range(2)]
        Eqs = [self.alloc(128) for _ in range(2)]; Eks = [self.alloc(128) for _ in range(2)]
        kins = [self.alloc(128, BF16) for _ in range(2)]
        qinzs = [[self.alloc(128, BF16) for _ in range(2)] for _ in range(2)]
        kintzs = [[self.alloc(128, BF16) for _ in range(2)] for _ in range(2)]
        for q_ in range(2):
            for i_ in range(2):
                P.op('pool', lambda e, i_=i_, q_=q_: e.memset(qinzs[q_][i_], 0.0), writes=[f'qin{q_}'])
                P.op('pool', lambda e, i_=i_, q_=q_: e.memset(kintzs[q_][i_], 0.0), writes=[f'kint{q_}'])
        ams = [self.alloc(256, BF16).rearrange("p (h t) -> p h t", h=2) for _ in range(2)]
        Kps = [self.alloc(512).rearrange("p (c n) -> p c n", c=2) for _ in range(2)]
        ot = self.alloc(128); sq = self.alloc(128); sd = self.alloc(128); on = self.alloc(128); sg = self.alloc(128)
        pzs = [self.ps[0][:, 0:128], self.ps[0][:, 128:256]]
        pbs = [self.ps[0][:, 256:384], self.ps[0][:, 384:512]]
        pkts = [self.psb(1)[:, 0:128], self.psb(1)[:, 128:256]]
        pa3s = [self.ps[2][:, 0:256].rearrange("p (h t) -> p h t", h=2), self.ps[2][:, 256:512].rearrange("p (h t) -> p h t", h=2)]
        pK3s = [self.ps[3][:, :].rearrange("p (c n) -> p c n", c=2), self.ps[4][:, :].rearrange("p (c n) -> p c n", c=2)]
        pO3 = [self.ps[5][:, 0:128], self.ps[6][:, 0:128]]
        kpO = ['pO0', 'pO1']
        pss = self.ps[7]
        tcnt = 0
        for pr in range(2):
            P.dma('sp', qT, self.S["proj_fm"][FM_GQ + pr * 128:FM_GQ + (pr + 1) * 128, :], writes=['qT'])
            P.dma('sp', kT, self.S["proj_fm"][FM_GK + pr * 128:FM_GK + (pr + 1) * 128, :], writes=['kT'])
            P.dma('sp', vt, self.S["proj_tm"][:, 2720 + pr * 256:2720 + (pr + 1) * 256].rearrange("(t p) c -> p t c", p=128), writes=['vt'])
            for hh in range(2):
                P.dma('sp', ogT[:, hh, :], self.S["proj_fm"][FM_OG + (2 * pr + hh) * 128:FM_OG + (2 * pr + hh + 1) * 128, :], writes=['ogT'])
            for d_ in (1, 0):
                tiles = [1, 0] + list(range(NT - 1, 1, -1)) if d_ == 1 else list(range(NT))
                corder = (1, 0) if d_ == 1 else (0, 1)
                cur = 0
                P.op('pool', lambda e: e.memset(Sf[0], 0.0), writes=['Sf0'])
                P.op('pool', lambda e: e.memset(Sb[0], 0.0), writes=['Sb0'])
                if GLA_NT:
                    tiles = tiles[:GLA_NT]
                state = {'cur': 0}

                def emitA(tt, q_, d_=d_, pr=pr):
                    sl = slice(tt * 128, (tt + 1) * 128)
                    isq = need_ctx or tt >= 2
                    e1, sp_, Eq, Ek, kin, qinz, kintz, am, Kp = e1s[q_], sps[q_], Eqs[q_], Eks[q_], kins[q_], qinzs[q_], kintzs[q_], ams[q_], Kps[q_]
                    pz, pb_, pkt, pa3, pK3 = pzs[q_], pbs[q_], pkts[q_], pa3s[q_], pK3s[q_]
                    Q = str(q_)
                    PO(1, 'pe', lambda e, sl=sl, d_=d_, pr=pr: e.matmul(pz, low[d_][0:16, sl], wgk[d_][0:16, pr * 128:(pr + 1) * 128], start=True, stop=False), reads=[f"low{d_}", f"wgk{d_}"], writes=[('ps0_' + Q)])
                    PO(1, 'pe', lambda e, d_=d_, pr=pr: e.matmul(pz, self.onesf[0:1, :], brow[d_][0:1, pr * 128:(pr + 1) * 128], start=False, stop=True), reads=[f"brow{d_}", 'onesf'], writes=[('ps0_' + Q)])
                    PO(1, 'act', lambda e: e.activation(e1, pz, AF.Exp, scale=-1.0), reads=[('ps0_' + Q)], writes=[('e1_' + Q)])
                    PO(1, 'act', lambda e: e.activation(sp_, e1, AF.Ln, bias=1.0), reads=[('e1_' + Q)], writes=[('sp_' + Q)])
                    PO(2, 'pe', lambda e, d_=d_: e.matmul(pb_, sp_, gc[:, d_, :], start=True, stop=True), reads=[('sp_' + Q), 'gc'], writes=[('ps1_' + Q)])
                    PO(2, 'act', lambda e: e.activation(Eq, pb_, AF.Exp), reads=[('ps1_' + Q)], writes=[('Eq_' + Q)])
                    PO(2, 'act', lambda e: e.activation(Ek, pb_, AF.Exp, scale=-1.0), reads=[('ps1_' + Q)], writes=[('Ek_' + Q)])
                    for hh in range(2):
                        PO(3, 'dve', lambda e, sl=sl, hh=hh: e.tensor_tensor(qinz[hh][hh * 64:(hh + 1) * 64, :], qT[hh * 64:(hh + 1) * 64, sl], Eq[hh * 64:(hh + 1) * 64, :], ALU.mult), reads=['qT', ('Eq_' + Q)], writes=[('qin_' + Q)])
                    PO(3, 'dve', lambda e, sl=sl: e.tensor_tensor(kin, kT[:, sl], Ek, ALU.mult), reads=['kT', ('Ek_' + Q)], writes=[('kin_' + Q)])
                    PO(3, 'pe', lambda e: e.transpose(pkt, kin, self.identb), reads=[('kin_' + Q), 'identb'], writes=[('ps2_' + Q)])
                    for c in range(2):
                        PO(3, 'act', lambda e, c=c: e.copy(kintz[c][c * 64:(c + 1) * 64, :], pkt[c * 64:(c + 1) * 64, :]), reads=[('ps2_' + Q)], writes=[('kint_' + Q)])
                    if isq:
                        for hh in range(2):
                            PO(4, 'pe', lambda e, hh=hh: e.matmul(pa3[:, hh, :], kin, qinz[hh], start=True, stop=True), reads=[('kin_' + Q), ('qin_' + Q)], writes=[('ps3_' + Q)])
                        PO(4, 'dve', lambda e, d_=d_: e.tensor_tensor(am, pa3, gc[:, 2 + d_, :].unsqueeze(1).broadcast_to([128, 2, 128]), ALU.mult), reads=[('ps3_' + Q), 'gc'], writes=[('am_' + Q)])
                    for c in corder:
                        PO(5, 'pe', lambda e, c=c, tt=tt: e.matmul(pK3[:, c, :], kintz[c], vt[:, tt, :], start=True, stop=True), reads=[('kint_' + Q), 'vt'], writes=[('ps4_' + Q)])
                        di = (c * 64 + 63) if d_ == 0 else (c * 64)
                        PO(5, 'dve', lambda e, c=c, di=di: e.tensor_scalar(Kp[:, c, :], pK3[:, c, :], Eq[:, di:di + 1], None, ALU.mult), reads=[('ps4_' + Q), ('Eq_' + Q)], writes=[f"Kp{c}_" + Q])

                def emitB(tt, q_, d_=d_, pr=pr, corder=corder):
                    sl = slice(tt * 128, (tt + 1) * 128)
                    isq = need_ctx or tt >= 2
                    e1, sp_, Eq, Ek, kin, qinz, kintz, am, Kp = e1s[q_], sps[q_], Eqs[q_], Eks[q_], kins[q_], qinzs[q_], kintzs[q_], ams[q_], Kps[q_]
                    pz, pb_, pkt, pa3, pK3 = pzs[q_], pbs[q_], pkts[q_], pa3s[q_], pK3s[q_]
                    Q = str(q_)
                    cur = state['cur']
                    if isq:
                        for hh in range(2):
                            PO(6, 'pe', lambda e, hh=hh, tt=tt: e.matmul(pO3[hh], vt[:, tt, hh * 128:(hh + 1) * 128], am[:, hh, :], start=True, stop=False), reads=['vt', ('am_' + Q)], writes=[kpO[hh]])
                    for ci, c in enumerate(corder):
                        if isq:
                            for hh in range(2):
                                PO(6, 'pe', lambda e, hh=hh, c=c, cur=cur, ci=ci: e.matmul(pO3[hh][:, c * 64:(c + 1) * 64], Sb[cur][:, hh * 128:(hh + 1) * 128], qinz[hh][:, c * 64:(c + 1) * 64], start=False, stop=(ci == 1)), reads=[f"Sb{cur}", ('qin_' + Q)], writes=[kpO[hh]])
                        di = (c * 64 + 63) if d_ == 0 else (c * 64)
                        nxt = 1 - cur
                        PO(7, 'dve', lambda e, c=c, di=di, cur=cur, nxt=nxt: e.scalar_tensor_tensor(Sf[nxt], Sf[cur], Eq[:, di:di + 1], Kp[:, c, :], ALU.mult, ALU.add), reads=[f"Sf{cur}", ('Eq_' + Q), f"Kp{c}_" + Q], writes=[f"Sf{nxt}"])
                        PO(7, 'act', lambda e, nxt=nxt: e.copy(Sb[nxt], Sf[nxt]), reads=[f"Sf{nxt}"], writes=[f"Sb{nxt}"])
                        cur = nxt
                        state['cur'] = cur
                    if not isq:
                        return
                    if d_ == 1:
                        for hh in range(2):
                            PO(8, 'act', lambda e, sl=sl, hh=hh: e.copy(obuf[:, hh, sl], pO3[hh]), reads=[kpO[hh]], writes=['obuf'])
                    else:
                        for hh in range(2):
                            PO(8, 'dve', lambda e, hh=hh, sl=sl: e.tensor_tensor(ot, pO3[hh], obuf[:, hh, sl], ALU.add), reads=[kpO[hh], 'obuf'], writes=['ot'])
                            PO(8, 'act', lambda e: e.activation(sq, ot, AF.Square), reads=['ot'], writes=['sq'])
                            PO(8, 'pe', lambda e: e.matmul(pss[:, 0:128], onesM, sq, start=True, stop=True), reads=['onesM', 'sq'], writes=['ps7'])
                            PO(8, 'act', lambda e: e.activation(sd, pss[:, 0:128], AF.Sqrt, bias=EPS), reads=['ps7'], writes=['sd'])
                            PO(8, 'dve', lambda e: e.reciprocal(sd, sd), reads=['sd'], writes=['sd'])
                            PO(8, 'dve', lambda e: e.scalar_tensor_tensor(on, ot, gnorm[:, 0:1], sd, ALU.mult, ALU.mult), reads=['ot', 'gnorm', 'sd'], writes=['on'])
                            PO(8, 'act', lambda e, hh=hh, sl=sl: e.activation(sg, ogT[:, hh, sl], AF.Silu), reads=['ogT'], writes=['sg'])
                            PO(8, 'dve', lambda e, hh=hh, sl=sl: e.tensor_tensor(gst[:, hh, sl], on, sg, ALU.mult), reads=['on', 'sg'], writes=['gst'])

                prev = None
                for ti_, tt in enumerate(tiles):
                    emitA(tt, ti_ % 2)
                    if prev is not None:
                        emitB(*prev)
                    prev = (tt, ti_ % 2)
                emitB(*prev)
            for hh in range(2):
                P.dma('pool', self.S["oT"][2, (2 * pr + hh) * 128:(2 * pr + hh + 1) * 128, :], gst[:, hh, :], reads=['gst'], writes=['oT'])

    def load_w_bf16(self, dst, src3, nk, ncol, tmp, name):
        P = self.P
        for k in range(nk):
            P.dma('sp', tmp, src3[k * 128:(k + 1) * 128, :], writes=['wtmpm'])
            P.op('pool', lambda e, k=k: e.tensor_copy(dst[:, k, :], tmp), reads=['wtmpm'], writes=[name])

    def phase_merge(self, l):
        P = self.P
        self.new_phase()
        wo = [self.alloc(4 * D, BF16).rearrange("p (k n) -> p k n", k=4) for _ in range(3)]
        wout = self.alloc(8 * D, BF16).rearrange("p (k n) -> p k n", k=8)
        tmp = self.alloc(D)
        for br, nm in enumerate(["w_o_mla", "w_o_na", "w_o_gla"]):
            self.load_w_bf16(wo[br], self.I[nm][l], 4, D, tmp, f"wo{br}")
        self.load_w_bf16(wout, self.I["w_out"][l], 8, D, tmp, 'wout')
        g1 = [self.alloc(D) for _ in range(2)]
        for r in range(2):
            self.bvec(g1[r], self.modrow(l, r, 2), f"g1{r}")
        oTt = [[self.alloc(512, BF16).rearrange("p (k t) -> p k t", k=4) for _ in range(3)] for _ in range(2)]
        gts = [self.alloc(3 * D, BF16) for _ in range(2)]
        xts = [self.alloc(D) for _ in range(2)]
        y = self.alloc(D); tb = self.alloc(D); yb = self.alloc(D, BF16)
        yT = self.alloc(D, BF16).rearrange("p (k t) -> p k t", k=8)
        x1 = [self.alloc(D) for _ in range(2)]
        for idx, tt in enumerate(self.qtiles(l)):
            b = idx % 2
            sl = slice(tt * 128, (tt + 1) * 128)
            for br in range(3):
                P.dma('sp', oTt[b][br], self.S["oT"][br, :, sl].rearrange("(k p) t -> p k t", p=128), writes=[f"oTt{b}{br}"])
            P.dma('sp', gts[b], self.S["proj_tm"][sl, C_GATE:C_GATE + 3 * D], writes=[f"gts{b}"])
            P.dma('sp', xts[b], self.S["xs"][sl, :], writes=[f"xt{b}"])
            for br in range(3):
                pbk = (0, 2)[br % 2]
                for half in range(2):
                    pst = self.ps[pbk + half]
                    for k in range(4):
                        P.op('pe', lambda e, k=k, half=half, pst=pst, br=br, b=b: e.matmul(pst[:, :], oTt[b][br][:, k, :], wo[br][:, k, half * 512:(half + 1) * 512], start=(k == 0), stop=(k == 3)), reads=[f"oTt{b}{br}", f"wo{br}"], writes=[f"ps{pbk + half}"])
                    hs = slice(half * 512, (half + 1) * 512)
                    gs = slice(br * D + half * 512, br * D + (half + 1) * 512)
                    if br == 0:
                        P.op('dve', lambda e, pst=pst, hs=hs, gs=gs, b=b: e.tensor_tensor(y[:, hs], pst[:, :], gts[b][:, gs], ALU.mult), reads=[f"ps{pbk + half}", f"gts{b}"], writes=['y'])
                    else:
                        P.op('dve', lambda e, pst=pst, hs=hs, gs=gs, b=b: e.tensor_tensor(tb[:, hs], pst[:, :], gts[b][:, gs], ALU.mult), reads=[f"ps{pbk + half}", f"gts{b}"], writes=['tb'])
                        if br == 1:
                            P.op('pool', lambda e, hs=hs: e.tensor_tensor(y[:, hs], y[:, hs], tb[:, hs], ALU.add), reads=['tb', 'y'], writes=['y'])
                        else:
                            P.op('pool', lambda e, hs=hs: e.tensor_tensor(yb[:, hs], y[:, hs], tb[:, hs], ALU.add), reads=['tb', 'y'], writes=['yb'])
            pv = self.psb(4).rearrange("p (k t) -> p k t", k=8)
            for k in range(8):
                P.op('pe', lambda e, k=k: e.transpose(pv[:, k, :], yb[:, k * 128:(k + 1) * 128], self.identb), reads=['yb', 'identb'], writes=['ps4'])
            P.op('act', lambda e: e.copy(yT, pv), reads=['ps4'], writes=['yT'])
            for half in range(2):
                pst = self.ps[5 + half]
                for k in range(8):
                    P.op('pe', lambda e, k=k, half=half, pst=pst: e.matmul(pst[:, :], yT[:, k, :], wout[:, k, half * 512:(half + 1) * 512], start=(k == 0), stop=(k == 7)), reads=['yT', 'wout'], writes=[f"ps{5 + half}"])
                hs = slice(half * 512, (half + 1) * 512)
                r = 1 if tt < 2 else 0
                P.op('dve', lambda e, pst=pst, hs=hs, r=r: e.tensor_tensor(tb[:, hs], pst[:, :], g1[r][:, hs], ALU.mult), reads=[f"ps{5 + half}", f"g1{r}"], writes=['tb'])
                P.op('pool', lambda e, hs=hs, b=b: e.tensor_tensor(x1[b][:, hs], tb[:, hs], xts[b][:, hs], ALU.add), reads=['tb', f"xt{b}"], writes=[f"x1{b}"])
            P.dma('pool', self.S["xmid"][sl, :], x1[b], reads=[f"x1{b}"], writes=['xmid'])

    def phase_peer_prep(self, l):
        P = self.P
        self.new_phase()
        uf = [self.alloc(D) for _ in range(2)]
        ub = [self.alloc(D, BF16) for _ in range(2)]
        us = [self.alloc(D, BF16).rearrange("p (k e) -> p k e", k=8) for _ in range(2)]
        vf = [self.alloc(D) for _ in range(2)]
        vb = [self.alloc(D, BF16) for _ in range(2)]
        for et in range(128):
            b = et % 2
            es = slice(et * 128, (et + 1) * 128)
            P.dma('sp', uf[b], self.I["peer_u"][l, es, :], writes=[f"uf{b}"])
            P.op('dve', lambda e, b=b: e.tensor_copy(ub[b], uf[b]), reads=[f"uf{b}"], writes=[f"ub{b}"])
            pv = self.psb(b).rearrange("p (k e) -> p k e", k=8)
            for k in range(8):
                P.op('pe', lambda e, k=k, b=b, pv=pv: e.transpose(pv[:, k, :], ub[b][:, k * 128:(k + 1) * 128], self.identb), reads=[f"ub{b}", 'identb'], writes=[f"ps{b}"])
            P.op('act', lambda e, b=b, pv=pv: e.copy(us[b], pv), reads=[f"ps{b}"], writes=[f"us{b}"])
            P.dma('pool', self.S["uT"][:, :, es].rearrange("k p e -> p k e"), us[b], reads=[f"us{b}"], writes=['uT'])
            P.dma('sp', vf[b], self.I["peer_v"][l, es, :], writes=[f"vf{b}"])
            P.op('pool', lambda e, b=b: e.tensor_copy(vb[b], vf[b]), reads=[f"vf{b}"], writes=[f"vb{b}"])
            P.dma('pool', self.S["vb"][es, :], vb[b], reads=[f"vb{b}"], writes=['vbd'])

    def phase_peer(self, l):
        P = self.P
        self.new_phase()
        need_ctx = (l == 0)
        mv = self.load_modvecs(l, "norm2", 3, 4)
        g2 = [self.alloc(D) for _ in range(2)]
        for r in range(2):
            self.bvec(g2[r], self.modrow(l, r, 5), f"g2{r}")
        wqmem = self.alloc(8 * D)
        wq = wqmem.rearrange("p (k n) -> p k n", k=8)
        dE = [wqmem[:, i_ * 2048:(i_ + 1) * 2048].rearrange("p (h i j) -> p h i j", h=4, i=4) for i_ in range(4)]
        dEk = ['dE0', 'dE1', 'dE2', 'dE3']
        Dg = [[self.alloc(128, BF16) for _ in range(8)] for _ in range(2)]
        skn = self.alloc(128)
        skT = self.alloc(128)
        for p_ in range(2):
            P.dma('sp', skn[:, p_ * 64:(p_ + 1) * 64], self.I["peer_sub_keys"][l, p_, :, :], writes=['skn'])
        P.op('pe', lambda e: e.transpose(self.ps[0][:, 0:128], skn, self.identf), reads=['skn', 'identf'], writes=['ps0'])
        skTz = [self.alloc(128) for _ in range(2)]
        for a_ in range(2):
            P.op('pool', lambda e, a_=a_: e.memset(skTz[a_], 0.0), writes=['skT'])
        for a_ in range(2):
            P.op('act', lambda e, a_=a_: e.copy(skTz[a_][a_ * 64:(a_ + 1) * 64, :], self.ps[0][a_ * 64:(a_ + 1) * 64, 0:128]), reads=['ps0'], writes=['skT'])
        x1t = [self.alloc(D) for _ in range(2)]
        h2f = self.alloc(D); h2b = self.alloc(D, BF16); tmpf = self.alloc(D)
        h2Tb = self.alloc(8 * 256, BF16).rearrange("p (k t) -> p k t", k=8)
        h2Tf = self.alloc(8 * 256).rearrange("p (k t) -> p k t", k=8)
        qTf = self.alloc(8 * 256).rearrange("p (h t) -> p h t", h=8)
        s_tok = [self.alloc(2048).rearrange("p (h a n) -> p h a n", h=8, a=2) for _ in range(2)]
        s1n = [self.alloc(1024).rearrange("p (h n) -> p h n", h=8) for _ in range(2)]
        v16 = self.alloc(256).rearrange("p (h a n) -> p h a n", h=8, a=2)
        tmp128 = self.alloc(128)
        cand = self.alloc(2048).rearrange("p (h n) -> p h n", h=8)
        c24 = self.alloc(192).rearrange("p (h n) -> p h n", h=8)
        tmpc = self.alloc(256); tmpc2 = self.alloc(256); ec = self.alloc(256); junkc = self.alloc(256)
        sm = [self.alloc(64) for _ in range(2)]
        stt = self.alloc(4)
        mB = [self.alloc(2048, BF16).rearrange("p (h i j) -> p h i j", h=4, i=4) for _ in range(2)]
        candf = cand.rearrange("p h n -> p (h n)")
        mB += [candf[:, i_ * 1024:(i_ + 1) * 1024].bitcast(BF16).rearrange("p (h i j) -> p h i j", h=4, i=4) for i_ in range(2)]
        uTs = [self.alloc(8 * 512, BF16).rearrange("p (k e) -> p k e", k=8) for _ in range(2)]
        vss = [self.alloc(4 * D, BF16).rearrange("p (c d) -> p c d", c=4) for _ in range(2)]
        ga_ = self.alloc(1024).rearrange("p (i t) -> p i t", i=4)
        Wt_ = self.alloc(1024, BF16).rearrange("p (i t) -> p i t", i=4)
        ga = [ga_, ga_]
        Wt = [Wt_, Wt_]
        tb = self.alloc(D); x2 = [tmpf, tmpf]
        groups = ([0] if need_ctx else []) + list(range(2, NT, 2))
        if PEER_GROUPS:
            groups = groups[:PEER_GROUPS]
        ecnt = 0
        wcnt = 0
        for g0 in groups:
            r = 1 if g0 < 2 else 0
            weff, kw, sh, ks = mv[r]
            for ti in range(2):
                tt = g0 + ti
                sl = slice(tt * 128, (tt + 1) * 128)
                ts = slice(ti * 128, (ti + 1) * 128)
                P.dma('sp', x1t[ti], self.S["xmid"][sl, :], writes=[f"x1t{ti}"])
                self.norm_mod(x1t[ti], f"x1t{ti}", h2f, 'h2f', weff, kw, sh, ks, tmpf, 'tmpf', stt, 'stt')
                P.op('act', lambda e: e.copy(h2b, h2f), reads=['h2f'], writes=['h2b'])
                pv = self.psb(0).rearrange("p (k t) -> p k t", k=8)
                for k in range(8):
                    P.op('pe', lambda e, k=k, pv=pv: e.transpose(pv[:, k, :], h2b[:, k * 128:(k + 1) * 128], self.identb), reads=['h2b', 'identb'], writes=['ps0'])
                P.op('act', lambda e, pv=pv, ts=ts: e.copy(h2Tb[:, :, ts], pv), reads=['ps0'], writes=['h2Tb'])
                for half in range(2):
                    pf = self.ps[1 + half][:, :].rearrange("p (k t) -> p k t", k=4)
                    for k4 in range(4):
                        k = half * 4 + k4
                        P.op('pe', lambda e, k=k, k4=k4, pf=pf: e.transpose(pf[:, k4, :], h2f[:, k * 128:(k + 1) * 128], self.identf), reads=['h2f', 'identf'], writes=[f"ps{1 + half}"])
                    P.op('dve', lambda e, pf=pf, half=half, ts=ts: e.tensor_copy(h2Tf[:, half * 4:(half + 1) * 4, ts], pf), reads=[f"ps{1 + half}"], writes=['h2Tf'])
            if PEER_STAGE < 2:
                continue
            P.dma('sp', wq, self.I["peer_w_q"][l].rearrange("(k p) n -> p k n", p=128), writes=['wq'] + dEk)
            for cc in range(8):
                pst = self.ps[cc % 2]
                for k in range(8):
                    P.op('pe', lambda e, k=k, cc=cc, pst=pst: e.matmul(pst[:, 0:256], wq[:, k, cc * 128:(cc + 1) * 128], h2Tf[:, k, :], start=(k == 0), stop=(k == 7)), reads=['wq', 'h2Tf'] + dEk, writes=[f"ps{cc % 2}"])
                P.op('act', lambda e, cc=cc, pst=pst: e.copy(qTf[:, cc, :], pst[:, 0:256]), reads=[f"ps{cc % 2}"], writes=['qTf'])
            for ti in (range(2) if PEER_STAGE >= 3 else []):
                ts = slice(ti * 128, (ti + 1) * 128)
                st_ = s_tok[ti]
                for h4 in range(4):
                    pst = self.ps[2 + (h4 % 2)]
                    ps4 = pst[:, :].rearrange("p (h a n) -> p h a n", h=2, a=2)
                    for hl in range(2):
                        h = h4 * 2 + hl
                        for a in range(2):
                            P.op('pe', lambda e, h=h, hl=hl, a=a, ps4=ps4, ts=ts: e.matmul(ps4[:, hl, a, :], qTf[:, h, ts], skTz[a], start=True, stop=True), reads=['qTf', 'skT'], writes=[f"ps{2 + (h4 % 2)}"])
                    P.op('act', lambda e, h4=h4, ps4=ps4, st_=st_: e.copy(st_[:, h4 * 2:h4 * 2 + 2, :, :], ps4), reads=[f"ps{2 + (h4 % 2)}"], writes=[f"stok{ti}"])
                sm_ = sm[ti]
                ksm = f"sm{ti}"
                if PEER_SUB < 2:
                    continue
                for h in range(8):
                    for a in range(2):
                        P.op('dve', lambda e, h=h, a=a, st_=st_: e.max(v16[:, h, a, 0:8], st_[:, h, a, :]), reads=[f"stok{ti}"], writes=['v16'])
                        P.op('dve', lambda e, h=h, a=a, st_=st_: e.match_replace(tmp128, v16[:, h, a, 0:8], st_[:, h, a, :], -1e30), reads=[f"stok{ti}", 'v16'], writes=['tmp128'])
                        P.op('dve', lambda e, h=h, a=a: e.max(v16[:, h, a, 8:16], tmp128), reads=['tmp128'], writes=['v16'])
                    if PEER_SUB < 3:
                        continue
                    ch = cand[:, h, :]
                    P.op('dve', lambda e, h=h, ch=ch: e.tensor_tensor(ch.rearrange("p (a b) -> p a b", a=16), v16[:, h, 0, :].unsqueeze(2).broadcast_to([128, 16, 16]), v16[:, h, 1, :].unsqueeze(1).broadcast_to([128, 16, 16]), ALU.add), reads=['v16'], writes=['cand', 'mB2', 'mB3'])
                    P.op('dve', lambda e, h=h, ch=ch: e.max(c24[:, h, 0:8], ch), reads=['cand'], writes=['c24'])
                    P.op('dve', lambda e, h=h, ch=ch: e.match_replace(tmpc, c24[:, h, 0:8], ch, -1e30), reads=['cand', 'c24'], writes=['tmpc'])
                    P.op('dve', lambda e, h=h: e.max(c24[:, h, 8:16], tmpc), reads=['tmpc'], writes=['c24'])
                    P.op('dve', lambda e, h=h: e.match_replace(tmpc2, c24[:, h, 8:16], tmpc, -1e30), reads=['tmpc', 'c24'], writes=['tmpc2'])
                    P.op('dve', lambda e, h=h: e.max(c24[:, h, 16:24], tmpc2), reads=['tmpc2'], writes=['c24'])
                if PEER_SUB < 4:
                    continue
                P.op('dve', lambda e, sm_=sm_: e.tensor_tensor(sm_[:, 0:8], c24[:, :, 15], c24[:, :, 16], ALU.add), reads=['c24'], writes=[ksm])
                P.op('dve', lambda e, sm_=sm_: e.tensor_scalar(sm_[:, 0:8], sm_[:, 0:8], 0.5, None, ALU.mult), reads=[ksm], writes=[ksm])
                P.op('dve', lambda e, sm_=sm_: e.tensor_scalar(sm_[:, 8:16], c24[:, :, 0], -1.0, None, ALU.mult), reads=['c24'], writes=[ksm])
                for h in range(8):
                    ch = cand[:, h, :]
                    P.op('act', lambda e, h=h, ch=ch, sm_=sm_: e.activation(ec, ch, AF.Exp, bias=sm_[:, 8 + h:9 + h]), reads=['cand', ksm], writes=['ec'])
                    P.op('dve', lambda e, h=h, ch=ch, sm_=sm_: e.scalar_tensor_tensor(junkc, ch, sm_[:, h:h + 1], ec, ALU.is_ge, ALU.mult, accum_out=sm_[:, 16 + h:17 + h]), reads=['cand', 'ec', ksm], writes=['junkc', ksm])
                P.op('act', lambda e, sm_=sm_: e.activation(sm_[:, 24:32], sm_[:, 16:24], AF.Ln), reads=[ksm], writes=[ksm])
                P.op('dve', lambda e, sm_=sm_: e.tensor_tensor(sm_[:, 32:40], sm_[:, 8:16], sm_[:, 24:32], ALU.subtract), reads=[ksm], writes=[ksm])
                P.op('dve', lambda e, sm_=sm_: e.tensor_tensor(sm_[:, 40:48], sm_[:, 0:8], sm_[:, 32:40], ALU.add), reads=[ksm], writes=[ksm])
                P.op('act', lambda e, sm_=sm_: e.activation(sm_[:, 48:56], sm_[:, 40:48], AF.Exp), reads=[ksm], writes=[ksm])
                for h in range(8):
                    P.op('dve', lambda e, h=h, st_=st_, sm_=sm_, ti=ti: e.tensor_scalar(s1n[ti][:, h, :], st_[:, h, 0, :], sm_[:, h:h + 1], None, ALU.subtract), reads=[f"stok{ti}", ksm], writes=[f"s1n{ti}"])
                    P.op('dve', lambda e, h=h, sm_=sm_, ti=ti: e.tensor_scalar(Dg[ti][h], self.identb, sm_[:, 48 + h:49 + h], None, ALU.mult), reads=['identb', ksm], writes=[f"Dg{ti}"])
            Gps = [self.ps[0], self.ps[1]]
            Aps = [self.ps[2], self.ps[3]]
            Ops = [[self.ps[4], self.ps[5]], [self.ps[6], self.ps[7]]]
            if PEER_STAGE < 4:
                continue
            for ib in range(32):
                wb_ = wcnt % 2
                wcnt += 1
                uT_, vs_ = uTs[wb_], vss[wb_]
                P.dma('sp', uT_, self.S["uT"][:, :, ib * 512:(ib + 1) * 512].rearrange("k p e -> p k e"), writes=[f"uT{wb_}"])
                P.dma('sp', vs_, self.S["vb"][ib * 512:(ib + 1) * 512, :].rearrange("(c j) d -> j c d", j=128), writes=[f"vs{wb_}"])
                for il in range(4):
                    A_ = Aps[il // 2]
                    for k in range(8):
                        P.op('pe', lambda e, A_=A_, il=il, k=k, uT_=uT_: e.matmul(A_[:, (il % 2) * 256:(il % 2 + 1) * 256], uT_[:, k, il * 128:(il + 1) * 128], h2Tb[:, k, :], start=(k == 0), stop=(k == 7)), reads=[f"uT{wb_}", 'h2Tb'], writes=[f"ps{2 + il // 2}"])
                gb = 0
                for hf in range(2):
                    P.op('act', lambda e, hf=hf, gb=gb: e.activation(ga[gb][:, hf * 2:hf * 2 + 2, :], Aps[hf][:, :].rearrange("p (i t) -> p i t", i=2), AF.Gelu_apprx_tanh), reads=[f"ps{2 + hf}"], writes=[f"ga{gb}"])
                def emit_de(ti, hb):
                    nonlocal ecnt
                    st_ = s_tok[ti]
                    eb = ecnt % 2
                    ecnt += 1
                    mi = 2 * ti + hb
                    d_, e_, m_ = dE[eb], dE[2 + eb], mB[mi]
                    kd, ke = dEk[eb], dEk[2 + eb]
                    mw = [f"mB{mi}"] + (['cand'] if mi >= 2 else [])
                    P.op('dve', lambda e, d_=d_, st_=st_, hb=hb, ti=ti, ib=ib: e.tensor_tensor(d_, st_[:, hb * 4:(hb + 1) * 4, 1, :].unsqueeze(2).broadcast_to([128, 4, 4, 128]), s1n[ti][:, hb * 4:(hb + 1) * 4, ib * 4:(ib + 1) * 4].unsqueeze(3).broadcast_to([128, 4, 4, 128]), ALU.add), reads=[f"stok{ti}", f"s1n{ti}"], writes=[kd])
                    P.op('act', lambda e, d_=d_, e_=e_: e.activation(e_, d_, AF.Exp), reads=[kd], writes=[ke])
                    return (ti, hb, d_, e_, m_, kd, ke, mw)

                def emit_m(u):
                    ti, hb, d_, e_, m_, kd, ke, mw = u
                    P.op('dve', lambda e, d_=d_, e_=e_, m_=m_: e.scalar_tensor_tensor(m_, d_, 0.0, e_, ALU.is_ge, ALU.mult), reads=[kd, ke], writes=mw)
                    if hb == 1:
                        for il in range(4):
                            G_ = Gps[il // 2]
                            for h in range(8):
                                P.op('pe', lambda e, G_=G_, il=il, ti=ti, h=h: e.matmul(G_[:, (il % 2) * 256 + ti * 128:(il % 2) * 256 + (ti + 1) * 128], mB[2 * ti + h // 4][:, h % 4, il, :], Dg[ti][h], start=(h == 0), stop=(h == 7)), reads=[f"mB{2 * ti + h // 4}", f"Dg{ti}"], writes=[f"ps{il // 2}"])
                pend = None
                for ti in range(2):
                    for hb in range(2):
                        u = emit_de(ti, hb)
                        if pend is not None:
                            emit_m(pend)
                        pend = u
                emit_m(pend)
                for hf in range(2):
                    P.op('dve', lambda e, hf=hf, gb=gb: e.tensor_tensor(Wt[gb][:, hf * 2:hf * 2 + 2, :], Gps[hf][:, :].rearrange("p (i t) -> p i t", i=2), ga[gb][:, hf * 2:hf * 2 + 2, :], ALU.mult), reads=[f"ps{hf}", f"ga{gb}"], writes=[f"Wt{gb}"])
                for il in range(4):
                    i = ib * 4 + il
                    for ti in range(2):
                        for half in range(2):
                            P.op('pe', lambda e, il=il, ti=ti, half=half, i=i, gb=gb, vs_=vs_: e.matmul(Ops[ti][half][:, :], Wt[gb][:, il, ti * 128:(ti + 1) * 128], vs_[:, il, half * 512:(half + 1) * 512], start=(i == 0), stop=(i == 127)), reads=[f"Wt{gb}", f"vs{wb_}"], writes=[f"ps{4 + 2 * ti + half}"])
            for ti in range(2):
                tt = g0 + ti
                sl = slice(tt * 128, (tt + 1) * 128)
                for half in range(2):
                    hs = slice(half * 512, (half + 1) * 512)
                    P.op('dve', lambda e, ti=ti, half=half, hs=hs, r=r: e.tensor_tensor(tb[:, hs], Ops[ti][half][:, :], g2[r][:, hs], ALU.mult), reads=[f"ps{4 + 2 * ti + half}", f"g2{r}"], writes=['tb'])
                    P.op('pool', lambda e, ti=ti, hs=hs: e.tensor_tensor(x2[ti][:, hs], tb[:, hs], x1t[ti][:, hs], ALU.add), reads=['tb', f"x1t{ti}"], writes=["tmpf"])
                P.dma('pool', self.S["xs"][sl, :], x2[ti], reads=["tmpf"], writes=['xs'])

    def phase_final(self):
        P = self.P
        self.new_phase()
        fn = self.alloc(D)
        self.bvec(fn, self.I["final_norm"].rearrange("(o d) -> o d", o=1), 'fn')
        xts = [self.alloc(D) for _ in range(2)]
        os_ = [self.alloc(D) for _ in range(2)]
        tmpf = self.alloc(D)
        stt = self.alloc(4 * NT)
        for tt in range(2, NT):
            b = tt % 2
            st = stt[:, 4 * tt:4 * tt + 4]
            kst = f"st{tt}"
            P.dma('sp', xts[b], self.S["xs"][tt * 128:(tt + 1) * 128, :], writes=[f"xt{b}"])
            P.op('act', lambda e, b=b, st=st: e.activation(tmpf, xts[b], AF.Square, accum_out=st[:, 0:1]), reads=[f"xt{b}"], writes=['tmpf', kst])
            P.op('act', lambda e, st=st: e.activation(st[:, 1:2], st[:, 0:1], AF.Sqrt, bias=EPS, scale=1.0 / D), reads=[kst], writes=[kst])
            P.op('dve', lambda e, st=st: e.reciprocal(st[:, 2:3], st[:, 1:2]), reads=[kst], writes=[kst])
            P.op('dve', lambda e, b=b, st=st: e.scalar_tensor_tensor(os_[b], xts[b], st[:, 2:3], fn, ALU.mult, ALU.mult), reads=[f"xt{b}", kst, 'fn'], writes=[f"os{b}"])
            P.dma('pool', self.out[(tt - 2) * 128:(tt - 1) * 128, :], os_[b], reads=[f"os{b}"], writes=['out'])


def _consts():
    t = np.arange(NLAT)
    rows = (t // 64).astype(np.float32)
    cols = (t % 64).astype(np.float32)
    inv = (10000.0 ** (-np.arange(0, 16, 2, dtype=np.float32) / 16)).astype(np.float32)
    ar = rows[:, None] * inv
    ac = cols[:, None] * inv
    ang = np.concatenate([ar, ar, ac, ac], axis=-1).astype(np.float32)
    cos = np.cos(ang).astype(np.float32)
    sin = np.sin(ang).astype(np.float32)
    sgn = np.concatenate([-np.ones(8), np.ones(8), -np.ones(8), np.ones(8)]).astype(np.float32)
    sinS = sin * sgn
    k_cos = np.concatenate([np.ones((NCTX, 32), np.float32), cos], 0)
    k_sin = np.concatenate([np.zeros((NCTX, 32), np.float32), sinS], 0)
    q_cosT = np.ascontiguousarray((k_cos * np.float32(SC_MLA)).T)
    q_sinT = np.ascontiguousarray((k_sin * np.float32(SC_MLA)).T)
    s = np.arange(128)
    same = (s[:, None] // 64) == (s[None, :] // 64)
    le = s[:, None] <= s[None, :]
    ge = s[:, None] >= s[None, :]
    gla_c = np.stack([np.where(same & le, -1.0 / 16, 0.0), np.where(same & ge, -1.0 / 16, 0.0),
                      np.where(same & le, 1.0, 0.0), np.where(same & ge, 1.0, 0.0)]).astype(np.float32)
    return dict(k_cos=k_cos, k_sin=k_sin, q_cosT=q_cosT, q_sinT=q_sinT, gla_c=gla_c)


def _na_index():
    dr = np.zeros((128, 5, 7, 128), np.int64)
    dc = np.zeros((128, 5, 7, 128), np.int64)
    valid = np.zeros((128, 5, 7, 128), bool)
    kp = np.arange(128)[:, None]
    qi = np.arange(128)[None, :]
    for pc, m in enumerate([0, 1, 2, 30, 31]):
        kb = min(max(m - 2, 0), 27)
        for j in range(5):
            kr = 2 * (kb + j) + kp // 64
            wk = kp % 64
            r = 2 * m + qi // 64
            wq = qi % 64
            rs = np.clip(r - 4, 0, 56)
            cs = np.clip(wq - 8, 0, 48)
            ok = (kr >= rs) & (kr < rs + 8) & (wk >= cs) & (wk < cs + 16)
            valid[:, pc, j, :] = ok
            dr[:, pc, j, :] = np.clip(kr - r + 7, 0, 14)
            dc[:, pc, j, :] = np.clip(wk - wq, -15, 15) + 15
    valid[:, :, 5:7, :] = True
    return dr, dc, valid


def _prep_shared(inp):
    sh = {}
    w_in = inp["w_in"]
    kr = w_in[:, :, 640:672]
    swap = np.concatenate([kr[..., 8:16], kr[..., 0:8], kr[..., 24:32], kr[..., 16:24]], -1)
    sh["w_in_x"] = np.ascontiguousarray(np.concatenate([w_in, swap], -1))
    wuq = inp["mla_w_uq"].reshape(2, 384, 8, 96)
    rp = wuq[..., 64:96]
    rsw = np.concatenate([rp[..., 8:16], rp[..., 0:8], rp[..., 24:32], rp[..., 16:24]], -1)
    z = np.zeros((2, 384, 8, 64), np.float32)
    sh["w_uq_x"] = np.ascontiguousarray(np.concatenate([wuq, z, rsw], -1).reshape(2, 384, 1536))
    dr, dc, valid = _na_index()
    rpb = inp["na_rpb"]
    nb = rpb[:, :, dr, dc]
    nb = np.where(valid[None, None], nb, np.float32(0.0)).astype(np.float32)
    nb[:, :, :, :, 5:7, :] = 0.0
    sh["na_bias"] = np.ascontiguousarray(nb.reshape(2, 8, 128, 4480))
    sh["na_mask"] = np.ascontiguousarray(np.where(valid, 0.0, -1e30).astype(np.float32).reshape(128, 4480))
    sh.update(_consts())
    for k in ["w_ada", "b_ada", "norm1", "mla_q_norm", "mla_kv_norm", "mla_w_ukv", "gla_w_gk_fwd", "gla_b_gk_fwd",
              "gla_w_gk_bwd", "gla_b_gk_bwd", "gla_norm", "w_o_mla", "w_o_na", "w_o_gla", "w_out", "norm2", "peer_w_q",
              "peer_sub_keys", "peer_u", "peer_v", "final_norm"]:
        sh[k] = np.ascontiguousarray(inp[k], dtype=np.float32)
    return sh


_CACHE = {}


def build_nc(nlayers=2, dbg=False, phases=None, scr_in=()):
    nc = bass.Bass("TRN2", target_bir_lowering=False)
    with contextlib.ExitStack() as st:
        b = Builder(nc, st, nlayers=nlayers, dbg=dbg, phases=phases, scr_in=scr_in)
        b.build()
    return nc, b


def kernel(**inp):
    inp = {k: np.asarray(v) for k, v in inp.items()}
    sh = _prep_shared(inp)
    nc, b = build_nc()
    in_maps = []
    ncores = 4
    for c in range(ncores):
        m = dict(sh)
        m["x"] = np.ascontiguousarray(inp["x"][c], dtype=np.float32)
        m["ctx"] = np.ascontiguousarray(inp["ctx"][c], dtype=np.float32)
        m["cvecs"] = np.ascontiguousarray(np.stack([inp["c"][c], inp["c_ctx"]]), dtype=np.float32)
        in_maps.append({k: v for k, v in m.items() if k in b.I})
    res = run_bass_kernel_spmd(nc, in_maps, core_ids=list(range(ncores)))
    return np.stack([np.asarray(r["out"], dtype=np.float32) for r in res.results], 0)
```
